# Optimizing a Trainium2 kernel written in Bass

```python
import math
import jax, jax.numpy as jnp
from jax import lax
import numpy as np

D_MODEL = 1024
BATCH = 16
SEQ = 2048
DEPTH = 2

S5_WIDTH = D_MODEL // 2
S5_GROUP = 16
S5_GROUPS = S5_WIDTH // S5_GROUP
S5_STATE = 64
DT_MIN = 1e-3
DT_MAX = 1e-1
SB_WIDTH = D_MODEL - S5_WIDTH
SB_HEAD_DIM = 64
SB_HEADS = SB_WIDTH // SB_HEAD_DIM
SB_BLOCK = 128
EVEN_IN = S5_WIDTH + 3 * SB_WIDTH
CONV_WIDTH = D_MODEL
CONV_SIZE = 31
FFN_HIDDEN = 4 * D_MODEL
N_EVEN = (DEPTH + 1) // 2
N_ODD = DEPTH // 2
DEEPNORM_ALPHA = (2 * DEPTH) ** 0.25
DEEPNORM_BETA = (8 * DEPTH) ** -0.25
LN_EPS = 1e-5

kernel_name = "s5_stickbreak_conformer_deepnorm_hybrid"


def layer_norm(x, g, b):
    xf = x.astype(jnp.float32)
    mu = jnp.mean(xf, axis=-1, keepdims=True)
    var = jnp.mean(jnp.square(xf - mu), axis=-1, keepdims=True)
    y = (xf - mu) * lax.rsqrt(var + LN_EPS)
    return (y * g.astype(jnp.float32) + b.astype(jnp.float32)).astype(x.dtype)


def s5_mixer(u, lam_re, lam_im, log_dt, b_re, b_im, c_re, c_im, d, w_glu, b_glu):
    bsz, seq, _ = u.shape
    f32 = jnp.float32
    lam = lax.complex(jnp.minimum(lam_re.astype(f32), -1e-4), lam_im.astype(f32))
    dt = jnp.exp(log_dt.astype(f32))[:, None]
    lam_bar = jnp.exp(lam * dt)
    b_bar = ((lam_bar - 1) / lam)[..., None] * lax.complex(b_re.astype(f32), b_im.astype(f32))
    uf = u.astype(f32)
    ug = uf.reshape(bsz, seq, S5_GROUPS, S5_GROUP).astype(jnp.complex64)
    bu = jnp.einsum('blgp,gnp->blgn', ug, b_bar)
    a = jnp.broadcast_to(lam_bar, bu.shape)

    def combine(left, right):
        a_l, b_l = left
        a_r, b_r = right
        return a_r * a_l, a_r * b_l + b_r

    _, states = lax.associative_scan(combine, (a, bu), axis=1)
    c = lax.complex(c_re.astype(f32), c_im.astype(f32))
    y = jnp.real(jnp.einsum('blgn,gpn->blgp', states, c)).reshape(bsz, seq, S5_WIDTH)
    y = jax.nn.gelu(y + d.astype(f32) * uf).astype(u.dtype)
    h = y @ w_glu + b_glu
    val, gate = jnp.split(h, 2, axis=-1)
    return val * jax.nn.sigmoid(gate)


def stick_breaking_attention(q, k, v):
    bsz, seq, nh, hd = q.shape
    n_blk = seq // SB_BLOCK
    scale = hd ** -0.5
    kf = k.astype(jnp.float32)
    vf = v.astype(jnp.float32)
    key_pos = jnp.arange(seq)
    qb = q.reshape(bsz, n_blk, SB_BLOCK, nh, hd).transpose(1, 0, 2, 3, 4)

    def one_block(args):
        q_blk, blk = args
        q_pos = blk * SB_BLOCK + jnp.arange(SB_BLOCK)
        z = jnp.einsum('bqhd,bkhd->bhqk', q_blk.astype(jnp.float32), kf) * scale
        strict = key_pos[None, :] < q_pos[:, None]
        log_beta = jax.nn.log_sigmoid(z)
        log_keep = jnp.where(strict, jax.nn.log_sigmoid(-z), 0.0)
        after = lax.cumsum(log_keep, axis=3, reverse=True) - log_keep
        weights = jnp.where(strict, jnp.exp(log_beta + after), 0.0)
        return jnp.einsum('bhqk,bkhd->bqhd', weights, vf)

    out = lax.map(one_block, (qb, jnp.arange(n_blk)))
    return out.transpose(1, 0, 2, 3, 4).reshape(bsz, seq, nh * hd).astype(q.dtype)


def conformer_conv(x, w_pw1, b_pw1, w_dw, b_dw, ln_g, ln_b, w_pw2, b_pw2):
    h = x @ w_pw1 + b_pw1
    val, gate = jnp.split(h, 2, axis=-1)
    h = val * jax.nn.sigmoid(gate)
    h = lax.conv_general_dilated(
        h, w_dw[:, None, :], window_strides=(1,), padding=[(CONV_SIZE - 1, 0)],
        dimension_numbers=('NWC', 'WIO', 'NWC'), feature_group_count=CONV_WIDTH) + b_dw
    h = jax.nn.silu(layer_norm(h, ln_g, ln_b))
    return h @ w_pw2 + b_pw2


def sqrelu_mlp(x, w1, b1, w2, b2):
    return jnp.square(jax.nn.relu(x @ w1 + b1)) @ w2 + b2


def setup_inputs(seed: int = 0) -> dict:
    key = jax.random.key(seed)
    keys = jax.random.split(key, 40)
    counter = [0]

    def nk():
        k = keys[counter[0]]
        counter[0] += 1
        return k

    def nrm(shape, scale):
        return scale * jax.random.normal(nk(), shape, jnp.float32)

    D = D_MODEL
    x = nrm((BATCH, SEQ, D), 1.0)
    ln1_g = 1.0 + nrm((DEPTH, D), 0.02)
    ln1_b = nrm((DEPTH, D), 0.01)
    ln2_g = 1.0 + nrm((DEPTH, D), 0.02)
    ln2_b = nrm((DEPTH, D), 0.01)
    ffn_w1 = nrm((DEPTH, D, FFN_HIDDEN), D ** -0.5)
    ffn_b1 = nrm((DEPTH, FFN_HIDDEN), 0.01)
    ffn_w2 = nrm((DEPTH, FFN_HIDDEN, D), FFN_HIDDEN ** -0.5 * DEEPNORM_BETA)
    ffn_b2 = nrm((DEPTH, D), 0.01)
    col_scale = jnp.concatenate([jnp.ones((S5_WIDTH + 2 * SB_WIDTH,), jnp.float32),
                                 jnp.full((SB_WIDTH,), DEEPNORM_BETA, jnp.float32)])
    mix_w_in = nrm((N_EVEN, D, EVEN_IN), D ** -0.5) * col_scale
    mix_b_in = nrm((N_EVEN, EVEN_IN), 0.01)
    s5_lambda_re = -0.5 + nrm((N_EVEN, S5_GROUPS, S5_STATE), 0.01)
    s5_lambda_im = (jnp.pi * jnp.arange(S5_STATE, dtype=jnp.float32)
                    + nrm((N_EVEN, S5_GROUPS, S5_STATE), 0.01))
    s5_log_dt = jax.random.uniform(nk(), (N_EVEN, S5_GROUPS), jnp.float32,
                                   minval=math.log(DT_MIN), maxval=math.log(DT_MAX))
    s5_b_re = nrm((N_EVEN, S5_GROUPS, S5_STATE, S5_GROUP), (2 * S5_GROUP) ** -0.5)
    s5_b_im = nrm((N_EVEN, S5_GROUPS, S5_STATE, S5_GROUP), (2 * S5_GROUP) ** -0.5)
    s5_c_re = nrm((N_EVEN, S5_GROUPS, S5_GROUP, S5_STATE), (2 * S5_STATE) ** -0.5)
    s5_c_im = nrm((N_EVEN, S5_GROUPS, S5_GROUP, S5_STATE), (2 * S5_STATE) ** -0.5)
    s5_d = nrm((N_EVEN, S5_WIDTH), 1.0)
    s5_w_glu = nrm((N_EVEN, S5_WIDTH, 2 * S5_WIDTH), S5_WIDTH ** -0.5)
    s5_b_glu = nrm((N_EVEN, 2 * S5_WIDTH), 0.01)
    mix_w_out = nrm((N_EVEN, S5_WIDTH + SB_WIDTH, D), (S5_WIDTH + SB_WIDTH) ** -0.5 * DEEPNORM_BETA)
    mix_b_out = nrm((N_EVEN, D), 0.01)
    conv_w_pw1 = nrm((N_ODD, D, 2 * CONV_WIDTH), D ** -0.5)
    conv_b_pw1 = nrm((N_ODD, 2 * CONV_WIDTH), 0.01)
    conv_w_dw = nrm((N_ODD, CONV_SIZE, CONV_WIDTH), CONV_SIZE ** -0.5)
    conv_b_dw = nrm((N_ODD, CONV_WIDTH), 0.01)
    conv_ln_g = 1.0 + nrm((N_ODD, CONV_WIDTH), 0.02)
    conv_ln_b = nrm((N_ODD, CONV_WIDTH), 0.01)
    conv_w_pw2 = nrm((N_ODD, CONV_WIDTH, D), CONV_WIDTH ** -0.5 * DEEPNORM_BETA)
    conv_b_pw2 = nrm((N_ODD, D), 0.01)
    return {
        "x": x, "ln1_g": ln1_g, "ln1_b": ln1_b, "ln2_g": ln2_g, "ln2_b": ln2_b,
        "ffn_w1": ffn_w1, "ffn_b1": ffn_b1, "ffn_w2": ffn_w2, "ffn_b2": ffn_b2,
        "mix_w_in": mix_w_in, "mix_b_in": mix_b_in,
        "s5_lambda_re": s5_lambda_re, "s5_lambda_im": s5_lambda_im, "s5_log_dt": s5_log_dt,
        "s5_b_re": s5_b_re, "s5_b_im": s5_b_im, "s5_c_re": s5_c_re, "s5_c_im": s5_c_im,
        "s5_d": s5_d, "s5_w_glu": s5_w_glu, "s5_b_glu": s5_b_glu,
        "mix_w_out": mix_w_out, "mix_b_out": mix_b_out,
        "conv_w_pw1": conv_w_pw1, "conv_b_pw1": conv_b_pw1, "conv_w_dw": conv_w_dw,
        "conv_b_dw": conv_b_dw, "conv_ln_g": conv_ln_g, "conv_ln_b": conv_ln_b,
        "conv_w_pw2": conv_w_pw2, "conv_b_pw2": conv_b_pw2,
    }


def reference(x, ln1_g, ln1_b, ln2_g, ln2_b, ffn_w1, ffn_b1, ffn_w2, ffn_b2,
              mix_w_in, mix_b_in, s5_lambda_re, s5_lambda_im, s5_log_dt,
              s5_b_re, s5_b_im, s5_c_re, s5_c_im, s5_d, s5_w_glu, s5_b_glu,
              mix_w_out, mix_b_out, conv_w_pw1, conv_b_pw1, conv_w_dw, conv_b_dw,
              conv_ln_g, conv_ln_b, conv_w_pw2, conv_b_pw2):
    bsz, seq, _ = x.shape
    for layer in range(DEPTH):
        i = layer // 2
        if layer % 2 == 0:
            h = x @ mix_w_in[i] + mix_b_in[i]
            u, q, k, v = jnp.split(
                h, [S5_WIDTH, S5_WIDTH + SB_WIDTH, S5_WIDTH + 2 * SB_WIDTH], axis=-1)
            s5_out = s5_mixer(u, s5_lambda_re[i], s5_lambda_im[i], s5_log_dt[i],
                              s5_b_re[i], s5_b_im[i], s5_c_re[i], s5_c_im[i],
                              s5_d[i], s5_w_glu[i], s5_b_glu[i])
            head_shape = (bsz, seq, SB_HEADS, SB_HEAD_DIM)
            sb_out = stick_breaking_attention(q.reshape(head_shape), k.reshape(head_shape),
                                              v.reshape(head_shape))
            mix = jnp.concatenate([s5_out, sb_out], axis=-1) @ mix_w_out[i] + mix_b_out[i]
        else:
            mix = conformer_conv(x, conv_w_pw1[i], conv_b_pw1[i], conv_w_dw[i], conv_b_dw[i],
                                 conv_ln_g[i], conv_ln_b[i], conv_w_pw2[i], conv_b_pw2[i])
        x = layer_norm(DEEPNORM_ALPHA * x + mix, ln1_g[layer], ln1_b[layer])
        ffn = sqrelu_mlp(x, ffn_w1[layer], ffn_b1[layer], ffn_w2[layer], ffn_b2[layer])
        x = layer_norm(DEEPNORM_ALPHA * x + ffn, ln2_g[layer], ln2_b[layer])
    return x
```

```python
import contextlib
import numpy as np
import concourse.bass as bass
import concourse.mybir as mybir
from concourse.bass_utils import run_bass_kernel_spmd

F32 = mybir.dt.float32
BF16 = mybir.dt.bfloat16
I32 = mybir.dt.int32
AF = mybir.ActivationFunctionType
ALU = mybir.AluOpType

D = 1024
L = 2048
NSEQ = 2
T = NSEQ * L
ALPHA = 4.0 ** 0.25
EPS = 1e-5
PI = float(np.pi)
PAD = 1024


class _Rec:
    def __init__(self):
        self.call = None

    def __getattr__(self, name):
        def f(*a, **kw):
            self.call = (name, a, kw)
            return self
        return f


class Ctx:
    def __init__(self, nc):
        self.nc = nc
        self.es = contextlib.ExitStack()
        self.stacks = [self.es]
        self.engs = ["pe", "act", "dve", "pool", "sp"]
        self.sem = {}
        self.cnt = {}
        for k in self.engs:
            self.sem[k] = self.es.enter_context(nc.semaphore("s_" + k))
            self.cnt[k] = 0
        self.waited = {k: {} for k in self.engs}
        self.lastw = {}
        self.reads = {}
        self.dsem = {}
        self.dcnt = {}
        self.nsb = 0
        self.prog = {k: [] for k in self.engs}

    def sb(self, shape, dt, name=None):
        self.nsb += 1
        return self.stacks[-1].enter_context(
            self.nc.sbuf_tensor(f"{name or 'sb'}_{self.nsb}", list(shape), dt))

    def ps(self, shape, dt, name=None):
        self.nsb += 1
        return self.stacks[-1].enter_context(
            self.nc.psum_tensor(f"{name or 'ps'}_{self.nsb}", list(shape), dt))

    @contextlib.contextmanager
    def scope(self):
        st = contextlib.ExitStack()
        self.stacks.append(st)
        try:
            yield
        finally:
            self.barrier()
            self.stacks.pop()
            st.close()

    def _semobj(self, key):
        return self.sem[key] if key in self.sem else self.dsem[key]

    def _deps(self, reads, writes):
        deps = []
        for r in reads:
            if r in self.lastw:
                deps.append(self.lastw[r])
        for w in writes:
            if w in self.lastw:
                deps.append(self.lastw[w])
            deps.extend(self.reads.get(w, []))
        return deps

    def _wait(self, e, deps):
        best = {}
        for (k, v) in deps:
            if e == "pe" and k == "pe":
                continue
            if v > best.get(k, 0):
                best[k] = v
        for k, v in best.items():
            if self.waited[e].get(k, 0) >= v:
                continue
            so = self._semobj(k)
            self.prog[e].append(lambda h, so=so, v=v: h.wait_ge(so, v))
            self.waited[e][k] = v

    def _record(self, ticket, reads, writes):
        for r in reads:
            self.reads.setdefault(r, []).append(ticket)
        for w in writes:
            self.lastw[w] = ticket
            self.reads[w] = []

    def op(self, e, fn, reads=(), writes=()):
        self._wait(e, self._deps(reads, writes))
        self.cnt[e] += 1
        so = self.sem[e]
        rec = _Rec()
        fn(rec)
        name, a, kw = rec.call
        self.prog[e].append(lambda h, name=name, a=a, kw=kw, so=so: getattr(h, name)(*a, **kw).then_inc(so, 1))
        t = (e, self.cnt[e])
        self._record(t, reads, writes)
        return t

    def dma(self, q, out, in_, reads=(), writes=(), key=None, **kw):
        if key not in self.dsem:
            self.dsem[key] = self.es.enter_context(self.nc.semaphore(f"d{len(self.dsem)}"))
            self.dcnt[key] = 0
        self._wait(q, self._deps(reads, writes))
        so = self.dsem[key]
        self.prog[q].append(lambda h, out=out, in_=in_, kw=kw, so=so:
                            h.dma_start(out=out, in_=in_, **kw).then_inc(so, 16))
        self.dcnt[key] += 16
        t = (key, self.dcnt[key])
        self._record(t, reads, writes)
        return t

    def barrier(self):
        deps = [(k, self.cnt[k]) for k in self.engs if self.cnt[k] > 0]
        deps += [(k, v) for k, v in self.dcnt.items() if v > 0]
        for e in self.engs:
            self._wait(e, deps)
        self.lastw = {}
        self.reads = {}

    def emit(self):
        with self.nc.Block() as block:
            def mk(e):
                def body(h):
                    for f in self.prog[e]:
                        f(h)
                return body
            block.tensor(mk("pe"))
            block.scalar(mk("act"))
            block.vector(mk("dve"))
            block.gpsimd(mk("pool"))
            block.sync(mk("sp"))

    def close(self):
        self.es.close()


VEC_SPEC = [("ln1_g0", 8), ("ln1_b0", 8), ("ln2_g0", 8), ("ln2_b0", 8),
            ("ln1_g1", 8), ("ln1_b1", 8), ("ln2_g1", 8), ("ln2_b1", 8),
            ("fb1_0", 32), ("fb1_1", 32), ("fb2_0", 8), ("fb2_1", 8),
            ("b_in", 12), ("s5_d", 4), ("b_glu", 8), ("b_out", 8),
            ("b_pw1", 16), ("b_dw", 8), ("cln_g", 8), ("cln_b", 8), ("b_pw2", 8),
            ("w_dw", 31 * 8)]
VOFF = {}
_o = 0
for _n, _w in VEC_SPEC:
    VOFF[_n] = (_o, _w)
    _o += _w
NV = _o


def build(dbg=None):
    nc = bass.Bass("TRN2", target_bir_lowering=False)

    def din(name, shape, dt=F32):
        return nc.dram_tensor(name, list(shape), dt, kind="ExternalInput").ap()

    x_d = din("x", [T, D])
    w_in_d = din("w_in", [D, 2048])
    w_glu_d = din("w_glu", [512, 1024])
    w_out_d = din("w_out", [D, D])
    pw1_d = din("pw1", [D, 2048])
    pw2_d = din("pw2", [D, D])
    w1_d = [din("w1_0", [D, 4096]), din("w1_1", [D, 4096])]
    w2_d = [din("w2_0", [4096, D]), din("w2_1", [4096, D])]
    vec_d = din("vecs", [128, NV])
    bv_d = din("bv_bc", [128, 512])
    cst_d = din("consts", [128, 640])
    lam_d = din("lamll", [128, 3, 16])
    bt_d = [din("bt_re", [128, 16, 128]), din("bt_im", [128, 16, 128])]
    cp_d = [din("cp_re", [128, 16, 128]), din("cp_im", [128, 16, 128])]
    y_d = nc.dram_tensor("y", [T, D], F32, kind="ExternalOutput").ap()
    S_d = nc.dram_tensor("S_scr", [D, T], F32, kind="Internal").ap()
    XB_d = nc.dram_tensor("XB_scr", [D, T], BF16, kind="Internal").ap()
    if dbg:
        dS_d = nc.dram_tensor("dbgS", [D, T], F32, kind="ExternalOutput").ap()
        dX_d = nc.dram_tensor("dbgX", [D, T], BF16, kind="ExternalOutput").ap()
        dC_d = nc.dram_tensor("dbgC", [D, T], BF16, kind="ExternalOutput").ap()
        dC_v = dC_d.rearrange("(c p) t -> p c t", p=128)
    HG_d = nc.dram_tensor("HG_scr", [16, 2, 128, 2048], BF16, kind="Internal").ap()
    S_v = S_d.rearrange("(c p) t -> p c t", p=128)
    XB_v = XB_d.rearrange("(c p) t -> p c t", p=128)

    c = Ctx(nc)
    op = c.op

    vec = c.sb([128, NV], F32, "vec")
    cst = c.sb([128, 640], F32, "cst")
    cstb = c.sb([128, 640], BF16, "cstb")
    der = c.sb([128, 160], F32, "der")
    banks = [c.ps([128, 512], F32, f"bank{i}") for i in range(8)]
    ident = cst[:, 0:128]
    identb = cstb[:, 0:128]
    trib = cstb[:, 128:256]
    nmaskb = cstb[:, 256:384]
    pmaskb = cstb[:, 384:512]
    onesb = cstb[:, 512:640]

    def V(name, i=0, n=1):
        o, w = VOFF[name]
        return vec[:, o + i:o + i + n]

    c.dma("sp", vec[:], vec_d, writes=["vec"], key="k_vec")
    c.dma("sp", cst[:], cst_d, writes=["cst"], key="k_cst")
    op("dve", lambda h: h.tensor_copy(out=cstb[:], in_=cst[:]), reads=["cst"], writes=["cstb"])

    DER = {}
    _do = [0]

    def dalloc(name, n):
        DER[name] = _do[0]
        _do[0] += n
        return der[:, DER[name]:DER[name] + n]

    bq8 = dalloc("bq8", 4)
    op("dve", lambda h: h.tensor_scalar(out=bq8, in0=V("b_in", 4, 4), scalar1=0.125, scalar2=None,
                                        op0=ALU.mult), reads=["vec"], writes=["der"])

    def ln_consts(tag, gname, bname, nextb):
        ga = dalloc("ga" + tag, 8)
        ba = dalloc("ba" + tag, 8)
        op("dve", lambda h: h.tensor_scalar(out=ga, in0=V(gname, 0, 8), scalar1=ALPHA, scalar2=None,
                                            op0=ALU.mult), reads=["vec"], writes=["der"])
        op("dve", lambda h: h.scalar_tensor_tensor(out=ba, in0=V(bname, 0, 8), scalar=ALPHA,
                                                   in1=V(nextb, 0, 8), op0=ALU.mult, op1=ALU.add),
           reads=["vec"], writes=["der"])
        return ga, ba

    ga10, ba10 = ln_consts("10", "ln1_g0", "ln1_b0", "fb2_0")
    ga20, ba20 = ln_consts("20", "ln2_g0", "ln2_b0", "b_pw2")
    ga11, ba11 = ln_consts("11", "ln1_g1", "ln1_b1", "fb2_1")

    bkrr = [0]

    def nb(pool=(0, 1, 2, 3, 4, 5, 6, 7)):
        bkrr[0] += 1
        return pool[bkrr[0] % len(pool)]

    def BK(i):
        return ("bk", i)

    def mm(bank, out_ap, lhsT, rhs, start, stop, reads, **kw):
        op("pe", lambda h: h.matmul(out_ap, lhsT=lhsT, rhs=rhs, start=start, stop=stop, **kw),
           reads=reads, writes=[BK(bank)])

    def load_w(dst, src_d, kc_n, ncols, key, rkey, cs0=0, c0b=0):
        sv = src_d.rearrange("(c p) f -> p c f", p=128)
        step = 2048
        for kc in range(kc_n):
            for c0 in range(0, ncols, step):
                c1 = min(ncols, c0 + step)
                c.dma("pool", dst[:, kc, c0b + c0:c0b + c1], sv[:, kc, cs0 + c0:cs0 + c1],
                      writes=[(rkey, kc, c0)], key=key)
        fin = (key, c.dcnt[key])
        for kc in range(kc_n):
            for c0 in range(0, ncols, step):
                c.lastw[(rkey, kc, c0)] = fin

    def wreads(rkey, kc, ncols):
        return [(rkey, kc, c0) for c0 in range(0, ncols, 2048)]

    def layer_norm(zt, N, zkey, xh, xhkey, lnb):
        zb, zq, msq, var, rstd, nmr = lnb["zb"], lnb["zq"], lnb["msq"], lnb["var"], lnb["rstd"], lnb["nmr"]
        op("act", lambda h: h.activation(out=zb[:, :, 0:N], in_=zt[:, :, 0:N], func=AF.Copy),
           reads=[zkey], writes=["ln_zb"])
        op("act", lambda h: h.activation(out=zq[:, :, 0:N], in_=zt[:, :, 0:N], func=AF.Square),
           reads=[zkey], writes=["ln_zq"])
        bm, bq = nb(), nb()
        for kc in range(8):
            mm(bm, banks[bm][:, 0:N], onesb, zb[:, kc, 0:N], kc == 0, kc == 7, ["ln_zb", "cstb"])
        for kc in range(8):
            mm(bq, banks[bq][:, 0:N], onesb, zq[:, kc, 0:N], kc == 0, kc == 7, ["ln_zq", "cstb"])
        op("act", lambda h: h.activation(out=msq[:, 0:N], in_=banks[bm][:, 0:N], func=AF.Square),
           writes=[BK(bm), "ln_msq"])
        op("dve", lambda h: h.tensor_tensor(out=var[:, 0:N], in0=banks[bq][:, 0:N], in1=msq[:, 0:N],
                                            op=ALU.subtract), reads=["ln_msq"], writes=[BK(bq), "ln_var"])
        op("dve", lambda h: h.tensor_scalar(out=var[:, 0:N], in0=var[:, 0:N], scalar1=EPS, scalar2=None,
                                            op0=ALU.add), reads=["ln_var"], writes=["ln_var"])
        op("act", lambda h: h.activation(out=var[:, 0:N], in_=var[:, 0:N], func=AF.Sqrt),
           reads=["ln_var"], writes=["ln_var"])
        op("dve", lambda h: h.reciprocal(out=banks[bq][:, 0:N], in_=var[:, 0:N]), reads=["ln_var"],
           writes=[BK(bq)])
        mb = banks[bm][:, 0:N].unsqueeze(1).broadcast_to([128, 8, N])
        rb = banks[bq][:, 0:N].unsqueeze(1).broadcast_to([128, 8, N])
        op("dve", lambda h: h.tensor_tensor(out=xh[:, :, 0:N], in0=zt[:, :, 0:N], in1=mb, op=ALU.subtract),
           reads=[zkey], writes=[BK(bm), xhkey])
        op("dve", lambda h: h.tensor_tensor(out=xh[:, :, 0:N], in0=xh[:, :, 0:N], in1=rb, op=ALU.mult),
           reads=[], writes=[BK(bq), xhkey])

    def ln_scratch(N):
        return {"zb": c.sb([128, 8, N], BF16, "zb"), "zq": c.sb([128, 8, N], BF16, "zq"),
                "msq": c.sb([128, N], F32, "msq"), "var": c.sb([128, N], F32, "var"),
                "rstd": c.sb([128, N], F32, "rstd"), "nmr": c.sb([128, N], F32, "nmr")}

    def ln_epilogue_stream(xh, xhkey, N, tok0, ga, ba, gname, bname, sst, xbt, slot, sstkey=None):
        sk = sstkey or ("sst", slot)
        for kc in range(8):
            op("act", lambda h, kc=kc: h.activation(out=xbt[:, kc, 0:N], in_=xh[:, kc, 0:N], func=AF.Identity,
                                                    bias=V(bname, kc, 1), scale=V(gname, kc, 1)),
               reads=[xhkey, "vec"], writes=[("xbt", slot)])
        for kc in range(8):
            op("act", lambda h, kc=kc: h.activation(out=sst[:, kc, 0:N], in_=xh[:, kc, 0:N], func=AF.Identity,
                                                    bias=ba[:, kc:kc + 1], scale=ga[:, kc:kc + 1]),
               reads=[xhkey, "der"], writes=[sk])
        c.dma("pool", S_v[:, :, tok0:tok0 + N], sst[:, :, 0:N], reads=[sk],
              writes=[("S_d", tok0)], key=f"k_sst{slot}")
        c.dma("pool", XB_v[:, :, tok0:tok0 + N], xbt[:, :, 0:N], reads=[("xbt", slot)],
              writes=[("XB_d", tok0)], key=f"k_xbt{slot}")

    with c.scope():
        bv = c.sb([128, 512], F32, "bv")
        c.dma("sp", bv[:], bv_d, writes=["bv"], key="k_bv")

        lam = c.sb([128, 3, 16], F32, "lam")
        c.dma("sp", lam[:], lam_d, writes=["lam"], key="k_lam")
        tb = c.sb([128, 24, 16], F32, "s5tmp")
        tbi = c.sb([128, 16], I32, "s5tmpi")
        AR = c.sb([128, 11, 16], F32, "AR")
        AI = c.sb([128, 11, 16], F32, "AI")
        NAI = c.sb([128, 11, 16], F32, "NAI")
        TK = "s5t"

        def tt(out, a, b, o, e="dve"):
            op(e, lambda h: h.tensor_tensor(out=out, in0=a, in1=b, op=o), reads=[TK, "lam"], writes=[TK])

        def ts(out, a, s1, o1, s2=None, o2=None):
            if o2 is None:
                op("dve", lambda h: h.tensor_scalar(out=out, in0=a, scalar1=s1, scalar2=None, op0=o1),
                   reads=[TK, "lam"], writes=[TK])
            else:
                op("dve", lambda h: h.tensor_scalar(out=out, in0=a, scalar1=s1, scalar2=s2, op0=o1, op1=o2),
                   reads=[TK, "lam"], writes=[TK])

        def act(out, a, f, **kw):
            op("act", lambda h: h.activation(out=out, in_=a, func=f, **kw), reads=[TK, "lam"], writes=[TK])

        dt_, lr_, a_, th_, mag_ = tb[:, 0, :], tb[:, 1, :], tb[:, 2, :], tb[:, 3, :], tb[:, 4, :]
        act(dt_, lam[:, 2, :], AF.Exp)
        ts(lr_, lam[:, 0, :], -1e-4, ALU.min)
        tt(a_, lr_, dt_, ALU.mult)
        tt(th_, lam[:, 1, :], dt_, ALU.mult)
        act(mag_, a_, AF.Exp)

        def sin_of(out, src, shift):
            u, kf, r, g = tb[:, 5, :], tb[:, 6, :], tb[:, 7, :], tb[:, 8, :]
            ts(u, src, shift, ALU.add, 1.0 / (2 * PI), ALU.mult)
            op("dve", lambda h: h.tensor_copy(out=tbi[:], in_=u), reads=[TK], writes=[TK])
            op("dve", lambda h: h.tensor_copy(out=kf, in_=tbi[:]), reads=[TK], writes=[TK])
            ts(kf, kf, -2 * PI, ALU.mult, shift, ALU.add)
            tt(r, src, kf, ALU.add)
            ts(g, r, PI, ALU.is_gt, -2 * PI, ALU.mult)
            tt(r, r, g, ALU.add)
            ts(g, r, -PI, ALU.is_lt, 2 * PI, ALU.mult)
            tt(r, r, g, ALU.add)
            act(out, r, AF.Sin)

        sn_, cs_ = tb[:, 9, :], tb[:, 10, :]
        sin_of(sn_, th_, 0.0)
        sin_of(cs_, th_, PI / 2)
        tt(AR[:, 0, :], mag_, cs_, ALU.mult)
        tt(AI[:, 0, :], mag_, sn_, ALU.mult)
        for s in range(10):
            t1, t2 = tb[:, 11, :], tb[:, 12, :]
            tt(t1, AR[:, s, :], AR[:, s, :], ALU.mult)
            tt(t2, AI[:, s, :], AI[:, s, :], ALU.mult)
            tt(AR[:, s + 1, :], t1, t2, ALU.subtract)
            tt(t1, AR[:, s, :], AI[:, s, :], ALU.mult)
            ts(AI[:, s + 1, :], t1, 2.0, ALU.mult)
        ts(NAI[:], AI[:], -1.0, ALU.mult)
        den, nr, Fr, Fi = tb[:, 13, :], tb[:, 14, :], tb[:, 15, :], tb[:, 16, :]
        t1, t2 = tb[:, 11, :], tb[:, 12, :]
        tt(t1, lr_, lr_, ALU.mult)
        tt(t2, lam[:, 1, :], lam[:, 1, :], ALU.mult)
        tt(den, t1, t2, ALU.add)
        op("dve", lambda h: h.reciprocal(out=den, in_=den), reads=[TK], writes=[TK])
        ts(nr, AR[:, 0, :], -1.0, ALU.add)
        tt(t1, nr, lr_, ALU.mult)
        tt(t2, AI[:, 0, :], lam[:, 1, :], ALU.mult)
        tt(t1, t1, t2, ALU.add)
        tt(Fr, t1, den, ALU.mult)
        tt(t1, AI[:, 0, :], lr_, ALU.mult)
        tt(t2, nr, lam[:, 1, :], ALU.mult)
        tt(t1, t1, t2, ALU.subtract)
        tt(Fi, t1, den, ALU.mult)

        bb = [c.sb([128, 16, 128], F32, "bb_re"), c.sb([128, 16, 128], F32, "bb_im")]
        cp = [c.sb([128, 16, 128], F32, "cp_re"), c.sb([128, 16, 128], F32, "cp_im")]
        PR = c.sb([128, 9, 16], F32, "PR")
        PI_ = c.sb([128, 9, 16], F32, "PI")
        A8R = c.sb([128, 8, 16], F32, "A8R")
        A8I = c.sb([128, 8, 16], F32, "A8I")
        NA8I = c.sb([128, 8, 16], F32, "NA8I")
        Kblk = c.sb([128, 4, 8, 128], BF16, "Kblk")
        for i in range(2):
            c.dma("sp", cp[i][:], cp_d[i], writes=[("cp", i)], key=f"k_cp{i}")
        op("dve", lambda h: h.memset(PR[:, 0, :], 1.0), reads=[TK], writes=[TK])
        op("dve", lambda h: h.memset(PI_[:, 0, :], 0.0), reads=[TK], writes=[TK])
        ts(PR[:, 1, :], AR[:, 0, :], 1.0, ALU.mult)
        ts(PI_[:, 1, :], AI[:, 0, :], 1.0, ALU.mult)
        for j in range(1, 8):
            t1, t2 = tb[:, 11, :], tb[:, 12, :]
            tt(t1, PR[:, j, :], AR[:, 0, :], ALU.mult)
            tt(t2, PI_[:, j, :], AI[:, 0, :], ALU.mult)
            tt(PR[:, j + 1, :], t1, t2, ALU.subtract)
            tt(t1, PR[:, j, :], AI[:, 0, :], ALU.mult)
            tt(t2, PI_[:, j, :], AR[:, 0, :], ALU.mult)
            tt(PI_[:, j + 1, :], t1, t2, ALU.add)
        ts(A8R[:, 0, :], AR[:, 3, :], 1.0, ALU.mult)
        ts(A8I[:, 0, :], AI[:, 3, :], 1.0, ALU.mult)
        for s_ in range(1, 8):
            ts(A8R[:, s_, :], AR[:, 3 + s_, :], 1.0, ALU.mult)
            ts(A8I[:, s_, :], AI[:, 3 + s_, :], 1.0, ALU.mult)
        ts(NA8I[:], A8I[:], -1.0, ALU.mult)
        NPI = c.sb([128, 9, 16], F32, "NPI")
        ts(NPI[:], PI_[:], -1.0, ALU.mult)
        with c.scope():
            bt = [c.sb([128, 16, 128], F32, "bt_re"), c.sb([128, 16, 128], F32, "bt_im")]
            w1t = c.sb([128, 16, 128], F32, "w1t")
            w2t = c.sb([128, 16, 128], F32, "w2t")
            sre = c.sb([128, 16, 128], F32, "sre")
            sim = c.sb([128, 16, 128], F32, "sim")
            for i in range(2):
                c.dma("sp", bt[i][:], bt_d[i], writes=[("bt", i)], key=f"k_bt{i}")
            Frb = Fr.unsqueeze(2).broadcast_to([128, 16, 128])
            Fib = Fi.unsqueeze(2).broadcast_to([128, 16, 128])

            def t3(out, a, b, o, rk, wk, e="dve"):
                op(e, lambda h: h.tensor_tensor(out=out, in0=a, in1=b, op=o), reads=rk + [TK], writes=wk)

            t3(w1t[:], bt[0][:], Frb, ALU.mult, [("bt", 0)], ["w1t"])
            t3(w2t[:], bt[1][:], Fib, ALU.mult, [("bt", 1)], ["w2t"])
            t3(bb[0][:], w1t[:], w2t[:], ALU.subtract, ["w1t", "w2t"], [("bb", 0)])
            t3(w1t[:], bt[1][:], Frb, ALU.mult, [("bt", 1)], ["w1t"])
            t3(w2t[:], bt[0][:], Fib, ALU.mult, [("bt", 0)], ["w2t"])
            t3(bb[1][:], w1t[:], w2t[:], ALU.add, ["w1t", "w2t"], [("bb", 1)])
            for tau in range(8):
                prb = PR[:, tau, :].unsqueeze(2).broadcast_to([128, 16, 128])
                pib = PI_[:, tau, :].unsqueeze(2).broadcast_to([128, 16, 128])
                t3(w1t[:], bb[0][:], prb, ALU.mult, [("bb", 0)], ["w1t"])
                t3(w2t[:], bb[1][:], pib, ALU.mult, [("bb", 1)], ["w2t"])
                t3(sre[:], w1t[:], w2t[:], ALU.subtract, ["w1t", "w2t"], ["sre"])
                t3(w1t[:], bb[0][:], pib, ALU.mult, [("bb", 0)], ["w1t"])
                t3(w2t[:], bb[1][:], prb, ALU.mult, [("bb", 1)], ["w2t"])
                t3(sim[:], w1t[:], w2t[:], ALU.add, ["w1t", "w2t"], ["sim"])
                op("dve", lambda h: h.tensor_scalar(out=sim[:], in0=sim[:], scalar1=-1.0, scalar2=None, op0=ALU.mult),
                   reads=["sim"], writes=["sim"])
                for q in range(4):
                    b_ = nb()
                    n_ = 0
                    for kk in range(4):
                        for (a_, c_, ak) in ((sre, cp[0], "sre"), (sim, cp[1], "sim")):
                            mm(b_, banks[b_][:, 0:128], a_[:, 4 * q + kk, :], c_[:, 4 * q + kk, :], n_ == 0, n_ == 7,
                               [ak, ("cp", 0), ("cp", 1)])
                            n_ += 1
                    op("act", lambda h, q=q, tau=tau, b_=b_: h.activation(out=Kblk[:, q, tau, :],
                                                                          in_=banks[b_][:, 0:128], func=AF.Copy),
                       writes=[BK(b_), "Kblk"])


        for sq in range(NSEQ):
            with c.scope():
                T0 = sq * L
                catA = c.sb([128, 4, L], BF16, "catA")

                def phaseA(part, outs):
                    with c.scope():
                        ncol = 512 if part == 0 else 1536
                        w_in = c.sb([128, 8, ncol], BF16, "w_in")
                        load_w(w_in, w_in_d, 8, ncol, f"k_w_in{part}", "w_in", cs0=(0 if part == 0 else 512))
                        XBt = [c.sb([128, 8, 512], BF16, "XBt0"), c.sb([128, 8, 512], BF16, "XBt1")]
                        xt = [c.sb([128, D], F32, f"xt{i}") for i in range(4)]
                        if part == 0:
                            sst = [c.sb([128, 8, 128], F32, f"sstA{i}") for i in range(4)]
                        for tt_ in range(4):
                            XB = XBt[tt_ % 2]
                            xk = ("XB", tt_ % 2)
                            for bl in range(4):
                                tbk = tt_ * 4 + bl
                                sl = tbk % 4
                                tok = T0 + tbk * 128
                                c.dma("sp", xt[sl][:], x_d[tok:tok + 128, :], writes=[("xt", sl)], key=f"k_xt{sl}")
                                b0, b1 = nb(), nb()
                                for fc in range(8):
                                    bq_ = b0 if fc < 4 else b1
                                    op("pe", lambda h, fc=fc, bq_=bq_, sl=sl: h.transpose(
                                        out=banks[bq_][:, (fc % 4) * 128:(fc % 4) * 128 + 128],
                                        in_=xt[sl][:, fc * 128:(fc + 1) * 128], identity=ident),
                                        reads=[("xt", sl), "cst"], writes=[BK(bq_)])
                                for fc in range(8):
                                    bq_ = b0 if fc < 4 else b1
                                    src = banks[bq_][:, (fc % 4) * 128:(fc % 4) * 128 + 128]
                                    if part == 0:
                                        op("dve", lambda h, fc=fc, src=src, sl=sl: h.tensor_scalar(
                                            out=sst[sl][:, fc, :], in0=src, scalar1=ALPHA, scalar2=V("b_out", fc, 1),
                                            op0=ALU.mult, op1=ALU.add),
                                            reads=["vec"], writes=[BK(bq_), ("sstA", sl)])
                                    if part == 0 or fc % 2 == 0:
                                        op("act", lambda h, fc=fc, src=src, bl=bl: h.activation(
                                            out=XB[:, fc, bl * 128:(bl + 1) * 128], in_=src, func=AF.Copy),
                                            writes=[BK(bq_), xk])
                                    else:
                                        op("dve", lambda h, fc=fc, src=src, bl=bl: h.tensor_copy(
                                            out=XB[:, fc, bl * 128:(bl + 1) * 128], in_=src),
                                            writes=[BK(bq_), xk])
                                if part == 0:
                                    c.dma("pool", S_v[:, :, tok:tok + 128], sst[sl][:], reads=[("sstA", sl)],
                                          writes=[("S_d", tok)], key=f"k_sstA{sl}")
                            cs = slice(tt_ * 512, (tt_ + 1) * 512)
                            for oc in range(4 if part == 0 else 8):
                                b_ = nb()
                                for kc in range(8):
                                    mm(b_, banks[b_][:, :], w_in[:, kc, oc * 128:(oc + 1) * 128], XB[:, kc, :],
                                       kc == 0, kc == 7, wreads("w_in", kc, ncol) + [xk])
                                if part == 0:
                                    u_f = outs[0]
                                    op("act", lambda h, oc=oc, b_=b_, cs=cs: h.activation(
                                        out=u_f[:, oc, cs], in_=banks[b_][:, :], func=AF.Identity,
                                        bias=V("b_in", oc, 1), scale=1.0), reads=["vec"],
                                        writes=[BK(b_), ("u_f", tt_)])
                                elif oc < 4:
                                    qT = outs[0]
                                    op("act", lambda h, oc=oc, b_=b_, cs=cs: h.activation(
                                        out=qT[:, oc, cs], in_=banks[b_][:, :], func=AF.Identity,
                                        bias=bq8[:, oc:oc + 1], scale=0.125), reads=["der"],
                                        writes=[BK(b_), ("qT", tt_)])
                                else:
                                    kT, nkT = outs[1], outs[2]
                                    op("act", lambda h, oc=oc, b_=b_, cs=cs: h.activation(
                                        out=kT[:, oc - 4, cs], in_=banks[b_][:, :], func=AF.Identity,
                                        bias=V("b_in", 4 + oc, 1), scale=1.0), reads=["vec"],
                                        writes=[BK(b_), ("kT", tt_)])
                                    op("dve", lambda h, oc=oc, cs=cs: h.tensor_scalar(
                                        out=nkT[:, oc - 4, cs], in0=kT[:, oc - 4, cs], scalar1=-1.0, scalar2=None,
                                        op0=ALU.mult), reads=[("kT", tt_)], writes=[("nkT", tt_)])
                            if part == 1:
                                Vt = outs[3]
                                for bl in range(4):
                                    tbk = tt_ * 4 + bl
                                    b_ = nb()
                                    for kc in range(8):
                                        mm(b_, banks[b_][:, :], XB[:, kc, bl * 128:(bl + 1) * 128],
                                           w_in[:, kc, 1024:1536], kc == 0, kc == 7,
                                           wreads("w_in", kc, ncol) + [xk])
                                    op("dve", lambda h, tbk=tbk, b_=b_: h.tensor_tensor(
                                        out=Vt[:, tbk, :], in0=banks[b_][:, :], in1=bv[:], op=ALU.add),
                                        reads=["bv"], writes=[BK(b_), ("Vt", tbk)])

                with c.scope():
                    u_f = c.sb([128, 4, L], BF16, "u_f")
                    phaseA(0, [u_f])
                    w_glu = c.sb([128, 4, 1024], BF16, "w_glu")
                    load_w(w_glu, w_glu_d, 4, 1024, "k_w_glu", "w_glu")
                    YB = (0, 1, 2, 3)
                    WB = (4, 5, 6, 7)
                    XAs = [[c.sb([128, 2, 256], F32, f"XA{a}{b}") for b in range(2)] for a in range(2)]
                    Xc = [c.sb([128, 2, 256], BF16, f"Xc{i}") for i in range(2)]
                    Hf = [c.sb([128, 8, 2, 128], F32, f"Hf{i}") for i in range(2)]
                    Hp = [c.sb([128, 8, 2, 128], BF16, f"Hp{i}") for i in range(2)]
                    Gp = [c.sb([128, 8, 2, 128], BF16, f"Gp{i}") for i in range(4)]
                    tmpg = c.sb([128, 128], F32, "tmpg")
                    zfull = c.sb([128, L], F32, "zfull")
                    gfull = c.sb([128, L], F32, "gfull")
                    yg = c.sb([128, 4, L], BF16, "yg")
                    sg_ = [c.sb([128, 512], F32, "sg0"), c.sb([128, 512], F32, "sg1")]
                    tmpA = c.sb([128, 8, 128], F32, "tmpA")
                    tmpB = c.sb([128, 8, 128], F32, "tmpB")

                    def uq_of(q):
                        return u_f[:, q, :].rearrange("p (c i) -> p i c", i=8)

                    UK = [("u_f", t_) for t_ in range(4)]

                    def tab_ops(p):
                        k, sl = p, p % 2
                        th = []
                        rk = [TK, ("bb", 0), ("bb", 1), ("cp", 0), ("cp", 1)]
                        if sq == 1:
                            th.append(lambda: c.dma("sp", Hp[sl][:].rearrange("p a b n -> p (a b n)"), HG_d[p, 0],
                                                    reads=[("HG_d", p, 0)],
                                                    writes=[("Hp", sl, jj) for jj in range(8)], key=f"k_hpl{sl}"))
                            th.append(lambda: c.dma("sp", Gp[p % 4][:].rearrange("p a b n -> p (a b n)"), HG_d[p, 1],
                                                    reads=[("HG_d", p, 1)],
                                                    writes=[(("Gp", p % 4), i, r) for i in range(8) for r in range(2)],
                                                    key=f"k_gpl{p % 4}"))
                            return th

                        def add(fn, reads, writes):
                            th.append(lambda: op("dve", fn, reads=reads, writes=writes))

                        for (o, a_re, a_im, j0, wk, neg) in ((Hf[sl], bb[0][:, k, :], bb[1][:, k, :], 0, ("Hf", sl), False),
                                                             (Gp[p % 4], cp[0][:, k, :], cp[1][:, k, :], 1, ("Gp", p % 4),
                                                              True)):
                            tA = ("tmpA", neg)
                            tB = ("tmpB", neg)
                            ta = tmpA if not neg else tmpB
                            for j in range(8):
                                si = PI_[:, j0 + j, k:k + 1]
                                add(lambda h, j=j, si=si, ta=ta, a_im=a_im: h.tensor_scalar(
                                    out=ta[:, j, :], in0=a_im, scalar1=si, scalar2=None, op0=ALU.mult), rk, [(tA, j)])
                            for j in range(8):
                                sr = PR[:, j0 + j, k:k + 1]
                                add(lambda h, j=j, sr=sr, ta=ta, a_re=a_re, o=o: h.scalar_tensor_tensor(
                                    out=o[:, j, 0, :], in0=a_re, scalar=sr, in1=ta[:, j, :], op0=ALU.mult,
                                    op1=ALU.subtract), rk + [(tA, j)], [(wk, j, 0)])
                            for j in range(8):
                                sr = PR[:, j0 + j, k:k + 1]
                                if neg:
                                    add(lambda h, j=j, sr=sr, ta=ta, a_im=a_im: h.tensor_scalar(
                                        out=ta[:, j, :], in0=a_im, scalar1=sr, scalar2=-1.0, op0=ALU.mult, op1=ALU.mult),
                                        rk, [(tA, j)])
                                else:
                                    add(lambda h, j=j, sr=sr, ta=ta, a_im=a_im: h.tensor_scalar(
                                        out=ta[:, j, :], in0=a_im, scalar1=sr, scalar2=None, op0=ALU.mult), rk, [(tA, j)])
                            for j in range(8):
                                si = (NPI if neg else PI_)[:, j0 + j, k:k + 1]
                                add(lambda h, j=j, si=si, ta=ta, a_re=a_re, o=o: h.scalar_tensor_tensor(
                                    out=o[:, j, 1, :], in0=a_re, scalar=si, in1=ta[:, j, :], op0=ALU.mult, op1=ALU.add),
                                    rk + [(tA, j)], [(wk, j, 1)])
                        return th

                    def emit_transposes_and_V(p):
                        k, sl, q = p, p % 2, p // 4
                        uq = uq_of(q)
                        for jj in range(8 if sq == 0 else 0):
                            b_ = nb(WB)
                            for r in range(2):
                                op("pe", lambda h, jj=jj, r=r, b_=b_: h.transpose(
                                    out=banks[b_][:, r * 128:(r + 1) * 128], in_=Hf[sl][:, jj, r, :], identity=ident),
                                    reads=[(("Hf", sl), jj, r), "cst"], writes=[BK(b_)])
                            op("act", lambda h, jj=jj, b_=b_: h.activation(
                                out=Hp[sl][:, jj, :, :], in_=banks[b_][:, 0:256].rearrange("p (r n) -> p r n", r=2),
                                func=AF.Copy), writes=[BK(b_), ("Hp", sl, jj)])
                        if sq == 0:
                            c.dma("pool", HG_d[p, 0], Hp[sl][:].rearrange("p a b n -> p (a b n)"),
                                  reads=[("Hp", sl, jj) for jj in range(8)], writes=[("HG_d", p, 0)], key=f"k_hps{sl}")
                            c.dma("pool", HG_d[p, 1], Gp[p % 4][:].rearrange("p a b n -> p (a b n)"),
                                  reads=[(("Gp", p % 4), i, r) for i in range(8) for r in range(2)],
                                  writes=[("HG_d", p, 1)], key=f"k_gps{p % 4}")
                        b_ = nb(WB)
                        for r in range(2):
                            for j in range(8):
                                mm(b_, banks[b_][:, r * 256:(r + 1) * 256], Hp[sl][:, 7 - j, r, :], uq[:, j, :],
                                   j == 0 and r == 0, j == 7 and r == 1, [("Hp", sl, 7 - j)] + UK)
                        op("act", lambda h, b_=b_: h.activation(
                            out=XAs[sl][0][:], in_=banks[b_][:, :].rearrange("p (r n) -> p r n", r=2), func=AF.Copy),
                            writes=[BK(b_), ("XA", sl, 0)])

                    def ks_ops(p):
                        k, sl = p, p % 2
                        th = []
                        for s in range(8):
                            d = 1 << s
                            sa, da = s % 2, 1 - (s % 2)
                            src, dst = XAs[sl][sa], XAs[sl][da]
                            last = (s == 7)
                            o = Xc[sl] if last else dst
                            ok = ("Xc", sl) if last else ("XA", sl, da)
                            kr = [("XA", sl, sa)]

                            def stt(out, in0, sc, in1, rk, wk):
                                th.append(lambda: op("dve", lambda h: h.scalar_tensor_tensor(
                                    out=out, in0=in0, scalar=sc, in1=in1, op0=ALU.mult, op1=ALU.add),
                                    reads=rk + [TK], writes=wk))
                            th.append(lambda o=o, src=src, d=d, kr=kr, ok=ok: op(
                                "pool", lambda h: h.tensor_copy(out=o[:, :, 0:d], in_=src[:, :, 0:d]), reads=kr,
                                writes=[ok]))
                            stt(dst[:, 0, d:], src[:, 0, 0:256 - d], A8R[:, s, k:k + 1], src[:, 0, d:], kr, [("XA", sl, da)])
                            stt(dst[:, 1, d:], src[:, 0, 0:256 - d], A8I[:, s, k:k + 1], src[:, 1, d:], kr, [("XA", sl, da)])
                            stt(o[:, 0, d:], src[:, 1, 0:256 - d], NA8I[:, s, k:k + 1], dst[:, 0, d:],
                                kr + [("XA", sl, da)], [ok])
                            stt(o[:, 1, d:], src[:, 1, 0:256 - d], A8R[:, s, k:k + 1], dst[:, 1, d:],
                                kr + [("XA", sl, da)], [ok])
                        return th

                    def toeplitz(q):
                        uq = uq_of(q)
                        for b in range(4):
                            first = True
                            for i in (2 * b, 2 * b + 1):
                                for tau in range(i + 1):
                                    mm(YB[b], banks[YB[b]][:, (i % 2) * 256:(i % 2) * 256 + 256], Kblk[:, q, tau, :],
                                       uq[:, i - tau, :], first, False, ["Kblk"] + UK)
                                    first = False

                    def farfield(p):
                        sl, kk = p % 2, p % 4
                        for i in range(8):
                            for r in range(2):
                                mm(YB[i // 2], banks[YB[i // 2]][:, (i % 2) * 256 + 1:(i % 2) * 256 + 256],
                                   Gp[p % 4][:, i, r, :], Xc[sl][:, r, 0:255], False, (kk == 3 and i % 2 == 1 and r == 1),
                                   [(("Gp", p % 4), i, r), ("Xc", sl)])

                    def zgelu(q):
                        uq = uq_of(q)
                        zv = zfull[:, :].rearrange("p (c i) -> p i c", i=8)
                        for b in range(4):
                            op("dve", lambda h, b=b: h.scalar_tensor_tensor(
                                out=zv[:, 2 * b:2 * b + 2, :], in0=uq[:, 2 * b:2 * b + 2, :], scalar=V("s5_d", q, 1),
                                in1=banks[YB[b]][:, :].rearrange("p (i c) -> p i c", i=2), op0=ALU.mult, op1=ALU.add),
                                reads=UK + ["vec"], writes=[BK(YB[b]), "zfull"])
                        op("act", lambda h: h.activation(out=gfull[:], in_=zfull[:], func=AF.Square),
                           reads=["zfull"], writes=["gfull"])
                        op("dve", lambda h: h.tensor_scalar(out=gfull[:], in0=gfull[:], scalar1=0.044715, scalar2=1.0,
                                                            op0=ALU.mult, op1=ALU.add), reads=["gfull"], writes=["gfull"])
                        op("dve", lambda h: h.tensor_tensor(out=gfull[:], in0=gfull[:], in1=zfull[:], op=ALU.mult),
                           reads=["gfull", "zfull"], writes=["gfull"])
                        op("act", lambda h: h.activation(out=gfull[:], in_=gfull[:], func=AF.Sigmoid,
                                                         scale=1.5957691216057308), reads=["gfull"], writes=["gfull"])
                        op("dve", lambda h: h.tensor_tensor(out=yg[:, q, :], in0=gfull[:], in1=zfull[:], op=ALU.mult),
                           reads=["gfull", "zfull"], writes=[("yg", q, t_) for t_ in range(4)])

                    def interleave(*lists):
                        idx = [0] * len(lists)
                        more = True
                        while more:
                            more = False
                            for li, l_ in enumerate(lists):
                                if idx[li] < len(l_):
                                    l_[idx[li]]()
                                    idx[li] += 1
                                    more = True

                    interleave(tab_ops(0) + tab_ops(1))
                    for pp in range(8):
                        p0, p1 = 2 * pp, 2 * pp + 1
                        emit_transposes_and_V(p0)
                        emit_transposes_and_V(p1)
                        if p0 % 4 == 0:
                            toeplitz(p0 // 4)
                        nxt = (tab_ops(p0 + 2) + tab_ops(p1 + 2)) if p0 + 2 < 16 else []
                        interleave(ks_ops(p0), ks_ops(p1), nxt[0::2], nxt[1::2])
                        farfield(p0)
                        farfield(p1)
                        if p1 % 4 == 3:
                            zgelu(p1 // 4)
                    for tt_ in range(4):
                        cs = slice(tt_ * 512, (tt_ + 1) * 512)
                        for oc in range(4):
                            bv_, bg_ = nb(), nb()
                            sl = oc % 2
                            for kc in range(4):
                                mm(bv_, banks[bv_][:, :], w_glu[:, kc, oc * 128:(oc + 1) * 128], yg[:, kc, cs],
                                   kc == 0, kc == 3, wreads("w_glu", kc, 1024) + [("yg", kc, tt_)])
                            for kc in range(4):
                                mm(bg_, banks[bg_][:, :], w_glu[:, kc, 512 + oc * 128:512 + (oc + 1) * 128],
                                   yg[:, kc, cs], kc == 0, kc == 3,
                                   wreads("w_glu", kc, 1024) + [("yg", kc, tt_)])
                            op("act", lambda h, oc=oc, bg_=bg_, sl=sl: h.activation(
                                out=sg_[sl][:], in_=banks[bg_][:, :], func=AF.Sigmoid,
                                bias=V("b_glu", 4 + oc, 1), scale=1.0), reads=["vec"],
                                writes=[BK(bg_), ("sg", sl)])
                            op("dve", lambda h, oc=oc, bv_=bv_, sl=sl, cs=cs: h.scalar_tensor_tensor(
                                out=catA[:, oc, cs], in0=banks[bv_][:, :], scalar=V("b_glu", oc, 1), in1=sg_[sl][:],
                                op0=ALU.add, op1=ALU.mult), reads=["vec", ("sg", sl)],
                                writes=[BK(bv_), ("catA", oc, tt_)])

                catB = c.sb([128, 4, L], BF16, "catB")
                with c.scope():
                    qT = c.sb([128, 4, L], BF16, "qT")
                    kT = c.sb([128, 4, L], BF16, "kT")
                    nkT = c.sb([128, 4, L], BF16, "nkT")
                    Vt = c.sb([128, 16, 512], BF16, "Vt")
                    phaseA(1, [qT, kT, nkT, Vt])
                    ZB = (0, 1, 2)
                    BB = (3, 4, 5)
                    OB2 = (6, 7)
                    NBUF = 4
                    FP16 = mybir.dt.float16
                    ebuf = [c.sb([128, 512], F32, f"ebuf{i}") for i in range(NBUF)]
                    spb = [c.sb([128, 512], BF16, f"spb{i}") for i in range(NBUF)]
                    Pb = [c.sb([128, 512], BF16, f"Pb{i}") for i in range(NBUF)]
                    zrow = [c.sb([1, 512], F32, f"zrow{i}") for i in range(NBUF)]
                    A16 = [c.sb([1, 512], FP16, f"A16_{i}") for i in range(2)]
                    onesr = c.sb([1, 128], BF16, "onesr")
                    op("dve", lambda h: h.memset(onesr[:], 1.0), writes=["onesr"])
                    items = []
                    for m_ in range(4):
                        for qt in range(4):
                            for kb in range(4 * qt + 3, -1, -1):
                                for st in range(2):
                                    items.append((2 * m_ + st, qt, kb, st))

                    def geo(i):
                        hd, qt, kb, st = items[i]
                        r = max(0, kb - 4 * qt)
                        return dict(hd=hd, qt=qt, kb=kb, st=st, ch=hd // 2, pb=(hd % 2) * 64, r=r, diag=kb >= 4 * qt,
                                    c0=r * 128, qs=slice(qt * 512 + r * 128, (qt + 1) * 512),
                                    ks=slice(kb * 128, (kb + 1) * 128), sl=i % NBUF,
                                    zb=ZB[i % 3], bb=BB[i % 3], ob=OB2[st],
                                    first=(kb == 4 * qt + 3), last=(kb == 0))

                    def stageA(ii):
                        gs = [geo(i) for i in ii]
                        for g in gs:
                            mm(g["zb"], banks[g["zb"]][:, g["c0"]:], kT[g["pb"]:g["pb"] + 64, g["ch"], g["ks"]],
                               qT[g["pb"]:g["pb"] + 64, g["ch"], g["qs"]], True, not g["diag"],
                               [("kT", g["kb"] // 4), ("qT", g["qt"])])
                        for g in gs:
                            if g["diag"]:
                                mm(g["zb"], banks[g["zb"]][:, g["c0"]:g["c0"] + 128], identb, nmaskb, False, True, ["cstb"])
                        for g in gs:
                            zb_, sl, c0 = g["zb"], g["sl"], g["c0"]
                            op("act", lambda h: h.activation(out=ebuf[sl][:, c0:], in_=banks[zb_][:, c0:], func=AF.Exp),
                               writes=[BK(zb_), ("ebuf", sl)])
                            if not g["last"]:
                                op("dve", lambda h: h.tensor_copy(out=zrow[sl][0:1, c0:], in_=banks[zb_][0:1, c0:]),
                                   writes=[BK(zb_), ("zrow", sl)])
                            op("act", lambda h: h.activation(out=spb[sl][:, c0:], in_=ebuf[sl][:, c0:], func=AF.Ln,
                                                             bias=1.0, scale=1.0),
                               reads=[("ebuf", sl)], writes=[("spb", sl)])

                    def stageB(ii):
                        gs = [geo(i) for i in ii]
                        for g in gs:
                            mm(g["bb"], banks[g["bb"]][:, g["c0"]:], trib, spb[g["sl"]][:, g["c0"]:], True, False,
                               ["cstb", ("spb", g["sl"])])
                        for g in gs:
                            mm(g["bb"], banks[g["bb"]][:, g["c0"]:], nkT[g["pb"]:g["pb"] + 64, g["ch"], g["ks"]],
                               qT[g["pb"]:g["pb"] + 64, g["ch"], g["qs"]], False, False,
                               [("nkT", g["kb"] // 4), ("qT", g["qt"])])
                        for g in gs:
                            if g["diag"]:
                                mm(g["bb"], banks[g["bb"]][:, g["c0"]:g["c0"] + 128], identb, pmaskb, False, g["first"],
                                   ["cstb"])
                        for g in gs:
                            if not g["first"]:
                                mm(g["bb"], banks[g["bb"]][:, g["c0"]:], onesr[:], A16[g["st"]][0:1, g["c0"]:], False,
                                   True, ["onesr", ("A16", g["st"])])
                        for g in gs:
                            bb_, sl, c0, st = g["bb"], g["sl"], g["c0"], g["st"]
                            op("act", lambda h: h.activation(out=Pb[sl][:, c0:], in_=banks[bb_][:, c0:], func=AF.Exp,
                                                             scale=-1.0),
                               writes=[BK(bb_), ("Pb", sl)])
                            if g["first"]:
                                op("dve", lambda h: h.memset(A16[st][:], 0.0), writes=[("A16", st)])
                            if not g["last"]:
                                op("dve", lambda h: h.tensor_tensor(out=A16[st][0:1, c0:], in0=banks[bb_][0:1, c0:],
                                                                    in1=zrow[sl][0:1, c0:], op=ALU.add),
                                   reads=[("zrow", sl)], writes=[BK(bb_), ("A16", st)])

                    def stageC(ii):
                        gs = [geo(i) for i in ii]
                        for g in gs:
                            ob, sl, c0, pb, hd, kb = g["ob"], g["sl"], g["c0"], g["pb"], g["hd"], g["kb"]
                            mm(ob, banks[ob][pb:pb + 64, c0:], Vt[:, kb, hd * 64:(hd + 1) * 64], Pb[sl][:, c0:],
                               g["first"], g["last"], [("Pb", sl), ("Vt", kb)], skip_group_check=True)
                        for g in gs:
                            if g["last"]:
                                ob, pb, ch, qt = g["ob"], g["pb"], g["ch"], g["qt"]
                                op("dve", lambda h: h.tensor_copy(out=catB[pb:pb + 64, ch, qt * 512:(qt + 1) * 512],
                                                                  in_=banks[ob][pb:pb + 64, :]),
                                   writes=[BK(ob), ("catB", ch, qt, pb)])

                    n_st = len(items) // 2
                    for step in range(n_st + 2):
                        if step < n_st:
                            stageA((2 * step, 2 * step + 1))
                        if 0 <= step - 1 < n_st:
                            stageB((2 * (step - 1), 2 * (step - 1) + 1))
                        if 0 <= step - 2 < n_st:
                            stageC((2 * (step - 2), 2 * (step - 2) + 1))

                with c.scope():
                    N = 512
                    if dbg:
                        c.dma("sp", dC_v[:, 0:4, T0:T0 + L], catA[:], writes=["dC0"], key="k_dbg")
                        c.dma("sp", dC_v[:, 4:8, T0:T0 + L], catB[:], writes=["dC1"], key="k_dbg")
                    w_out = c.sb([128, 8, 1024], BF16, "w_out")
                    load_w(w_out, w_out_d, 8, 1024, "k_w_out", "w_out")
                    lnb = ln_scratch(N)
                    ztD = [c.sb([128, 8, N], F32, f"ztD{i}") for i in range(2)]
                    sin_t = c.sb([128, 8, N], F32, "sinD")
                    xbt = c.sb([128, 8, N], BF16, "xbtD")

                    def d_mm(tt_):
                        tok0 = T0 + tt_ * N
                        cs = slice(tt_ * N, (tt_ + 1) * N)
                        z = ztD[tt_ % 2]
                        zk = ("ztD", tt_ % 2)
                        c.dma("sp", sin_t[:], S_v[:, :, tok0:tok0 + N], writes=["sinD"], key="k_sinD")
                        for oc in range(8):
                            b_ = nb()
                            for kc in range(8):
                                src = catA[:, kc, cs] if kc < 4 else catB[:, kc - 4, cs]
                                mm(b_, banks[b_][:, :], w_out[:, kc, oc * 128:(oc + 1) * 128], src,
                                   kc == 0, kc == 7, wreads("w_out", kc, 1024))
                            op("dve", lambda h: h.tensor_tensor(out=z[:, oc, :], in0=banks[b_][:, :],
                                                                in1=sin_t[:, oc, :], op=ALU.add),
                               reads=["sinD"], writes=[BK(b_), zk])

                    def d_ln(tt_):
                        tok0 = T0 + tt_ * N
                        z = ztD[tt_ % 2]
                        zk = ("ztD", tt_ % 2)
                        layer_norm(z, N, zk, z, zk, lnb)
                        ln_epilogue_stream(z, zk, N, tok0, ga10, ba10, "ln1_g0", "ln1_b0", z, xbt, 0, sstkey=zk)

                    for tt_ in range(4):
                        d_mm(tt_)
                        if tt_ > 0:
                            d_ln(tt_ - 1)
                    d_ln(3)


    def dbg_stop(tag):
        if dbg and dbg.get("stop") == tag:
            c.barrier()
            c.dma("sp", dS_d, S_d, writes=["dS"], key="k_dbg")
            c.dma("sp", dX_d, XB_d, writes=["dX"], key="k_dbg")
            c.barrier()
            c.emit()
            c.close()
            return True
        return False

    if dbg_stop("D"):
        return nc
    H2_d = nc.dram_tensor("H2_scr", [D, T], BF16, kind="Internal").ap()
    H2_v = H2_d.rearrange("(c p) t -> p c t", p=128)

    def ffn_phase(layer, final, ga, ba, gname, bname):
        N = 512
        NTL = T // N
        NQ = 4
        b1n = f"fb1_{layer}"
        with c.scope():
            w1s = [c.sb([128, 8, 1024], BF16, f"w1s{i}") for i in range(2)]
            w2s = [c.sb([128, 8, 1024], BF16, f"w2s{i}") for i in range(2)]
            zt = [c.sb([128, 8, N], F32, f"ztE{i}") for i in range(2)]
            xin = [c.sb([128, 8, N], BF16, f"xinE{i}") for i in range(2)]
            sin2 = [c.sb([128, 8, N], F32, f"sinE{i}") for i in range(2)]
            hb = c.sb([128, 8, N], BF16, "hb")
            rl = [c.sb([128, N], F32, f"rl{i}") for i in range(2)]
            lnb = ln_scratch(N)
            xbt = c.sb([128, 8, N], BF16, "xbtE")
            yo = c.sb([128, D], F32, "yo")

            def load_q(qr):
                sl = qr % 2
                load_w(w1s[sl], w1_d[layer], 8, 1024, f"k_w1s{sl}", ("w1s", sl), cs0=qr * 1024)
                load_w(w2s[sl], w2_d[layer][qr * 1024:(qr + 1) * 1024, :], 8, 1024, f"k_w2s{sl}", ("w2s", sl))

            def part_h(qr, tt_):
                sl = tt_ % 2
                ws = qr % 2
                tok0 = tt_ * N
                for hc in range(8):
                    b_ = nb()
                    s2 = hc % 2
                    hcg = qr * 8 + hc
                    for kc in range(8):
                        mm(b_, banks[b_][:, :], w1s[ws][:, kc, hc * 128:(hc + 1) * 128], xin[sl][:, kc, :],
                           kc == 0, kc == 7, wreads(("w1s", ws), kc, 1024) + [("xinE", sl)])
                    op("act", lambda h: h.activation(out=rl[s2][:], in_=banks[b_][:, :], func=AF.Relu,
                                                     bias=V(b1n, hcg, 1), scale=1.0),
                       reads=["vec"], writes=[BK(b_), ("rl", s2)])
                    op("dve", lambda h: h.scalar_tensor_tensor(
                        out=hb[:, hc, :], in0=banks[b_][:, :], scalar=V(b1n, hcg, 1), in1=rl[s2][:],
                        op0=ALU.add, op1=ALU.mult), reads=["vec", ("rl", s2)], writes=[BK(b_), ("hb", hc)])

            def part_out(qr, tt_, mode="store"):
                ws = qr % 2
                tok0 = tt_ * N
                z = zt[tt_ % 2]
                zk = ("ztE", tt_ % 2)
                sin_t = sin2[tt_ % 2]
                sk_ = ("sinE", tt_ % 2)
                for oc in range(8):
                    b_ = nb()
                    for hc in range(8):
                        mm(b_, banks[b_][:, :], w2s[ws][:, hc, oc * 128:(oc + 1) * 128], hb[:, hc, :],
                           hc == 0, hc == 7, wreads(("w2s", ws), hc, 1024) + [("hb", hc)])
                    if mode == "acc":
                        op("dve", lambda h: h.tensor_tensor(out=z[:, oc, :], in0=banks[b_][:, :], in1=z[:, oc, :],
                                                            op=ALU.add), reads=[], writes=[BK(b_), zk])
                    else:
                        op("dve", lambda h: h.tensor_tensor(out=z[:, oc, :], in0=banks[b_][:, :], in1=sin_t[:, oc, :],
                                                            op=ALU.add), reads=[sk_], writes=[BK(b_), zk])
                if mode == "store":
                    c.dma("pool", S_v[:, :, tok0:tok0 + N], z[:], reads=[zk], writes=[("S_d", tok0)],
                          key=f"k_zpart{tt_ % 2}")

            def loads(qr, tt_):
                sl = tt_ % 2
                tok0 = tt_ * N
                c.dma("sp", xin[sl][:], XB_v[:, :, tok0:tok0 + N], writes=[("xinE", sl)], key=f"k_xinE{sl}")
                c.dma("sp", sin2[sl][:], S_v[:, :, tok0:tok0 + N], reads=[("S_d", tok0)], writes=[("sinE", sl)],
                      key=f"k_sinE{sl}")

            def part_ln(tt_):
                tok0 = tt_ * N
                z = zt[tt_ % 2]
                zk = ("ztE", tt_ % 2)
                layer_norm(z, N, zk, z, zk, lnb)
                if not final:
                    ln_epilogue_stream(z, zk, N, tok0, ga, ba, gname, bname, z, xbt, 1, sstkey=zk)
                else:
                    for kc in range(8):
                        op("act", lambda h, kc=kc: h.activation(out=z[:, kc, :], in_=z[:, kc, :], func=AF.Identity,
                                                                bias=V(bname, kc, 1), scale=V(gname, kc, 1)),
                           reads=["vec"], writes=[zk])
                    for bl in range(N // 128):
                        b0, b1 = nb(), nb()
                        for fc in range(8):
                            bq_ = b0 if fc < 4 else b1
                            op("pe", lambda h, fc=fc, bq_=bq_: h.transpose(
                                out=banks[bq_][:, (fc % 4) * 128:(fc % 4) * 128 + 128],
                                in_=z[:, fc, bl * 128:(bl + 1) * 128], identity=ident),
                                reads=[zk, "cst"], writes=[BK(bq_)])
                        op("act", lambda h: h.activation(out=yo[:, 0:512], in_=banks[b0][:, :], func=AF.Copy),
                           writes=[BK(b0), "yo"])
                        op("dve", lambda h: h.tensor_copy(out=yo[:, 512:1024], in_=banks[b1][:, :]),
                           writes=[BK(b1), "yo"])
                        c.dma("sp", y_d[tok0 + bl * 128:tok0 + (bl + 1) * 128, :], yo[:],
                              reads=["yo"], writes=[("y", tok0, bl)], key="k_yo")

            load_q(0)
            load_q(1)
            for qr in range(2):
                loads(qr, 0)
                for tt_ in range(NTL):
                    if tt_ + 1 < NTL:
                        loads(qr, tt_ + 1)
                    part_h(qr, tt_)
                    part_out(qr, tt_, "store")
                    if qr == 0 and tt_ == NTL - 1:
                        load_q(2)
            load_q(3)
            loads(2, 0)
            for tt_ in range(NTL):
                if tt_ + 1 < NTL:
                    loads(2, tt_ + 1)
                part_h(2, tt_)
                part_out(2, tt_, "keep")
                part_h(3, tt_)
                if tt_ > 0:
                    part_ln(tt_ - 1)
                part_out(3, tt_, "acc")
            part_ln(NTL - 1)

    ffn_phase(0, False, ga20, ba20, "ln2_g0", "ln2_b0")
    if dbg_stop("E0"):
        return nc

    N = 512
    NTL = T // N
    with c.scope():
        pw1 = c.sb([128, 8, 2048], BF16, "pw1")
        load_w(pw1, pw1_d, 8, 2048, "k_pw1", "pw1")
        dg = c.sb([128, 8, 31, 128], BF16, "dg")
        for oc in range(8):
            for k in range(31):
                op("dve", lambda h, oc=oc, k=k: h.tensor_scalar(out=dg[:, oc, k, :], in0=ident,
                                                                scalar1=V("w_dw", k * 8 + oc, 1), scalar2=None,
                                                                op0=ALU.mult), reads=["cst", "vec"], writes=[("dg", oc)])
        lnb = ln_scratch(N)
        xin = [c.sb([128, 8, N], BF16, f"xinF{i}") for i in range(2)]
        hbuf = c.sb([128, 8, 30 + N], BF16, "hbuf")
        sgF = [c.sb([128, N], F32, f"sgF{i}") for i in range(2)]
        cvs = [c.sb([128, 8, N], F32, f"cv{i}") for i in range(2)]
        h2 = c.sb([128, 8, N], BF16, "h2")

        def f1_load(tt_):
            sl = tt_ % 2
            c.dma("sp", xin[sl][:], XB_v[:, :, tt_ * N:(tt_ + 1) * N], writes=[("xinF", sl)], key=f"k_xinF{sl}")

        def f1_glu(tt_):
            sl = tt_ % 2
            if tt_ % 4 == 0:
                op("dve", lambda h: h.memset(hbuf[:, :, 0:30], 0.0), writes=[("hbuf", i) for i in range(8)])
            else:
                for oc in range(8):
                    op("dve", lambda h, oc=oc: h.tensor_copy(out=hbuf[:, oc, 0:30], in_=hbuf[:, oc, N:N + 30]),
                       reads=[], writes=[("hbuf", oc)])
            for oc in range(8):
                bv_, bg_ = nb(), nb()
                s2 = oc % 2
                for kc in range(8):
                    mm(bv_, banks[bv_][:, :], pw1[:, kc, oc * 128:(oc + 1) * 128], xin[sl][:, kc, :],
                       kc == 0, kc == 7, wreads("pw1", kc, 2048) + [("xinF", sl)])
                for kc in range(8):
                    mm(bg_, banks[bg_][:, :], pw1[:, kc, 1024 + oc * 128:1024 + (oc + 1) * 128], xin[sl][:, kc, :],
                       kc == 0, kc == 7, wreads("pw1", kc, 2048) + [("xinF", sl)])
                op("act", lambda h: h.activation(out=sgF[s2][:], in_=banks[bg_][:, :], func=AF.Sigmoid,
                                                 bias=V("b_pw1", 8 + oc, 1), scale=1.0),
                   reads=["vec"], writes=[BK(bg_), ("sgF", s2)])
                op("dve", lambda h: h.scalar_tensor_tensor(
                    out=hbuf[:, oc, 30:30 + N], in0=banks[bv_][:, :], scalar=V("b_pw1", oc, 1), in1=sgF[s2][:],
                    op0=ALU.add, op1=ALU.mult), reads=["vec", ("sgF", s2)], writes=[BK(bv_), ("hbuf", oc)])

        def f1_conv(tt_):
            cv = cvs[tt_ % 2]
            for oc in range(8):
                b_ = nb()
                for k in range(31):
                    mm(b_, banks[b_][:, :], dg[:, oc, k, :], hbuf[:, oc, k:k + N], k == 0, k == 30,
                       [("dg", oc), ("hbuf", oc)])
                op("act", lambda h: h.activation(out=cv[:, oc, :], in_=banks[b_][:, :], func=AF.Identity,
                                                 bias=V("b_dw", oc, 1), scale=1.0),
                   reads=["vec"], writes=[BK(b_), ("cv", tt_ % 2)])

        def f1_ln(tt_):
            cv = cvs[tt_ % 2]
            ck = ("cv", tt_ % 2)
            layer_norm(cv, N, ck, cv, ck, lnb)
            for oc in range(8):
                op("act", lambda h, oc=oc: h.activation(out=h2[:, oc, :], in_=cv[:, oc, :], func=AF.Silu,
                                                        bias=V("cln_b", oc, 1), scale=V("cln_g", oc, 1)),
                   reads=[ck, "vec"], writes=["h2"])
            c.dma("pool", H2_v[:, :, tt_ * N:(tt_ + 1) * N], h2[:], reads=["h2"], writes=[("H2_d", tt_)], key="k_h2")

        f1_load(0)
        for tt_ in range(NTL):
            if tt_ + 1 < NTL:
                f1_load(tt_ + 1)
            f1_glu(tt_)
            if tt_ > 0:
                f1_ln(tt_ - 1)
            f1_conv(tt_)
        f1_ln(NTL - 1)
    with c.scope():
        pw2 = c.sb([128, 8, 1024], BF16, "pw2")
        load_w(pw2, pw2_d, 8, 1024, "k_pw2", "pw2")
        lnb = ln_scratch(N)
        h2i = [c.sb([128, 8, N], BF16, f"h2i{i}") for i in range(2)]
        sin2 = [c.sb([128, 8, N], F32, f"sinF{i}") for i in range(2)]
        zts = [c.sb([128, 8, N], F32, f"ztF{i}") for i in range(2)]
        xbt = c.sb([128, 8, N], BF16, "xbtF")

        def f2_load(tt_):
            sl = tt_ % 2
            tok0 = tt_ * N
            c.dma("sp", h2i[sl][:], H2_v[:, :, tok0:tok0 + N], writes=[("h2i", sl)], key=f"k_h2i{sl}")
            c.dma("sp", sin2[sl][:], S_v[:, :, tok0:tok0 + N], writes=[("sinF", sl)], key=f"k_sinF{sl}")

        def f2_mm(tt_):
            sl = tt_ % 2
            z = zts[sl]
            for oc in range(8):
                b_ = nb()
                for kc in range(8):
                    mm(b_, banks[b_][:, :], pw2[:, kc, oc * 128:(oc + 1) * 128], h2i[sl][:, kc, :], kc == 0, kc == 7,
                       wreads("pw2", kc, 1024) + [("h2i", sl)])
                op("dve", lambda h: h.tensor_tensor(out=z[:, oc, :], in0=banks[b_][:, :], in1=sin2[sl][:, oc, :],
                                                    op=ALU.add), reads=[("sinF", sl)], writes=[BK(b_), ("ztF", sl)])

        def f2_ln(tt_):
            sl = tt_ % 2
            z = zts[sl]
            zk = ("ztF", sl)
            layer_norm(z, N, zk, z, zk, lnb)
            ln_epilogue_stream(z, zk, N, tt_ * N, ga11, ba11, "ln1_g1", "ln1_b1", z, xbt, 2, sstkey=zk)

        f2_load(0)
        for tt_ in range(NTL):
            if tt_ + 1 < NTL:
                f2_load(tt_ + 1)
            f2_mm(tt_)
            if tt_ > 0:
                f2_ln(tt_ - 1)
        f2_ln(NTL - 1)

    if dbg and dbg.get("stop") == "F":
        c.barrier()
        c.dma("sp", dC_d, H2_d, writes=["dC0"], key="k_dbg")
    if dbg_stop("F"):
        return nc
    ffn_phase(1, True, None, None, "ln2_g1", "ln2_b1")
    if dbg:
        dbg_stop(dbg.get("stop"))
        return nc

    c.barrier()
    c.emit()
    c.close()
    return nc


def _col(v):
    v = np.asarray(v, np.float32).reshape(-1, 128)
    return np.ascontiguousarray(v.T)


def _host_layout(inp):
    f = lambda a: np.ascontiguousarray(np.asarray(a, np.float32))
    vec = np.zeros((128, NV), np.float32)

    def put(name, arr):
        o, w = VOFF[name]
        vec[:, o:o + w] = arr

    for l in range(2):
        put(f"ln1_g{l}", _col(inp["ln1_g"][l])); put(f"ln1_b{l}", _col(inp["ln1_b"][l]))
        put(f"ln2_g{l}", _col(inp["ln2_g"][l])); put(f"ln2_b{l}", _col(inp["ln2_b"][l]))
        put(f"fb1_{l}", _col(inp["ffn_b1"][l])); put(f"fb2_{l}", _col(inp["ffn_b2"][l]))
    put("b_in", _col(inp["mix_b_in"][0][:1536]))
    put("s5_d", _col(inp["s5_d"][0])); put("b_glu", _col(inp["s5_b_glu"][0])); put("b_out", _col(inp["mix_b_out"][0]))
    put("b_pw1", _col(inp["conv_b_pw1"][0])); put("b_dw", _col(inp["conv_b_dw"][0]))
    put("cln_g", _col(inp["conv_ln_g"][0])); put("cln_b", _col(inp["conv_ln_b"][0]))
    put("b_pw2", _col(inp["conv_b_pw2"][0]))
    wd = np.asarray(inp["conv_w_dw"][0], np.float32)
    put("w_dw", np.concatenate([_col(wd[k]) for k in range(31)], axis=1))
    bv = np.ascontiguousarray(np.broadcast_to(np.asarray(inp["mix_b_in"][0][1536:], np.float32)[None, :], (128, 512)))
    cst = np.zeros((128, 640), np.float32)
    cst[:, 0:128] = np.eye(128)
    j = np.arange(128)[:, None]; s = np.arange(128)[None, :]
    cst[:, 128:256] = (j >= s)
    cst[:, 256:384] = np.where(j >= s, -30000.0, 0.0)
    cst[:, 384:512] = np.where(j >= s, 30000.0, 0.0)
    cst[:, 512:640] = 1.0 / 1024.0
    lr = np.asarray(inp["s5_lambda_re"][0], np.float32); li = np.asarray(inp["s5_lambda_im"][0], np.float32)
    ld = np.asarray(inp["s5_log_dt"][0], np.float32)
    lam = np.zeros((128, 3, 16), np.float32)
    for k in range(16):
        for g2 in range(2):
            g = 2 * k + g2
            lam[g2 * 64:(g2 + 1) * 64, 0, k] = lr[g]
            lam[g2 * 64:(g2 + 1) * 64, 1, k] = li[g]
            lam[g2 * 64:(g2 + 1) * 64, 2, k] = ld[g]

    def pad_layout(arr_gnp):
        out = np.zeros((128, 16, 128), np.float32)
        for k in range(16):
            for g2 in range(2):
                g = 2 * k + g2
                c0 = 16 * (g % 8)
                out[g2 * 64:(g2 + 1) * 64, k, c0:c0 + 16] = arr_gnp[g]
        return out

    bre = np.asarray(inp["s5_b_re"][0], np.float32); bim = np.asarray(inp["s5_b_im"][0], np.float32)
    cre = np.asarray(inp["s5_c_re"][0], np.float32).transpose(0, 2, 1)
    cim = np.asarray(inp["s5_c_im"][0], np.float32).transpose(0, 2, 1)
    shared = {
        "w_in": f(inp["mix_w_in"][0]), "w_glu": f(inp["s5_w_glu"][0]), "w_out": f(inp["mix_w_out"][0]),
        "pw1": f(inp["conv_w_pw1"][0]), "pw2": f(inp["conv_w_pw2"][0]),
        "w1_0": f(inp["ffn_w1"][0]), "w1_1": f(inp["ffn_w1"][1]),
        "w2_0": f(inp["ffn_w2"][0]), "w2_1": f(inp["ffn_w2"][1]),
        "vecs": vec, "bv_bc": bv, "consts": cst, "lamll": lam,
        "bt_re": pad_layout(bre), "bt_im": pad_layout(bim), "cp_re": pad_layout(cre), "cp_im": pad_layout(cim),
    }
    return shared


def kernel(**inputs):
    x = np.asarray(inputs["x"], np.float32)
    shared = _host_layout(inputs)
    nc = build()
    in_maps = []
    for i in range(8):
        m = dict(shared)
        m["x"] = np.ascontiguousarray(x[2 * i:2 * i + 2].reshape(T, D))
        in_maps.append(m)
    res = run_bass_kernel_spmd(nc, in_maps, core_ids=list(range(8)))
    out = np.concatenate([r["y"].reshape(2, L, D) for r in res.results], axis=0)
    return out.astype(np.float32)
```

```python
import contextlib
import numpy as np
import concourse.bass as bass
import concourse.mybir as mybir
from concourse.bass_utils import run_bass_kernel_spmd

F32 = mybir.dt.float32
BF16 = mybir.dt.bfloat16
I32 = mybir.dt.int32
AF = mybir.ActivationFunctionType
ALU = mybir.AluOpType

D = 1024
L = 2048
NSEQ = 2
T = NSEQ * L
ALPHA = 4.0 ** 0.25
EPS = 1e-5
PI = float(np.pi)
PAD = 1024


class _Rec:
    def __init__(self):
        self.call = None

    def __getattr__(self, name):
        def f(*a, **kw):
            self.call = (name, a, kw)
            return self
        return f


class Ctx:
    def __init__(self, nc):
        self.nc = nc
        self.es = contextlib.ExitStack()
        self.stacks = [self.es]
        self.engs = ["pe", "act", "dve", "pool", "sp"]
        self.sem = {}
        self.cnt = {}
        for k in self.engs:
            self.sem[k] = self.es.enter_context(nc.semaphore("s_" + k))
            self.cnt[k] = 0
        self.waited = {k: {} for k in self.engs}
        self.lastw = {}
        self.reads = {}
        self.dsem = {}
        self.dcnt = {}
        self.nsb = 0
        self.prog = {k: [] for k in self.engs}

    def sb(self, shape, dt, name=None):
        self.nsb += 1
        return self.stacks[-1].enter_context(
            self.nc.sbuf_tensor(f"{name or 'sb'}_{self.nsb}", list(shape), dt))

    def ps(self, shape, dt, name=None):
        self.nsb += 1
        return self.stacks[-1].enter_context(
            self.nc.psum_tensor(f"{name or 'ps'}_{self.nsb}", list(shape), dt))

    @contextlib.contextmanager
    def scope(self):
        st = contextlib.ExitStack()
        self.stacks.append(st)
        try:
            yield
        finally:
            self.barrier()
            self.stacks.pop()
            st.close()

    def _semobj(self, key):
        return self.sem[key] if key in self.sem else self.dsem[key]

    def _deps(self, reads, writes):
        deps = []
        for r in reads:
            if r in self.lastw:
                deps.append(self.lastw[r])
        for w in writes:
            if w in self.lastw:
                deps.append(self.lastw[w])
            deps.extend(self.reads.get(w, []))
        return deps

    def _wait(self, e, deps):
        best = {}
        for (k, v) in deps:
            if e == "pe" and k == "pe":
                continue
            if v > best.get(k, 0):
                best[k] = v
        for k, v in best.items():
            if self.waited[e].get(k, 0) >= v:
                continue
            so = self._semobj(k)
            self.prog[e].append(lambda h, so=so, v=v: h.wait_ge(so, v))
            self.waited[e][k] = v

    def _record(self, ticket, reads, writes):
        for r in reads:
            self.reads.setdefault(r, []).append(ticket)
        for w in writes:
            self.lastw[w] = ticket
            self.reads[w] = []

    def op(self, e, fn, reads=(), writes=()):
        self._wait(e, self._deps(reads, writes))
        self.cnt[e] += 1
        so = self.sem[e]
        rec = _Rec()
        fn(rec)
        name, a, kw = rec.call
        self.prog[e].append(lambda h, name=name, a=a, kw=kw, so=so: getattr(h, name)(*a, **kw).then_inc(so, 1))
        t = (e, self.cnt[e])
        self._record(t, reads, writes)
        return t

    def dma(self, q, out, in_, reads=(), writes=(), key=None, **kw):
        if key not in self.dsem:
            self.dsem[key] = self.es.enter_context(self.nc.semaphore(f"d{len(self.dsem)}"))
            self.dcnt[key] = 0
        self._wait(q, self._deps(reads, writes))
        so = self.dsem[key]
        self.prog[q].append(lambda h, out=out, in_=in_, kw=kw, so=so:
                            h.dma_start(out=out, in_=in_, **kw).then_inc(so, 16))
        self.dcnt[key] += 16
        t = (key, self.dcnt[key])
        self._record(t, reads, writes)
        return t

    def barrier(self):
        deps = [(k, self.cnt[k]) for k in self.engs if self.cnt[k] > 0]
        deps += [(k, v) for k, v in self.dcnt.items() if v > 0]
        for e in self.engs:
            self._wait(e, deps)
        self.lastw = {}
        self.reads = {}

    def emit(self):
        with self.nc.Block() as block:
            def mk(e):
                def body(h):
                    for f in self.prog[e]:
                        f(h)
                return body
            block.tensor(mk("pe"))
            block.scalar(mk("act"))
            block.vector(mk("dve"))
            block.gpsimd(mk("pool"))
            block.sync(mk("sp"))

    def close(self):
        self.es.close()


VEC_SPEC = [("ln1_g0", 8), ("ln1_b0", 8), ("ln2_g0", 8), ("ln2_b0", 8),
            ("ln1_g1", 8), ("ln1_b1", 8), ("ln2_g1", 8), ("ln2_b1", 8),
            ("fb1_0", 32), ("fb1_1", 32), ("fb2_0", 8), ("fb2_1", 8),
            ("b_in", 12), ("s5_d", 4), ("b_glu", 8), ("b_out", 8),
            ("b_pw1", 16), ("b_dw", 8), ("cln_g", 8), ("cln_b", 8), ("b_pw2", 8),
            ("w_dw", 31 * 8)]
VOFF = {}
_o = 0
for _n, _w in VEC_SPEC:
    VOFF[_n] = (_o, _w)
    _o += _w
NV = _o


def build(dbg=None):
    nc = bass.Bass("TRN2", target_bir_lowering=False)

    def din(name, shape, dt=F32):
        return nc.dram_tensor(name, list(shape), dt, kind="ExternalInput").ap()

    x_d = din("x", [T, D])
    w_in_d = din("w_in", [D, 2048])
    w_glu_d = din("w_glu", [512, 1024])
    w_out_d = din("w_out", [D, D])
    pw1_d = din("pw1", [D, 2048])
    pw2_d = din("pw2", [D, D])
    w1_d = [din("w1_0", [D, 4096]), din("w1_1", [D, 4096])]
    w2_d = [din("w2_0", [4096, D]), din("w2_1", [4096, D])]
    vec_d = din("vecs", [128, NV])
    bv_d = din("bv_bc", [128, 512])
    cst_d = din("consts", [128, 640])
    lam_d = din("lamll", [128, 3, 16])
    bt_d = [din("bt_re", [128, 16, 128]), din("bt_im", [128, 16, 128])]
    cp_d = [din("cp_re", [128, 16, 128]), din("cp_im", [128, 16, 128])]
    y_d = nc.dram_tensor("y", [T, D], F32, kind="ExternalOutput").ap()
    S_d = nc.dram_tensor("S_scr", [D, T], F32, kind="Internal").ap()
    XB_d = nc.dram_tensor("XB_scr", [D, T], BF16, kind="Internal").ap()
    if dbg:
        dS_d = nc.dram_tensor("dbgS", [D, T], F32, kind="ExternalOutput").ap()
        dX_d = nc.dram_tensor("dbgX", [D, T], BF16, kind="ExternalOutput").ap()
        dC_d = nc.dram_tensor("dbgC", [D, T], BF16, kind="ExternalOutput").ap()
        dC_v = dC_d.rearrange("(c p) t -> p c t", p=128)
    HG_d = nc.dram_tensor("HG_scr", [16, 2, 128, 2048], BF16, kind="Internal").ap()
    S_v = S_d.rearrange("(c p) t -> p c t", p=128)
    XB_v = XB_d.rearrange("(c p) t -> p c t", p=128)

    c = Ctx(nc)
    op = c.op

    vec = c.sb([128, NV], F32, "vec")
    cst = c.sb([128, 640], F32, "cst")
    cstb = c.sb([128, 640], BF16, "cstb")
    der = c.sb([128, 160], F32, "der")
    banks = [c.ps([128, 512], F32, f"bank{i}") for i in range(8)]
    ident = cst[:, 0:128]
    identb = cstb[:, 0:128]
    trib = cstb[:, 128:256]
    nmaskb = cstb[:, 256:384]
    pmaskb = cstb[:, 384:512]
    onesb = cstb[:, 512:640]

    def V(name, i=0, n=1):
        o, w = VOFF[name]
        return vec[:, o + i:o + i + n]

    c.dma("sp", vec[:], vec_d, writes=["vec"], key="k_vec")
    c.dma("sp", cst[:], cst_d, writes=["cst"], key="k_cst")
    op("dve", lambda h: h.tensor_copy(out=cstb[:], in_=cst[:]), reads=["cst"], writes=["cstb"])

    DER = {}
    _do = [0]

    def dalloc(name, n):
        DER[name] = _do[0]
        _do[0] += n
        return der[:, DER[name]:DER[name] + n]

    bq8 = dalloc("bq8", 4)
    op("dve", lambda h: h.tensor_scalar(out=bq8, in0=V("b_in", 4, 4), scalar1=0.125, scalar2=None,
                                        op0=ALU.mult), reads=["vec"], writes=["der"])

    def ln_consts(tag, gname, bname, nextb):
        ga = dalloc("ga" + tag, 8)
        ba = dalloc("ba" + tag, 8)
        op("dve", lambda h: h.tensor_scalar(out=ga, in0=V(gname, 0, 8), scalar1=ALPHA, scalar2=None,
                                            op0=ALU.mult), reads=["vec"], writes=["der"])
        op("dve", lambda h: h.scalar_tensor_tensor(out=ba, in0=V(bname, 0, 8), scalar=ALPHA,
                                                   in1=V(nextb, 0, 8), op0=ALU.mult, op1=ALU.add),
           reads=["vec"], writes=["der"])
        return ga, ba

    ga10, ba10 = ln_consts("10", "ln1_g0", "ln1_b0", "fb2_0")
    ga20, ba20 = ln_consts("20", "ln2_g0", "ln2_b0", "b_pw2")
    ga11, ba11 = ln_consts("11", "ln1_g1", "ln1_b1", "fb2_1")

    bkrr = [0]

    def nb(pool=(0, 1, 2, 3, 4, 5, 6, 7)):
        bkrr[0] += 1
        return pool[bkrr[0] % len(pool)]

    def BK(i):
        return ("bk", i)

    def mm(bank, out_ap, lhsT, rhs, start, stop, reads, **kw):
        op("pe", lambda h: h.matmul(out_ap, lhsT=lhsT, rhs=rhs, start=start, stop=stop, **kw),
           reads=reads, writes=[BK(bank)])

    def load_w(dst, src_d, kc_n, ncols, key, rkey, cs0=0, c0b=0):
        sv = src_d.rearrange("(c p) f -> p c f", p=128)
        step = 2048
        for kc in range(kc_n):
            for c0 in range(0, ncols, step):
                c1 = min(ncols, c0 + step)
                c.dma("pool", dst[:, kc, c0b + c0:c0b + c1], sv[:, kc, cs0 + c0:cs0 + c1],
                      writes=[(rkey, kc, c0)], key=key)
        fin = (key, c.dcnt[key])
        for kc in range(kc_n):
            for c0 in range(0, ncols, step):
                c.lastw[(rkey, kc, c0)] = fin

    def wreads(rkey, kc, ncols):
        return [(rkey, kc, c0) for c0 in range(0, ncols, 2048)]

    def layer_norm(zt, N, zkey, xh, xhkey, lnb):
        zb, zq, msq, var, rstd, nmr = lnb["zb"], lnb["zq"], lnb["msq"], lnb["var"], lnb["rstd"], lnb["nmr"]
        op("act", lambda h: h.activation(out=zb[:, :, 0:N], in_=zt[:, :, 0:N], func=AF.Copy),
           reads=[zkey], writes=["ln_zb"])
        op("act", lambda h: h.activation(out=zq[:, :, 0:N], in_=zt[:, :, 0:N], func=AF.Square),
           reads=[zkey], writes=["ln_zq"])
        bm, bq = nb(), nb()
        for kc in range(8):
            mm(bm, banks[bm][:, 0:N], onesb, zb[:, kc, 0:N], kc == 0, kc == 7, ["ln_zb", "cstb"])
        for kc in range(8):
            mm(bq, banks[bq][:, 0:N], onesb, zq[:, kc, 0:N], kc == 0, kc == 7, ["ln_zq", "cstb"])
        op("act", lambda h: h.activation(out=msq[:, 0:N], in_=banks[bm][:, 0:N], func=AF.Square),
           writes=[BK(bm), "ln_msq"])
        op("dve", lambda h: h.tensor_tensor(out=var[:, 0:N], in0=banks[bq][:, 0:N], in1=msq[:, 0:N],
                                            op=ALU.subtract), reads=["ln_msq"], writes=[BK(bq), "ln_var"])
        op("dve", lambda h: h.tensor_scalar(out=var[:, 0:N], in0=var[:, 0:N], scalar1=EPS, scalar2=None,
                                            op0=ALU.add), reads=["ln_var"], writes=["ln_var"])
        op("act", lambda h: h.activation(out=var[:, 0:N], in_=var[:, 0:N], func=AF.Sqrt),
           reads=["ln_var"], writes=["ln_var"])
        op("dve", lambda h: h.reciprocal(out=banks[bq][:, 0:N], in_=var[:, 0:N]), reads=["ln_var"],
           writes=[BK(bq)])
        mb = banks[bm][:, 0:N].unsqueeze(1).broadcast_to([128, 8, N])
        rb = banks[bq][:, 0:N].unsqueeze(1).broadcast_to([128, 8, N])
        op("dve", lambda h: h.tensor_tensor(out=xh[:, :, 0:N], in0=zt[:, :, 0:N], in1=mb, op=ALU.subtract),
           reads=[zkey], writes=[BK(bm), xhkey])
        op("dve", lambda h: h.tensor_tensor(out=xh[:, :, 0:N], in0=xh[:, :, 0:N], in1=rb, op=ALU.mult),
           reads=[], writes=[BK(bq), xhkey])

    def ln_scratch(N):
        return {"zb": c.sb([128, 8, N], BF16, "zb"), "zq": c.sb([128, 8, N], BF16, "zq"),
                "msq": c.sb([128, N], F32, "msq"), "var": c.sb([128, N], F32, "var"),
                "rstd": c.sb([128, N], F32, "rstd"), "nmr": c.sb([128, N], F32, "nmr")}

    def ln_epilogue_stream(xh, xhkey, N, tok0, ga, ba, gname, bname, sst, xbt, slot, sstkey=None):
        sk = sstkey or ("sst", slot)
        for kc in range(8):
            op("act", lambda h, kc=kc: h.activation(out=xbt[:, kc, 0:N], in_=xh[:, kc, 0:N], func=AF.Identity,
                                                    bias=V(bname, kc, 1), scale=V(gname, kc, 1)),
               reads=[xhkey, "vec"], writes=[("xbt", slot)])
        for kc in range(8):
            op("act", lambda h, kc=kc: h.activation(out=sst[:, kc, 0:N], in_=xh[:, kc, 0:N], func=AF.Identity,
                                                    bias=ba[:, kc:kc + 1], scale=ga[:, kc:kc + 1]),
               reads=[xhkey, "der"], writes=[sk])
        c.dma("pool", S_v[:, :, tok0:tok0 + N], sst[:, :, 0:N], reads=[sk],
              writes=[("S_d", tok0)], key=f"k_sst{slot}")
        c.dma("pool", XB_v[:, :, tok0:tok0 + N], xbt[:, :, 0:N], reads=[("xbt", slot)],
              writes=[("XB_d", tok0)], key=f"k_xbt{slot}")

    with c.scope():
        bv = c.sb([128, 512], F32, "bv")
        c.dma("sp", bv[:], bv_d, writes=["bv"], key="k_bv")

        lam = c.sb([128, 3, 16], F32, "lam")
        c.dma("sp", lam[:], lam_d, writes=["lam"], key="k_lam")
        tb = c.sb([128, 24, 16], F32, "s5tmp")
        tbi = c.sb([128, 16], I32, "s5tmpi")
        AR = c.sb([128, 11, 16], F32, "AR")
        AI = c.sb([128, 11, 16], F32, "AI")
        NAI = c.sb([128, 11, 16], F32, "NAI")
        TK = "s5t"

        def tt(out, a, b, o, e="dve"):
            op(e, lambda h: h.tensor_tensor(out=out, in0=a, in1=b, op=o), reads=[TK, "lam"], writes=[TK])

        def ts(out, a, s1, o1, s2=None, o2=None):
            if o2 is None:
                op("dve", lambda h: h.tensor_scalar(out=out, in0=a, scalar1=s1, scalar2=None, op0=o1),
                   reads=[TK, "lam"], writes=[TK])
            else:
                op("dve", lambda h: h.tensor_scalar(out=out, in0=a, scalar1=s1, scalar2=s2, op0=o1, op1=o2),
                   reads=[TK, "lam"], writes=[TK])

        def act(out, a, f, **kw):
            op("act", lambda h: h.activation(out=out, in_=a, func=f, **kw), reads=[TK, "lam"], writes=[TK])

        dt_, lr_, a_, th_, mag_ = tb[:, 0, :], tb[:, 1, :], tb[:, 2, :], tb[:, 3, :], tb[:, 4, :]
        act(dt_, lam[:, 2, :], AF.Exp)
        ts(lr_, lam[:, 0, :], -1e-4, ALU.min)
        tt(a_, lr_, dt_, ALU.mult)
        tt(th_, lam[:, 1, :], dt_, ALU.mult)
        act(mag_, a_, AF.Exp)

        def sin_of(out, src, shift):
            u, kf, r, g = tb[:, 5, :], tb[:, 6, :], tb[:, 7, :], tb[:, 8, :]
            ts(u, src, shift, ALU.add, 1.0 / (2 * PI), ALU.mult)
            op("dve", lambda h: h.tensor_copy(out=tbi[:], in_=u), reads=[TK], writes=[TK])
            op("dve", lambda h: h.tensor_copy(out=kf, in_=tbi[:]), reads=[TK], writes=[TK])
            ts(kf, kf, -2 * PI, ALU.mult, shift, ALU.add)
            tt(r, src, kf, ALU.add)
            ts(g, r, PI, ALU.is_gt, -2 * PI, ALU.mult)
            tt(r, r, g, ALU.add)
            ts(g, r, -PI, ALU.is_lt, 2 * PI, ALU.mult)
            tt(r, r, g, ALU.add)
            act(out, r, AF.Sin)

        sn_, cs_ = tb[:, 9, :], tb[:, 10, :]
        sin_of(sn_, th_, 0.0)
        sin_of(cs_, th_, PI / 2)
        tt(AR[:, 0, :], mag_, cs_, ALU.mult)
        tt(AI[:, 0, :], mag_, sn_, ALU.mult)
        for s in range(10):
            t1, t2 = tb[:, 11, :], tb[:, 12, :]
            tt(t1, AR[:, s, :], AR[:, s, :], ALU.mult)
            tt(t2, AI[:, s, :], AI[:, s, :], ALU.mult)
            tt(AR[:, s + 1, :], t1, t2, ALU.subtract)
            tt(t1, AR[:, s, :], AI[:, s, :], ALU.mult)
            ts(AI[:, s + 1, :], t1, 2.0, ALU.mult)
        ts(NAI[:], AI[:], -1.0, ALU.mult)
        den, nr, Fr, Fi = tb[:, 13, :], tb[:, 14, :], tb[:, 15, :], tb[:, 16, :]
        t1, t2 = tb[:, 11, :], tb[:, 12, :]
        tt(t1, lr_, lr_, ALU.mult)
        tt(t2, lam[:, 1, :], lam[:, 1, :], ALU.mult)
        tt(den, t1, t2, ALU.add)
        op("dve", lambda h: h.reciprocal(out=den, in_=den), reads=[TK], writes=[TK])
        ts(nr, AR[:, 0, :], -1.0, ALU.add)
        tt(t1, nr, lr_, ALU.mult)
        tt(t2, AI[:, 0, :], lam[:, 1, :], ALU.mult)
        tt(t1, t1, t2, ALU.add)
        tt(Fr, t1, den, ALU.mult)
        tt(t1, AI[:, 0, :], lr_, ALU.mult)
        tt(t2, nr, lam[:, 1, :], ALU.mult)
        tt(t1, t1, t2, ALU.subtract)
        tt(Fi, t1, den, ALU.mult)

        bb = [c.sb([128, 16, 128], F32, "bb_re"), c.sb([128, 16, 128], F32, "bb_im")]
        cp = [c.sb([128, 16, 128], F32, "cp_re"), c.sb([128, 16, 128], F32, "cp_im")]
        PR = c.sb([128, 9, 16], F32, "PR")
        PI_ = c.sb([128, 9, 16], F32, "PI")
        A8R = c.sb([128, 8, 16], F32, "A8R")
        A8I = c.sb([128, 8, 16], F32, "A8I")
        NA8I = c.sb([128, 8, 16], F32, "NA8I")
        Kblk = c.sb([128, 4, 8, 128], BF16, "Kblk")
        for i in range(2):
            c.dma("sp", cp[i][:], cp_d[i], writes=[("cp", i)], key=f"k_cp{i}")
        op("dve", lambda h: h.memset(PR[:, 0, :], 1.0), reads=[TK], writes=[TK])
        op("dve", lambda h: h.memset(PI_[:, 0, :], 0.0), reads=[TK], writes=[TK])
        ts(PR[:, 1, :], AR[:, 0, :], 1.0, ALU.mult)
        ts(PI_[:, 1, :], AI[:, 0, :], 1.0, ALU.mult)
        for j in range(1, 8):
            t1, t2 = tb[:, 11, :], tb[:, 12, :]
            tt(t1, PR[:, j, :], AR[:, 0, :], ALU.mult)
            tt(t2, PI_[:, j, :], AI[:, 0, :], ALU.mult)
            tt(PR[:, j + 1, :], t1, t2, ALU.subtract)
            tt(t1, PR[:, j, :], AI[:, 0, :], ALU.mult)
            tt(t2, PI_[:, j, :], AR[:, 0, :], ALU.mult)
            tt(PI_[:, j + 1, :], t1, t2, ALU.add)
        ts(A8R[:, 0, :], AR[:, 3, :], 1.0, ALU.mult)
        ts(A8I[:, 0, :], AI[:, 3, :], 1.0, ALU.mult)
        for s_ in range(1, 8):
            ts(A8R[:, s_, :], AR[:, 3 + s_, :], 1.0, ALU.mult)
            ts(A8I[:, s_, :], AI[:, 3 + s_, :], 1.0, ALU.mult)
        ts(NA8I[:], A8I[:], -1.0, ALU.mult)
        NPI = c.sb([128, 9, 16], F32, "NPI")
        ts(NPI[:], PI_[:], -1.0, ALU.mult)
        with c.scope():
            bt = [c.sb([128, 16, 128], F32, "bt_re"), c.sb([128, 16, 128], F32, "bt_im")]
            w1t = c.sb([128, 16, 128], F32, "w1t")
            w2t = c.sb([128, 16, 128], F32, "w2t")
            sre = c.sb([128, 16, 128], F32, "sre")
            sim = c.sb([128, 16, 128], F32, "sim")
            for i in range(2):
                c.dma("sp", bt[i][:], bt_d[i], writes=[("bt", i)], key=f"k_bt{i}")
            Frb = Fr.unsqueeze(2).broadcast_to([128, 16, 128])
            Fib = Fi.unsqueeze(2).broadcast_to([128, 16, 128])

            def t3(out, a, b, o, rk, wk, e="dve"):
                op(e, lambda h: h.tensor_tensor(out=out, in0=a, in1=b, op=o), reads=rk + [TK], writes=wk)

            t3(w1t[:], bt[0][:], Frb, ALU.mult, [("bt", 0)], ["w1t"])
            t3(w2t[:], bt[1][:], Fib, ALU.mult, [("bt", 1)], ["w2t"])
            t3(bb[0][:], w1t[:], w2t[:], ALU.subtract, ["w1t", "w2t"], [("bb", 0)])
            t3(w1t[:], bt[1][:], Frb, ALU.mult, [("bt", 1)], ["w1t"])
            t3(w2t[:], bt[0][:], Fib, ALU.mult, [("bt", 0)], ["w2t"])
            t3(bb[1][:], w1t[:], w2t[:], ALU.add, ["w1t", "w2t"], [("bb", 1)])
            for tau in range(8):
                prb = PR[:, tau, :].unsqueeze(2).broadcast_to([128, 16, 128])
                pib = PI_[:, tau, :].unsqueeze(2).broadcast_to([128, 16, 128])
                t3(w1t[:], bb[0][:], prb, ALU.mult, [("bb", 0)], ["w1t"])
                t3(w2t[:], bb[1][:], pib, ALU.mult, [("bb", 1)], ["w2t"])
                t3(sre[:], w1t[:], w2t[:], ALU.subtract, ["w1t", "w2t"], ["sre"])
                t3(w1t[:], bb[0][:], pib, ALU.mult, [("bb", 0)], ["w1t"])
                t3(w2t[:], bb[1][:], prb, ALU.mult, [("bb", 1)], ["w2t"])
                t3(sim[:], w1t[:], w2t[:], ALU.add, ["w1t", "w2t"], ["sim"])
                op("dve", lambda h: h.tensor_scalar(out=sim[:], in0=sim[:], scalar1=-1.0, scalar2=None, op0=ALU.mult),
                   reads=["sim"], writes=["sim"])
                for q in range(4):
                    b_ = nb()
                    n_ = 0
                    for kk in range(4):
                        for (a_, c_, ak) in ((sre, cp[0], "sre"), (sim, cp[1], "sim")):
                            mm(b_, banks[b_][:, 0:128], a_[:, 4 * q + kk, :], c_[:, 4 * q + kk, :], n_ == 0, n_ == 7,
                               [ak, ("cp", 0), ("cp", 1)])
                            n_ += 1
                    op("act", lambda h, q=q, tau=tau, b_=b_: h.activation(out=Kblk[:, q, tau, :],
                                                                          in_=banks[b_][:, 0:128], func=AF.Copy),
                       writes=[BK(b_), "Kblk"])


        for sq in range(NSEQ):
            with c.scope():
                T0 = sq * L
                catA = c.sb([128, 4, L], BF16, "catA")

                def phaseA(part, outs):
                    with c.scope():
                        ncol = 512 if part == 0 else 1536
                        w_in = c.sb([128, 8, ncol], BF16, "w_in")
                        load_w(w_in, w_in_d, 8, ncol, f"k_w_in{part}", "w_in", cs0=(0 if part == 0 else 512))
                        XBt = [c.sb([128, 8, 512], BF16, "XBt0"), c.sb([128, 8, 512], BF16, "XBt1")]
                        xt = [c.sb([128, D], F32, f"xt{i}") for i in range(4)]
                        if part == 0:
                            sst = [c.sb([128, 8, 128], F32, f"sstA{i}") for i in range(4)]
                        for tt_ in range(4):
                            XB = XBt[tt_ % 2]
                            xk = ("XB", tt_ % 2)
                            for bl in range(4):
                                tbk = tt_ * 4 + bl
                                sl = tbk % 4
                                tok = T0 + tbk * 128
                                c.dma("sp", xt[sl][:], x_d[tok:tok + 128, :], writes=[("xt", sl)], key=f"k_xt{sl}")
                                b0, b1 = nb(), nb()
                                for fc in range(8):
                                    bq_ = b0 if fc < 4 else b1
                                    op("pe", lambda h, fc=fc, bq_=bq_, sl=sl: h.transpose(
                                        out=banks[bq_][:, (fc % 4) * 128:(fc % 4) * 128 + 128],
                                        in_=xt[sl][:, fc * 128:(fc + 1) * 128], identity=ident),
                                        reads=[("xt", sl), "cst"], writes=[BK(bq_)])
                                for fc in range(8):
                                    bq_ = b0 if fc < 4 else b1
                                    src = banks[bq_][:, (fc % 4) * 128:(fc % 4) * 128 + 128]
                                    if part == 0:
                                        op("act", lambda h, fc=fc, src=src, sl=sl: h.activation(
                                            out=sst[sl][:, fc, :], in_=src, func=AF.Identity,
                                            bias=V("b_out", fc, 1), scale=ALPHA),
                                            reads=["vec"], writes=[BK(bq_), ("sstA", sl)])
                                    op("dve", lambda h, fc=fc, src=src, bl=bl: h.tensor_copy(
                                        out=XB[:, fc, bl * 128:(bl + 1) * 128], in_=src),
                                        writes=[BK(bq_), xk])
                                if part == 0:
                                    c.dma("pool", S_v[:, :, tok:tok + 128], sst[sl][:], reads=[("sstA", sl)],
                                          writes=[("S_d", tok)], key=f"k_sstA{sl}")
                            cs = slice(tt_ * 512, (tt_ + 1) * 512)
                            for oc in range(4 if part == 0 else 8):
                                b_ = nb()
                                for kc in range(8):
                                    mm(b_, banks[b_][:, :], w_in[:, kc, oc * 128:(oc + 1) * 128], XB[:, kc, :],
                                       kc == 0, kc == 7, wreads("w_in", kc, ncol) + [xk])
                                if part == 0:
                                    u_f = outs[0]
                                    op("act", lambda h, oc=oc, b_=b_, cs=cs: h.activation(
                                        out=u_f[:, oc, cs], in_=banks[b_][:, :], func=AF.Identity,
                                        bias=V("b_in", oc, 1), scale=1.0), reads=["vec"],
                                        writes=[BK(b_), ("u_f", tt_)])
                                elif oc < 4:
                                    qT = outs[0]
                                    op("act", lambda h, oc=oc, b_=b_, cs=cs: h.activation(
                                        out=qT[:, oc, cs], in_=banks[b_][:, :], func=AF.Identity,
                                        bias=bq8[:, oc:oc + 1], scale=0.125), reads=["der"],
                                        writes=[BK(b_), ("qT", tt_)])
                                else:
                                    kT, nkT = outs[1], outs[2]
                                    op("act", lambda h, oc=oc, b_=b_, cs=cs: h.activation(
                                        out=kT[:, oc - 4, cs], in_=banks[b_][:, :], func=AF.Identity,
                                        bias=V("b_in", 4 + oc, 1), scale=1.0), reads=["vec"],
                                        writes=[BK(b_), ("kT", tt_)])
                                    op("dve", lambda h, oc=oc, cs=cs: h.tensor_scalar(
                                        out=nkT[:, oc - 4, cs], in0=kT[:, oc - 4, cs], scalar1=-1.0, scalar2=None,
                                        op0=ALU.mult), reads=[("kT", tt_)], writes=[("nkT", tt_)])
                            if part == 1:
                                Vt = outs[3]
                                for bl in range(4):
                                    tbk = tt_ * 4 + bl
                                    b_ = nb()
                                    for kc in range(8):
                                        mm(b_, banks[b_][:, :], XB[:, kc, bl * 128:(bl + 1) * 128],
                                           w_in[:, kc, 1024:1536], kc == 0, kc == 7,
                                           wreads("w_in", kc, ncol) + [xk])
                                    op("dve", lambda h, tbk=tbk, b_=b_: h.tensor_tensor(
                                        out=Vt[:, tbk, :], in0=banks[b_][:, :], in1=bv[:], op=ALU.add),
                                        reads=["bv"], writes=[BK(b_), ("Vt", tbk)])

                with c.scope():
                    u_f = c.sb([128, 4, L], BF16, "u_f")
                    phaseA(0, [u_f])
                    w_glu = c.sb([128, 4, 1024], BF16, "w_glu")
                    load_w(w_glu, w_glu_d, 4, 1024, "k_w_glu", "w_glu")
                    YB = (0, 1, 2, 3)
                    WB = (4, 5, 6, 7)
                    XAs = [[c.sb([128, 2, 256], F32, f"XA{a}{b}") for b in range(2)] for a in range(2)]
                    Xc = [c.sb([128, 2, 256], BF16, f"Xc{i}") for i in range(2)]
                    Hf = [c.sb([128, 8, 2, 128], F32, f"Hf{i}") for i in range(2)]
                    Hp = [c.sb([128, 8, 2, 128], BF16, f"Hp{i}") for i in range(2)]
                    Gp = [c.sb([128, 8, 2, 128], BF16, f"Gp{i}") for i in range(4)]
                    tmpg = c.sb([128, 128], F32, "tmpg")
                    zfull = c.sb([128, L], F32, "zfull")
                    gfull = c.sb([128, L], F32, "gfull")
                    yg = c.sb([128, 4, L], BF16, "yg")
                    sg_ = [c.sb([128, 512], F32, "sg0"), c.sb([128, 512], F32, "sg1")]
                    tmpA = c.sb([128, 8, 128], F32, "tmpA")
                    tmpB = c.sb([128, 8, 128], F32, "tmpB")

                    def uq_of(q):
                        return u_f[:, q, :].rearrange("p (c i) -> p i c", i=8)

                    UK = [("u_f", t_) for t_ in range(4)]

                    def tab_ops(p):
                        k, sl = p, p % 2
                        th = []
                        rk = [TK, ("bb", 0), ("bb", 1), ("cp", 0), ("cp", 1)]
                        if sq == 1:
                            th.append(lambda: c.dma("sp", Hp[sl][:].rearrange("p a b n -> p (a b n)"), HG_d[p, 0],
                                                    reads=[("HG_d", p, 0)],
                                                    writes=[("Hp", sl, jj) for jj in range(8)], key=f"k_hpl{sl}"))
                            th.append(lambda: c.dma("sp", Gp[p % 4][:].rearrange("p a b n -> p (a b n)"), HG_d[p, 1],
                                                    reads=[("HG_d", p, 1)],
                                                    writes=[(("Gp", p % 4), i, r) for i in range(8) for r in range(2)],
                                                    key=f"k_gpl{p % 4}"))
                            return th

                        def add(fn, reads, writes):
                            th.append(lambda: op("dve", fn, reads=reads, writes=writes))

                        for (o, a_re, a_im, j0, wk, neg) in ((Hf[sl], bb[0][:, k, :], bb[1][:, k, :], 0, ("Hf", sl), False),
                                                             (Gp[p % 4], cp[0][:, k, :], cp[1][:, k, :], 1, ("Gp", p % 4),
                                                              True)):
                            tA = ("tmpA", neg)
                            tB = ("tmpB", neg)
                            ta = tmpA if not neg else tmpB
                            for j in range(8):
                                si = PI_[:, j0 + j, k:k + 1]
                                add(lambda h, j=j, si=si, ta=ta, a_im=a_im: h.tensor_scalar(
                                    out=ta[:, j, :], in0=a_im, scalar1=si, scalar2=None, op0=ALU.mult), rk, [(tA, j)])
                            for j in range(8):
                                sr = PR[:, j0 + j, k:k + 1]
                                add(lambda h, j=j, sr=sr, ta=ta, a_re=a_re, o=o: h.scalar_tensor_tensor(
                                    out=o[:, j, 0, :], in0=a_re, scalar=sr, in1=ta[:, j, :], op0=ALU.mult,
                                    op1=ALU.subtract), rk + [(tA, j)], [(wk, j, 0)])
                            for j in range(8):
                                sr = PR[:, j0 + j, k:k + 1]
                                if neg:
                                    add(lambda h, j=j, sr=sr, ta=ta, a_im=a_im: h.tensor_scalar(
                                        out=ta[:, j, :], in0=a_im, scalar1=sr, scalar2=-1.0, op0=ALU.mult, op1=ALU.mult),
                                        rk, [(tA, j)])
                                else:
                                    add(lambda h, j=j, sr=sr, ta=ta, a_im=a_im: h.tensor_scalar(
                                        out=ta[:, j, :], in0=a_im, scalar1=sr, scalar2=None, op0=ALU.mult), rk, [(tA, j)])
                            for j in range(8):
                                si = (NPI if neg else PI_)[:, j0 + j, k:k + 1]
                                add(lambda h, j=j, si=si, ta=ta, a_re=a_re, o=o: h.scalar_tensor_tensor(
                                    out=o[:, j, 1, :], in0=a_re, scalar=si, in1=ta[:, j, :], op0=ALU.mult, op1=ALU.add),
                                    rk + [(tA, j)], [(wk, j, 1)])
                        return th

                    def emit_transposes_and_V(p):
                        k, sl, q = p, p % 2, p // 4
                        uq = uq_of(q)
                        for jj in range(8 if sq == 0 else 0):
                            b_ = nb(WB)
                            for r in range(2):
                                op("pe", lambda h, jj=jj, r=r, b_=b_: h.transpose(
                                    out=banks[b_][:, r * 128:(r + 1) * 128], in_=Hf[sl][:, jj, r, :], identity=ident),
                                    reads=[(("Hf", sl), jj, r), "cst"], writes=[BK(b_)])
                            op("act", lambda h, jj=jj, b_=b_: h.activation(
                                out=Hp[sl][:, jj, :, :], in_=banks[b_][:, 0:256].rearrange("p (r n) -> p r n", r=2),
                                func=AF.Copy), writes=[BK(b_), ("Hp", sl, jj)])
                        if sq == 0:
                            c.dma("pool", HG_d[p, 0], Hp[sl][:].rearrange("p a b n -> p (a b n)"),
                                  reads=[("Hp", sl, jj) for jj in range(8)], writes=[("HG_d", p, 0)], key=f"k_hps{sl}")
                            c.dma("pool", HG_d[p, 1], Gp[p % 4][:].rearrange("p a b n -> p (a b n)"),
                                  reads=[(("Gp", p % 4), i, r) for i in range(8) for r in range(2)],
                                  writes=[("HG_d", p, 1)], key=f"k_gps{p % 4}")
                        b_ = nb(WB)
                        for r in range(2):
                            for j in range(8):
                                mm(b_, banks[b_][:, r * 256:(r + 1) * 256], Hp[sl][:, 7 - j, r, :], uq[:, j, :],
                                   j == 0 and r == 0, j == 7 and r == 1, [("Hp", sl, 7 - j)] + UK)
                        op("act", lambda h, b_=b_: h.activation(
                            out=XAs[sl][0][:], in_=banks[b_][:, :].rearrange("p (r n) -> p r n", r=2), func=AF.Copy),
                            writes=[BK(b_), ("XA", sl, 0)])

                    def ks_ops(p):
                        k, sl = p, p % 2
                        th = []
                        for s in range(8):
                            d = 1 << s
                            sa, da = s % 2, 1 - (s % 2)
                            src, dst = XAs[sl][sa], XAs[sl][da]
                            last = (s == 7)
                            o = Xc[sl] if last else dst
                            ok = ("Xc", sl) if last else ("XA", sl, da)
                            kr = [("XA", sl, sa)]

                            def stt(out, in0, sc, in1, rk, wk):
                                th.append(lambda: op("dve", lambda h: h.scalar_tensor_tensor(
                                    out=out, in0=in0, scalar=sc, in1=in1, op0=ALU.mult, op1=ALU.add),
                                    reads=rk + [TK], writes=wk))
                            th.append(lambda o=o, src=src, d=d, kr=kr, ok=ok: op(
                                "pool", lambda h: h.tensor_copy(out=o[:, :, 0:d], in_=src[:, :, 0:d]), reads=kr,
                                writes=[ok]))
                            stt(dst[:, 0, d:], src[:, 0, 0:256 - d], A8R[:, s, k:k + 1], src[:, 0, d:], kr, [("XA", sl, da)])
                            stt(dst[:, 1, d:], src[:, 0, 0:256 - d], A8I[:, s, k:k + 1], src[:, 1, d:], kr, [("XA", sl, da)])
                            stt(o[:, 0, d:], src[:, 1, 0:256 - d], NA8I[:, s, k:k + 1], dst[:, 0, d:],
                                kr + [("XA", sl, da)], [ok])
                            stt(o[:, 1, d:], src[:, 1, 0:256 - d], A8R[:, s, k:k + 1], dst[:, 1, d:],
                                kr + [("XA", sl, da)], [ok])
                        return th

                    def toeplitz(q):
                        uq = uq_of(q)
                        for b in range(4):
                            first = True
                            for i in (2 * b, 2 * b + 1):
                                for tau in range(i + 1):
                                    mm(YB[b], banks[YB[b]][:, (i % 2) * 256:(i % 2) * 256 + 256], Kblk[:, q, tau, :],
                                       uq[:, i - tau, :], first, False, ["Kblk"] + UK)
                                    first = False

                    def farfield(p):
                        sl, kk = p % 2, p % 4
                        for i in range(8):
                            for r in range(2):
                                mm(YB[i // 2], banks[YB[i // 2]][:, (i % 2) * 256 + 1:(i % 2) * 256 + 256],
                                   Gp[p % 4][:, i, r, :], Xc[sl][:, r, 0:255], False, (kk == 3 and i % 2 == 1 and r == 1),
                                   [(("Gp", p % 4), i, r), ("Xc", sl)])

                    def zgelu(q):
                        uq = uq_of(q)
                        zv = zfull[:, :].rearrange("p (c i) -> p i c", i=8)
                        for b in range(4):
                            op("dve", lambda h, b=b: h.scalar_tensor_tensor(
                                out=zv[:, 2 * b:2 * b + 2, :], in0=uq[:, 2 * b:2 * b + 2, :], scalar=V("s5_d", q, 1),
                                in1=banks[YB[b]][:, :].rearrange("p (i c) -> p i c", i=2), op0=ALU.mult, op1=ALU.add),
                                reads=UK + ["vec"], writes=[BK(YB[b]), "zfull"])
                        op("act", lambda h: h.activation(out=gfull[:], in_=zfull[:], func=AF.Square),
                           reads=["zfull"], writes=["gfull"])
                        op("dve", lambda h: h.tensor_scalar(out=gfull[:], in0=gfull[:], scalar1=0.044715, scalar2=1.0,
                                                            op0=ALU.mult, op1=ALU.add), reads=["gfull"], writes=["gfull"])
                        op("dve", lambda h: h.tensor_tensor(out=gfull[:], in0=gfull[:], in1=zfull[:], op=ALU.mult),
                           reads=["gfull", "zfull"], writes=["gfull"])
                        op("act", lambda h: h.activation(out=gfull[:], in_=gfull[:], func=AF.Sigmoid,
                                                         scale=1.5957691216057308), reads=["gfull"], writes=["gfull"])
                        op("dve", lambda h: h.tensor_tensor(out=yg[:, q, :], in0=gfull[:], in1=zfull[:], op=ALU.mult),
                           reads=["gfull", "zfull"], writes=[("yg", q, t_) for t_ in range(4)])

                    def interleave(*lists):
                        idx = [0] * len(lists)
                        more = True
                        while more:
                            more = False
                            for li, l_ in enumerate(lists):
                                if idx[li] < len(l_):
                                    l_[idx[li]]()
                                    idx[li] += 1
                                    more = True

                    interleave(tab_ops(0) + tab_ops(1))
                    for pp in range(8):
                        p0, p1 = 2 * pp, 2 * pp + 1
                        emit_transposes_and_V(p0)
                        emit_transposes_and_V(p1)
                        if p0 % 4 == 0:
                            toeplitz(p0 // 4)
                        nxt = (tab_ops(p0 + 2) + tab_ops(p1 + 2)) if p0 + 2 < 16 else []
                        interleave(ks_ops(p0), ks_ops(p1), nxt[0::2], nxt[1::2])
                        farfield(p0)
                        farfield(p1)
                        if p1 % 4 == 3:
                            zgelu(p1 // 4)
                    for tt_ in range(4):
                        cs = slice(tt_ * 512, (tt_ + 1) * 512)
                        for oc in range(4):
                            bv_, bg_ = nb(), nb()
                            sl = oc % 2
                            for kc in range(4):
                                mm(bv_, banks[bv_][:, :], w_glu[:, kc, oc * 128:(oc + 1) * 128], yg[:, kc, cs],
                                   kc == 0, kc == 3, wreads("w_glu", kc, 1024) + [("yg", kc, tt_)])
                            for kc in range(4):
                                mm(bg_, banks[bg_][:, :], w_glu[:, kc, 512 + oc * 128:512 + (oc + 1) * 128],
                                   yg[:, kc, cs], kc == 0, kc == 3,
                                   wreads("w_glu", kc, 1024) + [("yg", kc, tt_)])
                            op("act", lambda h, oc=oc, bg_=bg_, sl=sl: h.activation(
                                out=sg_[sl][:], in_=banks[bg_][:, :], func=AF.Sigmoid,
                                bias=V("b_glu", 4 + oc, 1), scale=1.0), reads=["vec"],
                                writes=[BK(bg_), ("sg", sl)])
                            op("dve", lambda h, oc=oc, bv_=bv_, sl=sl, cs=cs: h.scalar_tensor_tensor(
                                out=catA[:, oc, cs], in0=banks[bv_][:, :], scalar=V("b_glu", oc, 1), in1=sg_[sl][:],
                                op0=ALU.add, op1=ALU.mult), reads=["vec", ("sg", sl)],
                                writes=[BK(bv_), ("catA", oc, tt_)])

                catB = c.sb([128, 4, L], BF16, "catB")
                with c.scope():
                    qT = c.sb([128, 4, L], BF16, "qT")
                    kT = c.sb([128, 4, L], BF16, "kT")
                    nkT = c.sb([128, 4, L], BF16, "nkT")
                    Vt = c.sb([128, 16, 512], BF16, "Vt")
                    phaseA(1, [qT, kT, nkT, Vt])
                    ZB = (0, 1, 2)
                    BB = (3, 4, 5)
                    OB2 = (6, 7)
                    NBUF = 4
                    FP16 = mybir.dt.float16
                    ebuf = [c.sb([128, 512], F32, f"ebuf{i}") for i in range(NBUF)]
                    spb = [c.sb([128, 512], BF16, f"spb{i}") for i in range(NBUF)]
                    Pb = [c.sb([128, 512], BF16, f"Pb{i}") for i in range(NBUF)]
                    zrow = [c.sb([1, 512], F32, f"zrow{i}") for i in range(NBUF)]
                    A16 = [c.sb([1, 512], FP16, f"A16_{i}") for i in range(2)]
                    onesr = c.sb([1, 128], BF16, "onesr")
                    op("dve", lambda h: h.memset(onesr[:], 1.0), writes=["onesr"])
                    items = []
                    for m_ in range(4):
                        for qt in range(4):
                            for kb in range(4 * qt + 3, -1, -1):
                                for st in range(2):
                                    items.append((2 * m_ + st, qt, kb, st))

                    def geo(i):
                        hd, qt, kb, st = items[i]
                        r = max(0, kb - 4 * qt)
                        return dict(hd=hd, qt=qt, kb=kb, st=st, ch=hd // 2, pb=(hd % 2) * 64, r=r, diag=kb >= 4 * qt,
                                    c0=r * 128, qs=slice(qt * 512 + r * 128, (qt + 1) * 512),
                                    ks=slice(kb * 128, (kb + 1) * 128), sl=i % NBUF,
                                    zb=ZB[i % 3], bb=BB[i % 3], ob=OB2[st],
                                    first=(kb == 4 * qt + 3), last=(kb == 0))

                    def stageA(ii):
                        gs = [geo(i) for i in ii]
                        for g in gs:
                            mm(g["zb"], banks[g["zb"]][:, g["c0"]:], kT[g["pb"]:g["pb"] + 64, g["ch"], g["ks"]],
                               qT[g["pb"]:g["pb"] + 64, g["ch"], g["qs"]], True, not g["diag"],
                               [("kT", g["kb"] // 4), ("qT", g["qt"])])
                        for g in gs:
                            if g["diag"]:
                                mm(g["zb"], banks[g["zb"]][:, g["c0"]:g["c0"] + 128], identb, nmaskb, False, True, ["cstb"])
                        for g in gs:
                            zb_, sl, c0 = g["zb"], g["sl"], g["c0"]
                            op("act", lambda h: h.activation(out=ebuf[sl][:, c0:], in_=banks[zb_][:, c0:], func=AF.Exp),
                               writes=[BK(zb_), ("ebuf", sl)])
                            if not g["last"]:
                                op("dve", lambda h: h.tensor_copy(out=zrow[sl][0:1, c0:], in_=banks[zb_][0:1, c0:]),
                                   writes=[BK(zb_), ("zrow", sl)])
                            op("act", lambda h: h.activation(out=spb[sl][:, c0:], in_=ebuf[sl][:, c0:], func=AF.Ln,
                                                             bias=1.0, scale=1.0),
                               reads=[("ebuf", sl)], writes=[("spb", sl)])

                    def stageB(ii):
                        gs = [geo(i) for i in ii]
                        for g in gs:
                            mm(g["bb"], banks[g["bb"]][:, g["c0"]:], trib, spb[g["sl"]][:, g["c0"]:], True, False,
                               ["cstb", ("spb", g["sl"])])
                        for g in gs:
                            mm(g["bb"], banks[g["bb"]][:, g["c0"]:], nkT[g["pb"]:g["pb"] + 64, g["ch"], g["ks"]],
                               qT[g["pb"]:g["pb"] + 64, g["ch"], g["qs"]], False, False,
                               [("nkT", g["kb"] // 4), ("qT", g["qt"])])
                        for g in gs:
                            if g["diag"]:
                                mm(g["bb"], banks[g["bb"]][:, g["c0"]:g["c0"] + 128], identb, pmaskb, False, g["first"],
                                   ["cstb"])
                        for g in gs:
                            if not g["first"]:
                                mm(g["bb"], banks[g["bb"]][:, g["c0"]:], onesr[:], A16[g["st"]][0:1, g["c0"]:], False,
                                   True, ["onesr", ("A16", g["st"])])
                        for g in gs:
                            bb_, sl, c0, st = g["bb"], g["sl"], g["c0"], g["st"]
                            op("act", lambda h: h.activation(out=Pb[sl][:, c0:], in_=banks[bb_][:, c0:], func=AF.Exp,
                                                             scale=-1.0),
                               writes=[BK(bb_), ("Pb", sl)])
                            if g["first"]:
                                op("dve", lambda h: h.memset(A16[st][:], 0.0), writes=[("A16", st)])
                            if not g["last"]:
                                op("dve", lambda h: h.tensor_tensor(out=A16[st][0:1, c0:], in0=banks[bb_][0:1, c0:],
                                                                    in1=zrow[sl][0:1, c0:], op=ALU.add),
                                   reads=[("zrow", sl)], writes=[BK(bb_), ("A16", st)])

                    def stageC(ii):
                        gs = [geo(i) for i in ii]
                        for g in gs:
                            ob, sl, c0, pb, hd, kb = g["ob"], g["sl"], g["c0"], g["pb"], g["hd"], g["kb"]
                            mm(ob, banks[ob][pb:pb + 64, c0:], Vt[:, kb, hd * 64:(hd + 1) * 64], Pb[sl][:, c0:],
                               g["first"], g["last"], [("Pb", sl), ("Vt", kb)], skip_group_check=True)
                        for g in gs:
                            if g["last"]:
                                ob, pb, ch, qt = g["ob"], g["pb"], g["ch"], g["qt"]
                                op("dve", lambda h: h.tensor_copy(out=catB[pb:pb + 64, ch, qt * 512:(qt + 1) * 512],
                                                                  in_=banks[ob][pb:pb + 64, :]),
                                   writes=[BK(ob), ("catB", ch, qt, pb)])

                    n_st = len(items) // 2
                    for step in range(n_st + 2):
                        if step < n_st:
                            stageA((2 * step, 2 * step + 1))
                        if 0 <= step - 1 < n_st:
                            stageB((2 * (step - 1), 2 * (step - 1) + 1))
                        if 0 <= step - 2 < n_st:
                            stageC((2 * (step - 2), 2 * (step - 2) + 1))

                with c.scope():
                    N = 512
                    if dbg:
                        c.dma("sp", dC_v[:, 0:4, T0:T0 + L], catA[:], writes=["dC0"], key="k_dbg")
                        c.dma("sp", dC_v[:, 4:8, T0:T0 + L], catB[:], writes=["dC1"], key="k_dbg")
                    w_out = c.sb([128, 8, 1024], BF16, "w_out")
                    load_w(w_out, w_out_d, 8, 1024, "k_w_out", "w_out")
                    lnb = ln_scratch(N)
                    ztD = [c.sb([128, 8, N], F32, f"ztD{i}") for i in range(2)]
                    sin_t = c.sb([128, 8, N], F32, "sinD")
                    xbt = c.sb([128, 8, N], BF16, "xbtD")

                    def d_mm(tt_):
                        tok0 = T0 + tt_ * N
                        cs = slice(tt_ * N, (tt_ + 1) * N)
                        z = ztD[tt_ % 2]
                        zk = ("ztD", tt_ % 2)
                        c.dma("sp", sin_t[:], S_v[:, :, tok0:tok0 + N], writes=["sinD"], key="k_sinD")
                        for oc in range(8):
                            b_ = nb()
                            for kc in range(8):
                                src = catA[:, kc, cs] if kc < 4 else catB[:, kc - 4, cs]
                                mm(b_, banks[b_][:, :], w_out[:, kc, oc * 128:(oc + 1) * 128], src,
                                   kc == 0, kc == 7, wreads("w_out", kc, 1024))
                            op("dve", lambda h: h.tensor_tensor(out=z[:, oc, :], in0=banks[b_][:, :],
                                                                in1=sin_t[:, oc, :], op=ALU.add),
                               reads=["sinD"], writes=[BK(b_), zk])

                    def d_ln(tt_):
                        tok0 = T0 + tt_ * N
                        z = ztD[tt_ % 2]
                        zk = ("ztD", tt_ % 2)
                        layer_norm(z, N, zk, z, zk, lnb)
                        ln_epilogue_stream(z, zk, N, tok0, ga10, ba10, "ln1_g0", "ln1_b0", z, xbt, 0, sstkey=zk)

                    for tt_ in range(4):
                        d_mm(tt_)
                        if tt_ > 0:
                            d_ln(tt_ - 1)
                    d_ln(3)


    def dbg_stop(tag):
        if dbg and dbg.get("stop") == tag:
            c.barrier()
            c.dma("sp", dS_d, S_d, writes=["dS"], key="k_dbg")
            c.dma("sp", dX_d, XB_d, writes=["dX"], key="k_dbg")
            c.barrier()
            c.emit()
            c.close()
            return True
        return False

    if dbg_stop("D"):
        return nc
    H2_d = nc.dram_tensor("H2_scr", [D, T], BF16, kind="Internal").ap()
    H2_v = H2_d.rearrange("(c p) t -> p c t", p=128)

    def ffn_phase(layer, final, ga, ba, gname, bname):
        N = 512
        NTL = T // N
        NQ = 4
        b1n = f"fb1_{layer}"
        with c.scope():
            w1s = [c.sb([128, 8, 1024], BF16, f"w1s{i}") for i in range(2)]
            w2s = [c.sb([128, 8, 1024], BF16, f"w2s{i}") for i in range(2)]
            zt = [c.sb([128, 8, N], F32, f"ztE{i}") for i in range(2)]
            xin = [c.sb([128, 8, N], BF16, f"xinE{i}") for i in range(2)]
            sin2 = [c.sb([128, 8, N], F32, f"sinE{i}") for i in range(2)]
            hb = c.sb([128, 8, N], BF16, "hb")
            rl = [c.sb([128, N], F32, f"rl{i}") for i in range(2)]
            lnb = ln_scratch(N)
            xbt = c.sb([128, 8, N], BF16, "xbtE")
            yo = c.sb([128, D], F32, "yo")

            def load_q(qr):
                sl = qr % 2
                load_w(w1s[sl], w1_d[layer], 8, 1024, f"k_w1s{sl}", ("w1s", sl), cs0=qr * 1024)
                load_w(w2s[sl], w2_d[layer][qr * 1024:(qr + 1) * 1024, :], 8, 1024, f"k_w2s{sl}", ("w2s", sl))

            def part_h(qr, tt_):
                sl = tt_ % 2
                ws = qr % 2
                tok0 = tt_ * N
                for hc in range(8):
                    b_ = nb()
                    s2 = hc % 2
                    hcg = qr * 8 + hc
                    for kc in range(8):
                        mm(b_, banks[b_][:, :], w1s[ws][:, kc, hc * 128:(hc + 1) * 128], xin[sl][:, kc, :],
                           kc == 0, kc == 7, wreads(("w1s", ws), kc, 1024) + [("xinE", sl)])
                    op("act", lambda h: h.activation(out=rl[s2][:], in_=banks[b_][:, :], func=AF.Relu,
                                                     bias=V(b1n, hcg, 1), scale=1.0),
                       reads=["vec"], writes=[BK(b_), ("rl", s2)])
                    op("dve", lambda h: h.scalar_tensor_tensor(
                        out=hb[:, hc, :], in0=banks[b_][:, :], scalar=V(b1n, hcg, 1), in1=rl[s2][:],
                        op0=ALU.add, op1=ALU.mult), reads=["vec", ("rl", s2)], writes=[BK(b_), ("hb", hc)])

            def part_out(qr, tt_, mode="store"):
                ws = qr % 2
                tok0 = tt_ * N
                z = zt[tt_ % 2]
                zk = ("ztE", tt_ % 2)
                sin_t = sin2[tt_ % 2]
                sk_ = ("sinE", tt_ % 2)
                for oc in range(8):
                    b_ = nb()
                    for hc in range(8):
                        mm(b_, banks[b_][:, :], w2s[ws][:, hc, oc * 128:(oc + 1) * 128], hb[:, hc, :],
                           hc == 0, hc == 7, wreads(("w2s", ws), hc, 1024) + [("hb", hc)])
                    if mode == "acc":
                        op("dve", lambda h: h.tensor_tensor(out=z[:, oc, :], in0=banks[b_][:, :], in1=z[:, oc, :],
                                                            op=ALU.add), reads=[], writes=[BK(b_), zk])
                    else:
                        op("dve", lambda h: h.tensor_tensor(out=z[:, oc, :], in0=banks[b_][:, :], in1=sin_t[:, oc, :],
                                                            op=ALU.add), reads=[sk_], writes=[BK(b_), zk])
                if mode == "store":
                    c.dma("pool", S_v[:, :, tok0:tok0 + N], z[:], reads=[zk], writes=[("S_d", tok0)],
                          key=f"k_zpart{tt_ % 2}")

            def loads(qr, tt_):
                sl = tt_ % 2
                tok0 = tt_ * N
                c.dma("sp", xin[sl][:], XB_v[:, :, tok0:tok0 + N], writes=[("xinE", sl)], key=f"k_xinE{sl}")
                c.dma("sp", sin2[sl][:], S_v[:, :, tok0:tok0 + N], reads=[("S_d", tok0)], writes=[("sinE", sl)],
                      key=f"k_sinE{sl}")

            def part_ln(tt_):
                tok0 = tt_ * N
                z = zt[tt_ % 2]
                zk = ("ztE", tt_ % 2)
                layer_norm(z, N, zk, z, zk, lnb)
                if not final:
                    ln_epilogue_stream(z, zk, N, tok0, ga, ba, gname, bname, z, xbt, 1, sstkey=zk)
                else:
                    for kc in range(8):
                        op("act", lambda h, kc=kc: h.activation(out=z[:, kc, :], in_=z[:, kc, :], func=AF.Identity,
                                                                bias=V(bname, kc, 1), scale=V(gname, kc, 1)),
                           reads=["vec"], writes=[zk])
                    for bl in range(N // 128):
                        b0, b1 = nb(), nb()
                        for fc in range(8):
                            bq_ = b0 if fc < 4 else b1
                            op("pe", lambda h, fc=fc, bq_=bq_: h.transpose(
                                out=banks[bq_][:, (fc % 4) * 128:(fc % 4) * 128 + 128],
                                in_=z[:, fc, bl * 128:(bl + 1) * 128], identity=ident),
                                reads=[zk, "cst"], writes=[BK(bq_)])
                        op("act", lambda h: h.activation(out=yo[:, 0:512], in_=banks[b0][:, :], func=AF.Copy),
                           writes=[BK(b0), "yo"])
                        op("dve", lambda h: h.tensor_copy(out=yo[:, 512:1024], in_=banks[b1][:, :]),
                           writes=[BK(b1), "yo"])
                        c.dma("sp", y_d[tok0 + bl * 128:tok0 + (bl + 1) * 128, :], yo[:],
                              reads=["yo"], writes=[("y", tok0, bl)], key="k_yo")

            load_q(0)
            load_q(1)
            for qr in range(2):
                loads(qr, 0)
                for tt_ in range(NTL):
                    if tt_ + 1 < NTL:
                        loads(qr, tt_ + 1)
                    part_h(qr, tt_)
                    part_out(qr, tt_, "store")
                    if qr == 0 and tt_ == NTL - 1:
                        load_q(2)
            load_q(3)
            loads(2, 0)
            for tt_ in range(NTL):
                if tt_ + 1 < NTL:
                    loads(2, tt_ + 1)
                part_h(2, tt_)
                part_out(2, tt_, "keep")
                part_h(3, tt_)
                if tt_ > 0:
                    part_ln(tt_ - 1)
                part_out(3, tt_, "acc")
            part_ln(NTL - 1)

    ffn_phase(0, False, ga20, ba20, "ln2_g0", "ln2_b0")
    if dbg_stop("E0"):
        return nc

    N = 512
    NTL = T // N
    with c.scope():
        pw1 = c.sb([128, 8, 2048], BF16, "pw1")
        load_w(pw1, pw1_d, 8, 2048, "k_pw1", "pw1")
        dg = c.sb([128, 8, 31, 128], BF16, "dg")
        for oc in range(8):
            for k in range(31):
                op("dve", lambda h, oc=oc, k=k: h.tensor_scalar(out=dg[:, oc, k, :], in0=ident,
                                                                scalar1=V("w_dw", k * 8 + oc, 1), scalar2=None,
                                                                op0=ALU.mult), reads=["cst", "vec"], writes=[("dg", oc)])
        lnb = ln_scratch(N)
        xin = [c.sb([128, 8, N], BF16, f"xinF{i}") for i in range(2)]
        hbuf = c.sb([128, 8, 30 + N], BF16, "hbuf")
        sgF = [c.sb([128, N], F32, f"sgF{i}") for i in range(2)]
        cvs = [c.sb([128, 8, N], F32, f"cv{i}") for i in range(2)]
        h2 = c.sb([128, 8, N], BF16, "h2")

        def f1_load(tt_):
            sl = tt_ % 2
            c.dma("sp", xin[sl][:], XB_v[:, :, tt_ * N:(tt_ + 1) * N], writes=[("xinF", sl)], key=f"k_xinF{sl}")

        def f1_glu(tt_):
            sl = tt_ % 2
            if tt_ % 4 == 0:
                op("dve", lambda h: h.memset(hbuf[:, :, 0:30], 0.0), writes=[("hbuf", i) for i in range(8)])
            else:
                for oc in range(8):
                    op("dve", lambda h, oc=oc: h.tensor_copy(out=hbuf[:, oc, 0:30], in_=hbuf[:, oc, N:N + 30]),
                       reads=[], writes=[("hbuf", oc)])
            for oc in range(8):
                bv_, bg_ = nb(), nb()
                s2 = oc % 2
                for kc in range(8):
                    mm(bv_, banks[bv_][:, :], pw1[:, kc, oc * 128:(oc + 1) * 128], xin[sl][:, kc, :],
                       kc == 0, kc == 7, wreads("pw1", kc, 2048) + [("xinF", sl)])
                for kc in range(8):
                    mm(bg_, banks[bg_][:, :], pw1[:, kc, 1024 + oc * 128:1024 + (oc + 1) * 128], xin[sl][:, kc, :],
                       kc == 0, kc == 7, wreads("pw1", kc, 2048) + [("xinF", sl)])
                op("act", lambda h: h.activation(out=sgF[s2][:], in_=banks[bg_][:, :], func=AF.Sigmoid,
                                                 bias=V("b_pw1", 8 + oc, 1), scale=1.0),
                   reads=["vec"], writes=[BK(bg_), ("sgF", s2)])
                op("dve", lambda h: h.scalar_tensor_tensor(
                    out=hbuf[:, oc, 30:30 + N], in0=banks[bv_][:, :], scalar=V("b_pw1", oc, 1), in1=sgF[s2][:],
                    op0=ALU.add, op1=ALU.mult), reads=["vec", ("sgF", s2)], writes=[BK(bv_), ("hbuf", oc)])

        NDVE = 5
        cacc = [c.sb([128, N], F32, f"cacc{i}") for i in range(2)]

        def f1_conv(tt_):
            cv = cvs[tt_ % 2]
            for oc in range(8):
                b_ = nb()
                s_ = oc % 2
                acc = cacc[s_]
                for k in range(NDVE, 31):
                    mm(b_, banks[b_][:, :], dg[:, oc, k, :], hbuf[:, oc, k:k + N], k == NDVE, k == 30,
                       [("dg", oc), ("hbuf", oc)])
                op("dve", lambda h: h.tensor_scalar(out=acc[:], in0=hbuf[:, oc, 0:N], scalar1=V("w_dw", oc, 1),
                                                    scalar2=None, op0=ALU.mult),
                   reads=[("hbuf", oc), "vec"], writes=[("cacc", s_)])
                for k in range(1, NDVE):
                    op("dve", lambda h, k=k: h.scalar_tensor_tensor(out=acc[:], in0=hbuf[:, oc, k:k + N],
                                                                    scalar=V("w_dw", k * 8 + oc, 1), in1=acc[:],
                                                                    op0=ALU.mult, op1=ALU.add),
                       reads=[("hbuf", oc), "vec"], writes=[("cacc", s_)])
                op("dve", lambda h: h.scalar_tensor_tensor(out=cv[:, oc, :], in0=banks[b_][:, :],
                                                           scalar=V("b_dw", oc, 1), in1=acc[:],
                                                           op0=ALU.add, op1=ALU.add),
                   reads=["vec", ("cacc", s_)], writes=[BK(b_), ("cv", tt_ % 2)])

        def f1_ln(tt_):
            cv = cvs[tt_ % 2]
            ck = ("cv", tt_ % 2)
            layer_norm(cv, N, ck, cv, ck, lnb)
            for oc in range(8):
                op("act", lambda h, oc=oc: h.activation(out=h2[:, oc, :], in_=cv[:, oc, :], func=AF.Silu,
                                                        bias=V("cln_b", oc, 1), scale=V("cln_g", oc, 1)),
                   reads=[ck, "vec"], writes=["h2"])
            c.dma("pool", H2_v[:, :, tt_ * N:(tt_ + 1) * N], h2[:], reads=["h2"], writes=[("H2_d", tt_)], key="k_h2")

        f1_load(0)
        for tt_ in range(NTL):
            if tt_ + 1 < NTL:
                f1_load(tt_ + 1)
            f1_glu(tt_)
            if tt_ > 0:
                f1_ln(tt_ - 1)
            f1_conv(tt_)
        f1_ln(NTL - 1)
    with c.scope():
        pw2 = c.sb([128, 8, 1024], BF16, "pw2")
        load_w(pw2, pw2_d, 8, 1024, "k_pw2", "pw2")
        lnb = ln_scratch(N)
        h2i = [c.sb([128, 8, N], BF16, f"h2i{i}") for i in range(2)]
        sin2 = [c.sb([128, 8, N], F32, f"sinF{i}") for i in range(2)]
        zts = [c.sb([128, 8, N], F32, f"ztF{i}") for i in range(2)]
        xbt = c.sb([128, 8, N], BF16, "xbtF")

        def f2_load(tt_):
            sl = tt_ % 2
            tok0 = tt_ * N
            c.dma("sp", h2i[sl][:], H2_v[:, :, tok0:tok0 + N], writes=[("h2i", sl)], key=f"k_h2i{sl}")
            c.dma("sp", sin2[sl][:], S_v[:, :, tok0:tok0 + N], writes=[("sinF", sl)], key=f"k_sinF{sl}")

        def f2_mm(tt_):
            sl = tt_ % 2
            z = zts[sl]
            for oc in range(8):
                b_ = nb()
                for kc in range(8):
                    mm(b_, banks[b_][:, :], pw2[:, kc, oc * 128:(oc + 1) * 128], h2i[sl][:, kc, :], kc == 0, kc == 7,
                       wreads("pw2", kc, 1024) + [("h2i", sl)])
                op("dve", lambda h: h.tensor_tensor(out=z[:, oc, :], in0=banks[b_][:, :], in1=sin2[sl][:, oc, :],
                                                    op=ALU.add), reads=[("sinF", sl)], writes=[BK(b_), ("ztF", sl)])

        def f2_ln(tt_):
            sl = tt_ % 2
            z = zts[sl]
            zk = ("ztF", sl)
            layer_norm(z, N, zk, z, zk, lnb)
            ln_epilogue_stream(z, zk, N, tt_ * N, ga11, ba11, "ln1_g1", "ln1_b1", z, xbt, 2, sstkey=zk)

        f2_load(0)
        for tt_ in range(NTL):
            if tt_ + 1 < NTL:
                f2_load(tt_ + 1)
            f2_mm(tt_)
            if tt_ > 0:
                f2_ln(tt_ - 1)
        f2_ln(NTL - 1)

    if dbg and dbg.get("stop") == "F":
        c.barrier()
        c.dma("sp", dC_d, H2_d, writes=["dC0"], key="k_dbg")
    if dbg_stop("F"):
        return nc
    ffn_phase(1, True, None, None, "ln2_g1", "ln2_b1")
    if dbg:
        dbg_stop(dbg.get("stop"))
        return nc

    c.barrier()
    c.emit()
    c.close()
    return nc


def _col(v):
    v = np.asarray(v, np.float32).reshape(-1, 128)
    return np.ascontiguousarray(v.T)


def _host_layout(inp):
    f = lambda a: np.ascontiguousarray(np.asarray(a, np.float32))
    vec = np.zeros((128, NV), np.float32)

    def put(name, arr):
        o, w = VOFF[name]
        vec[:, o:o + w] = arr

    for l in range(2):
        put(f"ln1_g{l}", _col(inp["ln1_g"][l])); put(f"ln1_b{l}", _col(inp["ln1_b"][l]))
        put(f"ln2_g{l}", _col(inp["ln2_g"][l])); put(f"ln2_b{l}", _col(inp["ln2_b"][l]))
        put(f"fb1_{l}", _col(inp["ffn_b1"][l])); put(f"fb2_{l}", _col(inp["ffn_b2"][l]))
    put("b_in", _col(inp["mix_b_in"][0][:1536]))
    put("s5_d", _col(inp["s5_d"][0])); put("b_glu", _col(inp["s5_b_glu"][0])); put("b_out", _col(inp["mix_b_out"][0]))
    put("b_pw1", _col(inp["conv_b_pw1"][0])); put("b_dw", _col(inp["conv_b_dw"][0]))
    put("cln_g", _col(inp["conv_ln_g"][0])); put("cln_b", _col(inp["conv_ln_b"][0]))
    put("b_pw2", _col(inp["conv_b_pw2"][0]))
    wd = np.asarray(inp["conv_w_dw"][0], np.float32)
    put("w_dw", np.concatenate([_col(wd[k]) for k in range(31)], axis=1))
    bv = np.ascontiguousarray(np.broadcast_to(np.asarray(inp["mix_b_in"][0][1536:], np.float32)[None, :], (128, 512)))
    cst = np.zeros((128, 640), np.float32)
    cst[:, 0:128] = np.eye(128)
    j = np.arange(128)[:, None]; s = np.arange(128)[None, :]
    cst[:, 128:256] = (j >= s)
    cst[:, 256:384] = np.where(j >= s, -30000.0, 0.0)
    cst[:, 384:512] = np.where(j >= s, 30000.0, 0.0)
    cst[:, 512:640] = 1.0 / 1024.0
    lr = np.asarray(inp["s5_lambda_re"][0], np.float32); li = np.asarray(inp["s5_lambda_im"][0], np.float32)
    ld = np.asarray(inp["s5_log_dt"][0], np.float32)
    lam = np.zeros((128, 3, 16), np.float32)
    for k in range(16):
        for g2 in range(2):
            g = 2 * k + g2
            lam[g2 * 64:(g2 + 1) * 64, 0, k] = lr[g]
            lam[g2 * 64:(g2 + 1) * 64, 1, k] = li[g]
            lam[g2 * 64:(g2 + 1) * 64, 2, k] = ld[g]

    def pad_layout(arr_gnp):
        out = np.zeros((128, 16, 128), np.float32)
        for k in range(16):
            for g2 in range(2):
                g = 2 * k + g2
                c0 = 16 * (g % 8)
                out[g2 * 64:(g2 + 1) * 64, k, c0:c0 + 16] = arr_gnp[g]
        return out

    bre = np.asarray(inp["s5_b_re"][0], np.float32); bim = np.asarray(inp["s5_b_im"][0], np.float32)
    cre = np.asarray(inp["s5_c_re"][0], np.float32).transpose(0, 2, 1)
    cim = np.asarray(inp["s5_c_im"][0], np.float32).transpose(0, 2, 1)
    shared = {
        "w_in": f(inp["mix_w_in"][0]), "w_glu": f(inp["s5_w_glu"][0]), "w_out": f(inp["mix_w_out"][0]),
        "pw1": f(inp["conv_w_pw1"][0]), "pw2": f(inp["conv_w_pw2"][0]),
        "w1_0": f(inp["ffn_w1"][0]), "w1_1": f(inp["ffn_w1"][1]),
        "w2_0": f(inp["ffn_w2"][0]), "w2_1": f(inp["ffn_w2"][1]),
        "vecs": vec, "bv_bc": bv, "consts": cst, "lamll": lam,
        "bt_re": pad_layout(bre), "bt_im": pad_layout(bim), "cp_re": pad_layout(cre), "cp_im": pad_layout(cim),
    }
    return shared


def kernel(**inputs):
    x = np.asarray(inputs["x"], np.float32)
    shared = _host_layout(inputs)
    nc = build()
    in_maps = []
    for i in range(8):
        m = dict(shared)
        m["x"] = np.ascontiguousarray(x[2 * i:2 * i + 2].reshape(T, D))
        in_maps.append(m)
    res = run_bass_kernel_spmd(nc, in_maps, core_ids=list(range(8)))
    out = np.concatenate([r["y"].reshape(2, L, D) for r in res.results], axis=0)
    return out.astype(np.float32)
```

```python
import contextlib
import numpy as np
import concourse.bass as bass
import concourse.mybir as mybir
from concourse.bass_utils import run_bass_kernel_spmd

F32 = mybir.dt.float32
BF16 = mybir.dt.bfloat16
I32 = mybir.dt.int32
AF = mybir.ActivationFunctionType
ALU = mybir.AluOpType

D = 1024
L = 2048
NSEQ = 2
T = NSEQ * L
ALPHA = 4.0 ** 0.25
EPS = 1e-5
PI = float(np.pi)
PAD = 1024


class _Rec:
    def __init__(self):
        self.call = None

    def __getattr__(self, name):
        def f(*a, **kw):
            self.call = (name, a, kw)
            return self
        return f


class Ctx:
    def __init__(self, nc):
        self.nc = nc
        self.es = contextlib.ExitStack()
        self.stacks = [self.es]
        self.engs = ["pe", "act", "dve", "pool", "sp"]
        self.sem = {}
        self.cnt = {}
        for k in self.engs:
            self.sem[k] = self.es.enter_context(nc.semaphore("s_" + k))
            self.cnt[k] = 0
        self.waited = {k: {} for k in self.engs}
        self.lastw = {}
        self.reads = {}
        self.dsem = {}
        self.dcnt = {}
        self.nsb = 0
        self.prog = {k: [] for k in self.engs}

    def sb(self, shape, dt, name=None):
        self.nsb += 1
        return self.stacks[-1].enter_context(
            self.nc.sbuf_tensor(f"{name or 'sb'}_{self.nsb}", list(shape), dt))

    def ps(self, shape, dt, name=None):
        self.nsb += 1
        return self.stacks[-1].enter_context(
            self.nc.psum_tensor(f"{name or 'ps'}_{self.nsb}", list(shape), dt))

    @contextlib.contextmanager
    def scope(self):
        st = contextlib.ExitStack()
        self.stacks.append(st)
        try:
            yield
        finally:
            self.barrier()
            self.stacks.pop()
            st.close()

    def _semobj(self, key):
        return self.sem[key] if key in self.sem else self.dsem[key]

    def _deps(self, reads, writes):
        deps = []
        for r in reads:
            if r in self.lastw:
                deps.append(self.lastw[r])
        for w in writes:
            if w in self.lastw:
                deps.append(self.lastw[w])
            deps.extend(self.reads.get(w, []))
        return deps

    def _wait(self, e, deps):
        best = {}
        for (k, v) in deps:
            if e == "pe" and k == "pe":
                continue
            if v > best.get(k, 0):
                best[k] = v
        for k, v in best.items():
            if self.waited[e].get(k, 0) >= v:
                continue
            so = self._semobj(k)
            self.prog[e].append(lambda h, so=so, v=v: h.wait_ge(so, v))
            self.waited[e][k] = v

    def _record(self, ticket, reads, writes):
        for r in reads:
            self.reads.setdefault(r, []).append(ticket)
        for w in writes:
            self.lastw[w] = ticket
            self.reads[w] = []

    def op(self, e, fn, reads=(), writes=()):
        self._wait(e, self._deps(reads, writes))
        self.cnt[e] += 1
        so = self.sem[e]
        rec = _Rec()
        fn(rec)
        name, a, kw = rec.call
        self.prog[e].append(lambda h, name=name, a=a, kw=kw, so=so: getattr(h, name)(*a, **kw).then_inc(so, 1))
        t = (e, self.cnt[e])
        self._record(t, reads, writes)
        return t

    def dma(self, q, out, in_, reads=(), writes=(), key=None, **kw):
        if key not in self.dsem:
            self.dsem[key] = self.es.enter_context(self.nc.semaphore(f"d{len(self.dsem)}"))
            self.dcnt[key] = 0
        self._wait(q, self._deps(reads, writes))
        so = self.dsem[key]
        self.prog[q].append(lambda h, out=out, in_=in_, kw=kw, so=so:
                            h.dma_start(out=out, in_=in_, **kw).then_inc(so, 16))
        self.dcnt[key] += 16
        t = (key, self.dcnt[key])
        self._record(t, reads, writes)
        return t

    def barrier(self):
        deps = [(k, self.cnt[k]) for k in self.engs if self.cnt[k] > 0]
        deps += [(k, v) for k, v in self.dcnt.items() if v > 0]
        for e in self.engs:
            self._wait(e, deps)
        self.lastw = {}
        self.reads = {}

    def emit(self):
        with self.nc.Block() as block:
            def mk(e):
                def body(h):
                    for f in self.prog[e]:
                        f(h)
                return body
            block.tensor(mk("pe"))
            block.scalar(mk("act"))
            block.vector(mk("dve"))
            block.gpsimd(mk("pool"))
            block.sync(mk("sp"))

    def close(self):
        self.es.close()


VEC_SPEC = [("ln1_g0", 8), ("ln1_b0", 8), ("ln2_g0", 8), ("ln2_b0", 8),
            ("ln1_g1", 8), ("ln1_b1", 8), ("ln2_g1", 8), ("ln2_b1", 8),
            ("fb1_0", 32), ("fb1_1", 32), ("fb2_0", 8), ("fb2_1", 8),
            ("b_in", 12), ("s5_d", 4), ("b_glu", 8), ("b_out", 8),
            ("b_pw1", 16), ("b_dw", 8), ("cln_g", 8), ("cln_b", 8), ("b_pw2", 8),
            ("w_dw", 31 * 8)]
VOFF = {}
_o = 0
for _n, _w in VEC_SPEC:
    VOFF[_n] = (_o, _w)
    _o += _w
NV = _o


def build(dbg=None):
    nc = bass.Bass("TRN2", target_bir_lowering=False)

    def din(name, shape, dt=F32):
        return nc.dram_tensor(name, list(shape), dt, kind="ExternalInput").ap()

    x_d = din("x", [T, D])
    w_in_d = din("w_in", [D, 2048])
    w_glu_d = din("w_glu", [512, 1024])
    w_out_d = din("w_out", [D, D])
    pw1_d = din("pw1", [D, 2048])
    pw2_d = din("pw2", [D, D])
    w1_d = [din("w1_0", [D, 4096]), din("w1_1", [D, 4096])]
    w2_d = [din("w2_0", [4096, D]), din("w2_1", [4096, D])]
    vec_d = din("vecs", [128, NV])
    bv_d = din("bv_bc", [128, 512])
    cst_d = din("consts", [128, 640])
    lam_d = din("lamll", [128, 3, 16])
    bt_d = [din("bt_re", [128, 16, 128]), din("bt_im", [128, 16, 128])]
    cp_d = [din("cp_re", [128, 16, 128]), din("cp_im", [128, 16, 128])]
    y_d = nc.dram_tensor("y", [T, D], F32, kind="ExternalOutput").ap()
    S_d = nc.dram_tensor("S_scr", [D, T], F32, kind="Internal").ap()
    XB_d = nc.dram_tensor("XB_scr", [D, T], BF16, kind="Internal").ap()
    if dbg:
        dS_d = nc.dram_tensor("dbgS", [D, T], F32, kind="ExternalOutput").ap()
        dX_d = nc.dram_tensor("dbgX", [D, T], BF16, kind="ExternalOutput").ap()
        dC_d = nc.dram_tensor("dbgC", [D, T], BF16, kind="ExternalOutput").ap()
        dC_v = dC_d.rearrange("(c p) t -> p c t", p=128)
    HG_d = nc.dram_tensor("HG_scr", [16, 2, 128, 2048], BF16, kind="Internal").ap()
    S_v = S_d.rearrange("(c p) t -> p c t", p=128)
    XB_v = XB_d.rearrange("(c p) t -> p c t", p=128)

    c = Ctx(nc)
    op = c.op

    vec = c.sb([128, NV], F32, "vec")
    cst = c.sb([128, 640], F32, "cst")
    cstb = c.sb([128, 640], BF16, "cstb")
    der = c.sb([128, 160], F32, "der")
    banks = [c.ps([128, 512], F32, f"bank{i}") for i in range(8)]
    ident = cst[:, 0:128]
    identb = cstb[:, 0:128]
    trib = cstb[:, 128:256]
    nmaskb = cstb[:, 256:384]
    pmaskb = cstb[:, 384:512]
    onesb = cstb[:, 512:640]

    def V(name, i=0, n=1):
        o, w = VOFF[name]
        return vec[:, o + i:o + i + n]

    c.dma("sp", vec[:], vec_d, writes=["vec"], key="k_vec")
    c.dma("sp", cst[:], cst_d, writes=["cst"], key="k_cst")
    op("dve", lambda h: h.tensor_copy(out=cstb[:], in_=cst[:]), reads=["cst"], writes=["cstb"])

    DER = {}
    _do = [0]

    def dalloc(name, n):
        DER[name] = _do[0]
        _do[0] += n
        return der[:, DER[name]:DER[name] + n]

    bq8 = dalloc("bq8", 4)
    op("dve", lambda h: h.tensor_scalar(out=bq8, in0=V("b_in", 4, 4), scalar1=0.125, scalar2=None,
                                        op0=ALU.mult), reads=["vec"], writes=["der"])

    def ln_consts(tag, gname, bname, nextb):
        ga = dalloc("ga" + tag, 8)
        ba = dalloc("ba" + tag, 8)
        op("dve", lambda h: h.tensor_scalar(out=ga, in0=V(gname, 0, 8), scalar1=ALPHA, scalar2=None,
                                            op0=ALU.mult), reads=["vec"], writes=["der"])
        op("dve", lambda h: h.scalar_tensor_tensor(out=ba, in0=V(bname, 0, 8), scalar=ALPHA,
                                                   in1=V(nextb, 0, 8), op0=ALU.mult, op1=ALU.add),
           reads=["vec"], writes=["der"])
        return ga, ba

    ga10, ba10 = ln_consts("10", "ln1_g0", "ln1_b0", "fb2_0")
    ga20, ba20 = ln_consts("20", "ln2_g0", "ln2_b0", "b_pw2")
    ga11, ba11 = ln_consts("11", "ln1_g1", "ln1_b1", "fb2_1")

    bkrr = [0]

    def nb(pool=(0, 1, 2, 3, 4, 5, 6, 7)):
        bkrr[0] += 1
        return pool[bkrr[0] % len(pool)]

    def BK(i):
        return ("bk", i)

    def mm(bank, out_ap, lhsT, rhs, start, stop, reads, **kw):
        op("pe", lambda h: h.matmul(out_ap, lhsT=lhsT, rhs=rhs, start=start, stop=stop, **kw),
           reads=reads, writes=[BK(bank)])

    def load_w(dst, src_d, kc_n, ncols, key, rkey, cs0=0, c0b=0):
        sv = src_d.rearrange("(c p) f -> p c f", p=128)
        step = 2048
        for kc in range(kc_n):
            for c0 in range(0, ncols, step):
                c1 = min(ncols, c0 + step)
                c.dma("pool", dst[:, kc, c0b + c0:c0b + c1], sv[:, kc, cs0 + c0:cs0 + c1],
                      writes=[(rkey, kc, c0)], key=key)
        fin = (key, c.dcnt[key])
        for kc in range(kc_n):
            for c0 in range(0, ncols, step):
                c.lastw[(rkey, kc, c0)] = fin

    def wreads(rkey, kc, ncols):
        return [(rkey, kc, c0) for c0 in range(0, ncols, 2048)]

    def layer_norm(zt, N, zkey, xh, xhkey, lnb):
        zb, zq, msq, var, rstd, nmr = lnb["zb"], lnb["zq"], lnb["msq"], lnb["var"], lnb["rstd"], lnb["nmr"]
        op("act", lambda h: h.activation(out=zb[:, :, 0:N], in_=zt[:, :, 0:N], func=AF.Copy),
           reads=[zkey], writes=["ln_zb"])
        op("act", lambda h: h.activation(out=zq[:, :, 0:N], in_=zt[:, :, 0:N], func=AF.Square),
           reads=[zkey], writes=["ln_zq"])
        bm, bq = nb(), nb()
        for kc in range(8):
            mm(bm, banks[bm][:, 0:N], onesb, zb[:, kc, 0:N], kc == 0, kc == 7, ["ln_zb", "cstb"])
        for kc in range(8):
            mm(bq, banks[bq][:, 0:N], onesb, zq[:, kc, 0:N], kc == 0, kc == 7, ["ln_zq", "cstb"])
        op("act", lambda h: h.activation(out=msq[:, 0:N], in_=banks[bm][:, 0:N], func=AF.Square),
           writes=[BK(bm), "ln_msq"])
        op("dve", lambda h: h.tensor_tensor(out=var[:, 0:N], in0=banks[bq][:, 0:N], in1=msq[:, 0:N],
                                            op=ALU.subtract), reads=["ln_msq"], writes=[BK(bq), "ln_var"])
        op("dve", lambda h: h.tensor_scalar(out=var[:, 0:N], in0=var[:, 0:N], scalar1=EPS, scalar2=None,
                                            op0=ALU.add), reads=["ln_var"], writes=["ln_var"])
        op("act", lambda h: h.activation(out=var[:, 0:N], in_=var[:, 0:N], func=AF.Ln),
           reads=["ln_var"], writes=["ln_var"])
        op("act", lambda h: h.activation(out=banks[bq][:, 0:N], in_=var[:, 0:N], func=AF.Exp, scale=-0.5),
           reads=["ln_var"], writes=[BK(bq)])
        mb = banks[bm][:, 0:N].unsqueeze(1).broadcast_to([128, 8, N])
        rb = banks[bq][:, 0:N].unsqueeze(1).broadcast_to([128, 8, N])
        op("dve", lambda h: h.tensor_tensor(out=xh[:, :, 0:N], in0=zt[:, :, 0:N], in1=mb, op=ALU.subtract),
           reads=[zkey], writes=[BK(bm), xhkey])
        op("dve", lambda h: h.tensor_tensor(out=xh[:, :, 0:N], in0=xh[:, :, 0:N], in1=rb, op=ALU.mult),
           reads=[], writes=[BK(bq), xhkey])

    def ln_scratch(N):
        return {"zb": c.sb([128, 8, N], BF16, "zb"), "zq": c.sb([128, 8, N], BF16, "zq"),
                "msq": c.sb([128, N], F32, "msq"), "var": c.sb([128, N], F32, "var"),
                "rstd": c.sb([128, N], F32, "rstd"), "nmr": c.sb([128, N], F32, "nmr")}

    def ln_epilogue_stream(xh, xhkey, N, tok0, ga, ba, gname, bname, sst, xbt, slot, sstkey=None):
        sk = sstkey or ("sst", slot)
        for kc in range(8):
            op("act", lambda h, kc=kc: h.activation(out=xbt[:, kc, 0:N], in_=xh[:, kc, 0:N], func=AF.Identity,
                                                    bias=V(bname, kc, 1), scale=V(gname, kc, 1)),
               reads=[xhkey, "vec"], writes=[("xbt", slot)])
        for kc in range(8):
            op("act", lambda h, kc=kc: h.activation(out=sst[:, kc, 0:N], in_=xh[:, kc, 0:N], func=AF.Identity,
                                                    bias=ba[:, kc:kc + 1], scale=ga[:, kc:kc + 1]),
               reads=[xhkey, "der"], writes=[sk])
        c.dma("pool", S_v[:, :, tok0:tok0 + N], sst[:, :, 0:N], reads=[sk],
              writes=[("S_d", tok0)], key=f"k_sst{slot}")
        c.dma("pool", XB_v[:, :, tok0:tok0 + N], xbt[:, :, 0:N], reads=[("xbt", slot)],
              writes=[("XB_d", tok0)], key=f"k_xbt{slot}")

    with c.scope():
        bv = c.sb([128, 512], F32, "bv")
        c.dma("sp", bv[:], bv_d, writes=["bv"], key="k_bv")

        lam = c.sb([128, 3, 16], F32, "lam")
        c.dma("sp", lam[:], lam_d, writes=["lam"], key="k_lam")
        tb = c.sb([128, 24, 16], F32, "s5tmp")
        tbi = c.sb([128, 16], I32, "s5tmpi")
        AR = c.sb([128, 11, 16], F32, "AR")
        AI = c.sb([128, 11, 16], F32, "AI")
        NAI = c.sb([128, 11, 16], F32, "NAI")
        TK = "s5t"

        def tt(out, a, b, o, e="dve"):
            op(e, lambda h: h.tensor_tensor(out=out, in0=a, in1=b, op=o), reads=[TK, "lam"], writes=[TK])

        def ts(out, a, s1, o1, s2=None, o2=None):
            if o2 is None:
                op("dve", lambda h: h.tensor_scalar(out=out, in0=a, scalar1=s1, scalar2=None, op0=o1),
                   reads=[TK, "lam"], writes=[TK])
            else:
                op("dve", lambda h: h.tensor_scalar(out=out, in0=a, scalar1=s1, scalar2=s2, op0=o1, op1=o2),
                   reads=[TK, "lam"], writes=[TK])

        def act(out, a, f, **kw):
            op("act", lambda h: h.activation(out=out, in_=a, func=f, **kw), reads=[TK, "lam"], writes=[TK])

        dt_, lr_, a_, th_, mag_ = tb[:, 0, :], tb[:, 1, :], tb[:, 2, :], tb[:, 3, :], tb[:, 4, :]
        act(dt_, lam[:, 2, :], AF.Exp)
        ts(lr_, lam[:, 0, :], -1e-4, ALU.min)
        tt(a_, lr_, dt_, ALU.mult)
        tt(th_, lam[:, 1, :], dt_, ALU.mult)
        act(mag_, a_, AF.Exp)

        def sin_of(out, src, shift):
            u, kf, r, g = tb[:, 5, :], tb[:, 6, :], tb[:, 7, :], tb[:, 8, :]
            ts(u, src, shift, ALU.add, 1.0 / (2 * PI), ALU.mult)
            op("dve", lambda h: h.tensor_copy(out=tbi[:], in_=u), reads=[TK], writes=[TK])
            op("dve", lambda h: h.tensor_copy(out=kf, in_=tbi[:]), reads=[TK], writes=[TK])
            ts(kf, kf, -2 * PI, ALU.mult, shift, ALU.add)
            tt(r, src, kf, ALU.add)
            ts(g, r, PI, ALU.is_gt, -2 * PI, ALU.mult)
            tt(r, r, g, ALU.add)
            ts(g, r, -PI, ALU.is_lt, 2 * PI, ALU.mult)
            tt(r, r, g, ALU.add)
            act(out, r, AF.Sin)

        sn_, cs_ = tb[:, 9, :], tb[:, 10, :]
        sin_of(sn_, th_, 0.0)
        sin_of(cs_, th_, PI / 2)
        tt(AR[:, 0, :], mag_, cs_, ALU.mult)
        tt(AI[:, 0, :], mag_, sn_, ALU.mult)
        for s in range(10):
            t1, t2 = tb[:, 11, :], tb[:, 12, :]
            tt(t1, AR[:, s, :], AR[:, s, :], ALU.mult)
            tt(t2, AI[:, s, :], AI[:, s, :], ALU.mult)
            tt(AR[:, s + 1, :], t1, t2, ALU.subtract)
            tt(t1, AR[:, s, :], AI[:, s, :], ALU.mult)
            ts(AI[:, s + 1, :], t1, 2.0, ALU.mult)
        ts(NAI[:], AI[:], -1.0, ALU.mult)
        den, nr, Fr, Fi = tb[:, 13, :], tb[:, 14, :], tb[:, 15, :], tb[:, 16, :]
        t1, t2 = tb[:, 11, :], tb[:, 12, :]
        tt(t1, lr_, lr_, ALU.mult)
        tt(t2, lam[:, 1, :], lam[:, 1, :], ALU.mult)
        tt(den, t1, t2, ALU.add)
        op("dve", lambda h: h.reciprocal(out=den, in_=den), reads=[TK], writes=[TK])
        ts(nr, AR[:, 0, :], -1.0, ALU.add)
        tt(t1, nr, lr_, ALU.mult)
        tt(t2, AI[:, 0, :], lam[:, 1, :], ALU.mult)
        tt(t1, t1, t2, ALU.add)
        tt(Fr, t1, den, ALU.mult)
        tt(t1, AI[:, 0, :], lr_, ALU.mult)
        tt(t2, nr, lam[:, 1, :], ALU.mult)
        tt(t1, t1, t2, ALU.subtract)
        tt(Fi, t1, den, ALU.mult)

        bb = [c.sb([128, 16, 128], F32, "bb_re"), c.sb([128, 16, 128], F32, "bb_im")]
        cp = [c.sb([128, 16, 128], F32, "cp_re"), c.sb([128, 16, 128], F32, "cp_im")]
        PR = c.sb([128, 9, 16], F32, "PR")
        PI_ = c.sb([128, 9, 16], F32, "PI")
        A8R = c.sb([128, 8, 16], F32, "A8R")
        A8I = c.sb([128, 8, 16], F32, "A8I")
        NA8I = c.sb([128, 8, 16], F32, "NA8I")
        Kblk = c.sb([128, 4, 8, 128], BF16, "Kblk")
        for i in range(2):
            c.dma("sp", cp[i][:], cp_d[i], writes=[("cp", i)], key=f"k_cp{i}")
        op("dve", lambda h: h.memset(PR[:, 0, :], 1.0), reads=[TK], writes=[TK])
        op("dve", lambda h: h.memset(PI_[:, 0, :], 0.0), reads=[TK], writes=[TK])
        ts(PR[:, 1, :], AR[:, 0, :], 1.0, ALU.mult)
        ts(PI_[:, 1, :], AI[:, 0, :], 1.0, ALU.mult)
        for j in range(1, 8):
            t1, t2 = tb[:, 11, :], tb[:, 12, :]
            tt(t1, PR[:, j, :], AR[:, 0, :], ALU.mult)
            tt(t2, PI_[:, j, :], AI[:, 0, :], ALU.mult)
            tt(PR[:, j + 1, :], t1, t2, ALU.subtract)
            tt(t1, PR[:, j, :], AI[:, 0, :], ALU.mult)
            tt(t2, PI_[:, j, :], AR[:, 0, :], ALU.mult)
            tt(PI_[:, j + 1, :], t1, t2, ALU.add)
        ts(A8R[:, 0, :], AR[:, 3, :], 1.0, ALU.mult)
        ts(A8I[:, 0, :], AI[:, 3, :], 1.0, ALU.mult)
        for s_ in range(1, 8):
            ts(A8R[:, s_, :], AR[:, 3 + s_, :], 1.0, ALU.mult)
            ts(A8I[:, s_, :], AI[:, 3 + s_, :], 1.0, ALU.mult)
        ts(NA8I[:], A8I[:], -1.0, ALU.mult)
        NPI = c.sb([128, 9, 16], F32, "NPI")
        ts(NPI[:], PI_[:], -1.0, ALU.mult)
        with c.scope():
            bt = [c.sb([128, 16, 128], F32, "bt_re"), c.sb([128, 16, 128], F32, "bt_im")]
            w1t = c.sb([128, 16, 128], F32, "w1t")
            w2t = c.sb([128, 16, 128], F32, "w2t")
            sre = c.sb([128, 16, 128], F32, "sre")
            sim = c.sb([128, 16, 128], F32, "sim")
            for i in range(2):
                c.dma("sp", bt[i][:], bt_d[i], writes=[("bt", i)], key=f"k_bt{i}")
            Frb = Fr.unsqueeze(2).broadcast_to([128, 16, 128])
            Fib = Fi.unsqueeze(2).broadcast_to([128, 16, 128])

            def t3(out, a, b, o, rk, wk, e="dve"):
                op(e, lambda h: h.tensor_tensor(out=out, in0=a, in1=b, op=o), reads=rk + [TK], writes=wk)

            t3(w1t[:], bt[0][:], Frb, ALU.mult, [("bt", 0)], ["w1t"])
            t3(w2t[:], bt[1][:], Fib, ALU.mult, [("bt", 1)], ["w2t"])
            t3(bb[0][:], w1t[:], w2t[:], ALU.subtract, ["w1t", "w2t"], [("bb", 0)])
            t3(w1t[:], bt[1][:], Frb, ALU.mult, [("bt", 1)], ["w1t"])
            t3(w2t[:], bt[0][:], Fib, ALU.mult, [("bt", 0)], ["w2t"])
            t3(bb[1][:], w1t[:], w2t[:], ALU.add, ["w1t", "w2t"], [("bb", 1)])
            for tau in range(8):
                prb = PR[:, tau, :].unsqueeze(2).broadcast_to([128, 16, 128])
                pib = PI_[:, tau, :].unsqueeze(2).broadcast_to([128, 16, 128])
                t3(w1t[:], bb[0][:], prb, ALU.mult, [("bb", 0)], ["w1t"])
                t3(w2t[:], bb[1][:], pib, ALU.mult, [("bb", 1)], ["w2t"])
                t3(sre[:], w1t[:], w2t[:], ALU.subtract, ["w1t", "w2t"], ["sre"])
                t3(w1t[:], bb[0][:], pib, ALU.mult, [("bb", 0)], ["w1t"])
                t3(w2t[:], bb[1][:], prb, ALU.mult, [("bb", 1)], ["w2t"])
                t3(sim[:], w1t[:], w2t[:], ALU.add, ["w1t", "w2t"], ["sim"])
                op("dve", lambda h: h.tensor_scalar(out=sim[:], in0=sim[:], scalar1=-1.0, scalar2=None, op0=ALU.mult),
                   reads=["sim"], writes=["sim"])
                for q in range(4):
                    b_ = nb()
                    n_ = 0
                    for kk in range(4):
                        for (a_, c_, ak) in ((sre, cp[0], "sre"), (sim, cp[1], "sim")):
                            mm(b_, banks[b_][:, 0:128], a_[:, 4 * q + kk, :], c_[:, 4 * q + kk, :], n_ == 0, n_ == 7,
                               [ak, ("cp", 0), ("cp", 1)])
                            n_ += 1
                    op("act", lambda h, q=q, tau=tau, b_=b_: h.activation(out=Kblk[:, q, tau, :],
                                                                          in_=banks[b_][:, 0:128], func=AF.Copy),
                       writes=[BK(b_), "Kblk"])


        for sq in range(NSEQ):
            with c.scope():
                T0 = sq * L
                catA = c.sb([128, 4, L], BF16, "catA")

                def phaseA(part, outs):
                    with c.scope():
                        ncol = 512 if part == 0 else 1536
                        w_in = c.sb([128, 8, ncol], BF16, "w_in")
                        load_w(w_in, w_in_d, 8, ncol, f"k_w_in{part}", "w_in", cs0=(0 if part == 0 else 512))
                        XBt = [c.sb([128, 8, 512], BF16, "XBt0"), c.sb([128, 8, 512], BF16, "XBt1")]
                        xt = [c.sb([128, D], F32, f"xt{i}") for i in range(4)]
                        if part == 0:
                            sst = [c.sb([128, 8, 128], F32, f"sstA{i}") for i in range(4)]
                        for tt_ in range(4):
                            XB = XBt[tt_ % 2]
                            xk = ("XB", tt_ % 2)
                            for bl in range(4):
                                tbk = tt_ * 4 + bl
                                sl = tbk % 4
                                tok = T0 + tbk * 128
                                c.dma("sp", xt[sl][:], x_d[tok:tok + 128, :], writes=[("xt", sl)], key=f"k_xt{sl}")
                                b0, b1 = nb(), nb()
                                for fc in range(8):
                                    bq_ = b0 if fc < 4 else b1
                                    op("pe", lambda h, fc=fc, bq_=bq_, sl=sl: h.transpose(
                                        out=banks[bq_][:, (fc % 4) * 128:(fc % 4) * 128 + 128],
                                        in_=xt[sl][:, fc * 128:(fc + 1) * 128], identity=ident),
                                        reads=[("xt", sl), "cst"], writes=[BK(bq_)])
                                for fc in range(8):
                                    bq_ = b0 if fc < 4 else b1
                                    src = banks[bq_][:, (fc % 4) * 128:(fc % 4) * 128 + 128]
                                    if part == 0:
                                        op("act", lambda h, fc=fc, src=src, sl=sl: h.activation(
                                            out=sst[sl][:, fc, :], in_=src, func=AF.Identity,
                                            bias=V("b_out", fc, 1), scale=ALPHA),
                                            reads=["vec"], writes=[BK(bq_), ("sstA", sl)])
                                    op("dve", lambda h, fc=fc, src=src, bl=bl: h.tensor_copy(
                                        out=XB[:, fc, bl * 128:(bl + 1) * 128], in_=src),
                                        writes=[BK(bq_), xk])
                                if part == 0:
                                    c.dma("pool", S_v[:, :, tok:tok + 128], sst[sl][:], reads=[("sstA", sl)],
                                          writes=[("S_d", tok)], key=f"k_sstA{sl}")
                            cs = slice(tt_ * 512, (tt_ + 1) * 512)
                            for oc in range(4 if part == 0 else 8):
                                b_ = nb()
                                for kc in range(8):
                                    mm(b_, banks[b_][:, :], w_in[:, kc, oc * 128:(oc + 1) * 128], XB[:, kc, :],
                                       kc == 0, kc == 7, wreads("w_in", kc, ncol) + [xk])
                                if part == 0:
                                    u_f = outs[0]
                                    op("act", lambda h, oc=oc, b_=b_, cs=cs: h.activation(
                                        out=u_f[:, oc, cs], in_=banks[b_][:, :], func=AF.Identity,
                                        bias=V("b_in", oc, 1), scale=1.0), reads=["vec"],
                                        writes=[BK(b_), ("u_f", tt_)])
                                elif oc < 4:
                                    qT = outs[0]
                                    op("act", lambda h, oc=oc, b_=b_, cs=cs: h.activation(
                                        out=qT[:, oc, cs], in_=banks[b_][:, :], func=AF.Identity,
                                        bias=bq8[:, oc:oc + 1], scale=0.125), reads=["der"],
                                        writes=[BK(b_), ("qT", tt_)])
                                else:
                                    kT, nkT = outs[1], outs[2]
                                    op("act", lambda h, oc=oc, b_=b_, cs=cs: h.activation(
                                        out=kT[:, oc - 4, cs], in_=banks[b_][:, :], func=AF.Identity,
                                        bias=V("b_in", 4 + oc, 1), scale=1.0), reads=["vec"],
                                        writes=[BK(b_), ("kT", tt_)])
                                    op("dve", lambda h, oc=oc, cs=cs: h.tensor_scalar(
                                        out=nkT[:, oc - 4, cs], in0=kT[:, oc - 4, cs], scalar1=-1.0, scalar2=None,
                                        op0=ALU.mult), reads=[("kT", tt_)], writes=[("nkT", tt_)])
                            if part == 1:
                                Vt = outs[3]
                                for bl in range(4):
                                    tbk = tt_ * 4 + bl
                                    b_ = nb()
                                    for kc in range(8):
                                        mm(b_, banks[b_][:, :], XB[:, kc, bl * 128:(bl + 1) * 128],
                                           w_in[:, kc, 1024:1536], kc == 0, kc == 7,
                                           wreads("w_in", kc, ncol) + [xk])
                                    op("dve", lambda h, tbk=tbk, b_=b_: h.tensor_tensor(
                                        out=Vt[:, tbk, :], in0=banks[b_][:, :], in1=bv[:], op=ALU.add),
                                        reads=["bv"], writes=[BK(b_), ("Vt", tbk)])

                with c.scope():
                    u_f = c.sb([128, 4, L], BF16, "u_f")
                    phaseA(0, [u_f])
                    w_glu = c.sb([128, 4, 1024], BF16, "w_glu")
                    load_w(w_glu, w_glu_d, 4, 1024, "k_w_glu", "w_glu")
                    YB = (0, 1, 2, 3)
                    WB = (4, 5, 6, 7)
                    XAs = [[c.sb([128, 2, 256], F32, f"XA{a}{b}") for b in range(2)] for a in range(2)]
                    Xc = [c.sb([128, 2, 256], BF16, f"Xc{i}") for i in range(2)]
                    Hf = [c.sb([128, 8, 2, 128], F32, f"Hf{i}") for i in range(2)]
                    Hp = [c.sb([128, 8, 2, 128], BF16, f"Hp{i}") for i in range(2)]
                    Gp = [c.sb([128, 8, 2, 128], BF16, f"Gp{i}") for i in range(4)]
                    tmpg = c.sb([128, 128], F32, "tmpg")
                    zfull = c.sb([128, L], F32, "zfull")
                    gfull = c.sb([128, L], F32, "gfull")
                    yg = c.sb([128, 4, L], BF16, "yg")
                    sg_ = [c.sb([128, 512], F32, "sg0"), c.sb([128, 512], F32, "sg1")]
                    tmpA = c.sb([128, 8, 128], F32, "tmpA")
                    tmpB = c.sb([128, 8, 128], F32, "tmpB")

                    def uq_of(q):
                        return u_f[:, q, :].rearrange("p (c i) -> p i c", i=8)

                    UK = [("u_f", t_) for t_ in range(4)]

                    def tab_ops(p):
                        k, sl = p, p % 2
                        th = []
                        rk = [TK, ("bb", 0), ("bb", 1), ("cp", 0), ("cp", 1)]
                        if sq == 1:
                            th.append(lambda: c.dma("sp", Hp[sl][:].rearrange("p a b n -> p (a b n)"), HG_d[p, 0],
                                                    reads=[("HG_d", p, 0)],
                                                    writes=[("Hp", sl, jj) for jj in range(8)], key=f"k_hpl{sl}"))
                            th.append(lambda: c.dma("sp", Gp[p % 4][:].rearrange("p a b n -> p (a b n)"), HG_d[p, 1],
                                                    reads=[("HG_d", p, 1)],
                                                    writes=[(("Gp", p % 4), i, r) for i in range(8) for r in range(2)],
                                                    key=f"k_gpl{p % 4}"))
                            return th

                        def add(fn, reads, writes):
                            th.append(lambda: op("dve", fn, reads=reads, writes=writes))

                        for (o, a_re, a_im, j0, wk, neg) in ((Hf[sl], bb[0][:, k, :], bb[1][:, k, :], 0, ("Hf", sl), False),
                                                             (Gp[p % 4], cp[0][:, k, :], cp[1][:, k, :], 1, ("Gp", p % 4),
                                                              True)):
                            tA = ("tmpA", neg)
                            tB = ("tmpB", neg)
                            ta = tmpA if not neg else tmpB
                            for j in range(8):
                                si = PI_[:, j0 + j, k:k + 1]
                                add(lambda h, j=j, si=si, ta=ta, a_im=a_im: h.tensor_scalar(
                                    out=ta[:, j, :], in0=a_im, scalar1=si, scalar2=None, op0=ALU.mult), rk, [(tA, j)])
                            for j in range(8):
                                sr = PR[:, j0 + j, k:k + 1]
                                add(lambda h, j=j, sr=sr, ta=ta, a_re=a_re, o=o: h.scalar_tensor_tensor(
                                    out=o[:, j, 0, :], in0=a_re, scalar=sr, in1=ta[:, j, :], op0=ALU.mult,
                                    op1=ALU.subtract), rk + [(tA, j)], [(wk, j, 0)])
                            for j in range(8):
                                sr = PR[:, j0 + j, k:k + 1]
                                if neg:
                                    add(lambda h, j=j, sr=sr, ta=ta, a_im=a_im: h.tensor_scalar(
                                        out=ta[:, j, :], in0=a_im, scalar1=sr, scalar2=-1.0, op0=ALU.mult, op1=ALU.mult),
                                        rk, [(tA, j)])
                                else:
                                    add(lambda h, j=j, sr=sr, ta=ta, a_im=a_im: h.tensor_scalar(
                                        out=ta[:, j, :], in0=a_im, scalar1=sr, scalar2=None, op0=ALU.mult), rk, [(tA, j)])
                            for j in range(8):
                                si = (NPI if neg else PI_)[:, j0 + j, k:k + 1]
                                add(lambda h, j=j, si=si, ta=ta, a_re=a_re, o=o: h.scalar_tensor_tensor(
                                    out=o[:, j, 1, :], in0=a_re, scalar=si, in1=ta[:, j, :], op0=ALU.mult, op1=ALU.add),
                                    rk + [(tA, j)], [(wk, j, 1)])
                        return th

                    def emit_transposes_and_V(p):
                        k, sl, q = p, p % 2, p // 4
                        uq = uq_of(q)
                        for jj in range(8 if sq == 0 else 0):
                            b_ = nb(WB)
                            for r in range(2):
                                op("pe", lambda h, jj=jj, r=r, b_=b_: h.transpose(
                                    out=banks[b_][:, r * 128:(r + 1) * 128], in_=Hf[sl][:, jj, r, :], identity=ident),
                                    reads=[(("Hf", sl), jj, r), "cst"], writes=[BK(b_)])
                            op("act", lambda h, jj=jj, b_=b_: h.activation(
                                out=Hp[sl][:, jj, :, :], in_=banks[b_][:, 0:256].rearrange("p (r n) -> p r n", r=2),
                                func=AF.Copy), writes=[BK(b_), ("Hp", sl, jj)])
                        if sq == 0:
                            c.dma("pool", HG_d[p, 0], Hp[sl][:].rearrange("p a b n -> p (a b n)"),
                                  reads=[("Hp", sl, jj) for jj in range(8)], writes=[("HG_d", p, 0)], key=f"k_hps{sl}")
                            c.dma("pool", HG_d[p, 1], Gp[p % 4][:].rearrange("p a b n -> p (a b n)"),
                                  reads=[(("Gp", p % 4), i, r) for i in range(8) for r in range(2)],
                                  writes=[("HG_d", p, 1)], key=f"k_gps{p % 4}")
                        b_ = nb(WB)
                        for r in range(2):
                            for j in range(8):
                                mm(b_, banks[b_][:, r * 256:(r + 1) * 256], Hp[sl][:, 7 - j, r, :], uq[:, j, :],
                                   j == 0 and r == 0, j == 7 and r == 1, [("Hp", sl, 7 - j)] + UK)
                        op("act", lambda h, b_=b_: h.activation(
                            out=XAs[sl][0][:], in_=banks[b_][:, :].rearrange("p (r n) -> p r n", r=2), func=AF.Copy),
                            writes=[BK(b_), ("XA", sl, 0)])

                    def ks_ops(p):
                        k, sl = p, p % 2
                        th = []
                        for s in range(8):
                            d = 1 << s
                            sa, da = s % 2, 1 - (s % 2)
                            src, dst = XAs[sl][sa], XAs[sl][da]
                            last = (s == 7)
                            o = Xc[sl] if last else dst
                            ok = ("Xc", sl) if last else ("XA", sl, da)
                            kr = [("XA", sl, sa)]

                            def stt(out, in0, sc, in1, rk, wk):
                                th.append(lambda: op("dve", lambda h: h.scalar_tensor_tensor(
                                    out=out, in0=in0, scalar=sc, in1=in1, op0=ALU.mult, op1=ALU.add),
                                    reads=rk + [TK], writes=wk))
                            th.append(lambda o=o, src=src, d=d, kr=kr, ok=ok: op(
                                "pool", lambda h: h.tensor_copy(out=o[:, :, 0:d], in_=src[:, :, 0:d]), reads=kr,
                                writes=[ok]))
                            stt(dst[:, 0, d:], src[:, 0, 0:256 - d], A8R[:, s, k:k + 1], src[:, 0, d:], kr, [("XA", sl, da)])
                            stt(dst[:, 1, d:], src[:, 0, 0:256 - d], A8I[:, s, k:k + 1], src[:, 1, d:], kr, [("XA", sl, da)])
                            stt(o[:, 0, d:], src[:, 1, 0:256 - d], NA8I[:, s, k:k + 1], dst[:, 0, d:],
                                kr + [("XA", sl, da)], [ok])
                            stt(o[:, 1, d:], src[:, 1, 0:256 - d], A8R[:, s, k:k + 1], dst[:, 1, d:],
                                kr + [("XA", sl, da)], [ok])
                        return th

                    def toeplitz(q):
                        uq = uq_of(q)
                        for b in range(4):
                            first = True
                            for i in (2 * b, 2 * b + 1):
                                for tau in range(i + 1):
                                    mm(YB[b], banks[YB[b]][:, (i % 2) * 256:(i % 2) * 256 + 256], Kblk[:, q, tau, :],
                                       uq[:, i - tau, :], first, False, ["Kblk"] + UK)
                                    first = False

                    def farfield(p):
                        sl, kk = p % 2, p % 4
                        for i in range(8):
                            for r in range(2):
                                mm(YB[i // 2], banks[YB[i // 2]][:, (i % 2) * 256 + 1:(i % 2) * 256 + 256],
                                   Gp[p % 4][:, i, r, :], Xc[sl][:, r, 0:255], False, (kk == 3 and i % 2 == 1 and r == 1),
                                   [(("Gp", p % 4), i, r), ("Xc", sl)])

                    def zgelu(q):
                        uq = uq_of(q)
                        zv = zfull[:, :].rearrange("p (c i) -> p i c", i=8)
                        for b in range(4):
                            op("dve", lambda h, b=b: h.scalar_tensor_tensor(
                                out=zv[:, 2 * b:2 * b + 2, :], in0=uq[:, 2 * b:2 * b + 2, :], scalar=V("s5_d", q, 1),
                                in1=banks[YB[b]][:, :].rearrange("p (i c) -> p i c", i=2), op0=ALU.mult, op1=ALU.add),
                                reads=UK + ["vec"], writes=[BK(YB[b]), "zfull"])
                        op("act", lambda h: h.activation(out=gfull[:], in_=zfull[:], func=AF.Square),
                           reads=["zfull"], writes=["gfull"])
                        op("dve", lambda h: h.tensor_scalar(out=gfull[:], in0=gfull[:], scalar1=0.044715, scalar2=1.0,
                                                            op0=ALU.mult, op1=ALU.add), reads=["gfull"], writes=["gfull"])
                        op("dve", lambda h: h.tensor_tensor(out=gfull[:], in0=gfull[:], in1=zfull[:], op=ALU.mult),
                           reads=["gfull", "zfull"], writes=["gfull"])
                        op("act", lambda h: h.activation(out=gfull[:], in_=gfull[:], func=AF.Sigmoid,
                                                         scale=1.5957691216057308), reads=["gfull"], writes=["gfull"])
                        op("dve", lambda h: h.tensor_tensor(out=yg[:, q, :], in0=gfull[:], in1=zfull[:], op=ALU.mult),
                           reads=["gfull", "zfull"], writes=[("yg", q, t_) for t_ in range(4)])

                    def interleave(*lists):
                        idx = [0] * len(lists)
                        more = True
                        while more:
                            more = False
                            for li, l_ in enumerate(lists):
                                if idx[li] < len(l_):
                                    l_[idx[li]]()
                                    idx[li] += 1
                                    more = True

                    interleave(tab_ops(0) + tab_ops(1))
                    for pp in range(8):
                        p0, p1 = 2 * pp, 2 * pp + 1
                        emit_transposes_and_V(p0)
                        emit_transposes_and_V(p1)
                        if p0 % 4 == 0:
                            toeplitz(p0 // 4)
                        nxt = (tab_ops(p0 + 2) + tab_ops(p1 + 2)) if p0 + 2 < 16 else []
                        interleave(ks_ops(p0), ks_ops(p1), nxt[0::2], nxt[1::2])
                        farfield(p0)
                        farfield(p1)
                        if p1 % 4 == 3:
                            zgelu(p1 // 4)
                    for tt_ in range(4):
                        cs = slice(tt_ * 512, (tt_ + 1) * 512)
                        for oc in range(4):
                            bv_, bg_ = nb(), nb()
                            sl = oc % 2
                            for kc in range(4):
                                mm(bv_, banks[bv_][:, :], w_glu[:, kc, oc * 128:(oc + 1) * 128], yg[:, kc, cs],
                                   kc == 0, kc == 3, wreads("w_glu", kc, 1024) + [("yg", kc, tt_)])
                            for kc in range(4):
                                mm(bg_, banks[bg_][:, :], w_glu[:, kc, 512 + oc * 128:512 + (oc + 1) * 128],
                                   yg[:, kc, cs], kc == 0, kc == 3,
                                   wreads("w_glu", kc, 1024) + [("yg", kc, tt_)])
                            op("act", lambda h, oc=oc, bg_=bg_, sl=sl: h.activation(
                                out=sg_[sl][:], in_=banks[bg_][:, :], func=AF.Sigmoid,
                                bias=V("b_glu", 4 + oc, 1), scale=1.0), reads=["vec"],
                                writes=[BK(bg_), ("sg", sl)])
                            op("dve", lambda h, oc=oc, bv_=bv_, sl=sl, cs=cs: h.scalar_tensor_tensor(
                                out=catA[:, oc, cs], in0=banks[bv_][:, :], scalar=V("b_glu", oc, 1), in1=sg_[sl][:],
                                op0=ALU.add, op1=ALU.mult), reads=["vec", ("sg", sl)],
                                writes=[BK(bv_), ("catA", oc, tt_)])

                catB = c.sb([128, 4, L], BF16, "catB")
                with c.scope():
                    qT = c.sb([128, 4, L], BF16, "qT")
                    kT = c.sb([128, 4, L], BF16, "kT")
                    nkT = c.sb([128, 4, L], BF16, "nkT")
                    Vt = c.sb([128, 16, 512], BF16, "Vt")
                    phaseA(1, [qT, kT, nkT, Vt])
                    ZB = (0, 1, 2)
                    BB = (3, 4, 5)
                    OB2 = (6, 7)
                    NBUF = 4
                    FP16 = mybir.dt.float16
                    ebuf = [c.sb([128, 512], F32, f"ebuf{i}") for i in range(NBUF)]
                    spb = [c.sb([128, 512], BF16, f"spb{i}") for i in range(NBUF)]
                    Pb = [c.sb([128, 512], BF16, f"Pb{i}") for i in range(NBUF)]
                    zrow = [c.sb([1, 512], F32, f"zrow{i}") for i in range(NBUF)]
                    A16 = [c.sb([1, 512], FP16, f"A16_{i}") for i in range(2)]
                    onesr = c.sb([1, 128], BF16, "onesr")
                    op("dve", lambda h: h.memset(onesr[:], 1.0), writes=["onesr"])
                    items = []
                    for m_ in range(4):
                        for qt in range(4):
                            for kb in range(4 * qt + 3, -1, -1):
                                for st in range(2):
                                    items.append((2 * m_ + st, qt, kb, st))

                    def geo(i):
                        hd, qt, kb, st = items[i]
                        r = max(0, kb - 4 * qt)
                        return dict(hd=hd, qt=qt, kb=kb, st=st, ch=hd // 2, pb=(hd % 2) * 64, r=r, diag=kb >= 4 * qt,
                                    c0=r * 128, qs=slice(qt * 512 + r * 128, (qt + 1) * 512),
                                    ks=slice(kb * 128, (kb + 1) * 128), sl=i % NBUF,
                                    zb=ZB[i % 3], bb=BB[i % 3], ob=OB2[st],
                                    first=(kb == 4 * qt + 3), last=(kb == 0))

                    def stageA(ii):
                        gs = [geo(i) for i in ii]
                        for g in gs:
                            mm(g["zb"], banks[g["zb"]][:, g["c0"]:], kT[g["pb"]:g["pb"] + 64, g["ch"], g["ks"]],
                               qT[g["pb"]:g["pb"] + 64, g["ch"], g["qs"]], True, not g["diag"],
                               [("kT", g["kb"] // 4), ("qT", g["qt"])])
                        for g in gs:
                            if g["diag"]:
                                mm(g["zb"], banks[g["zb"]][:, g["c0"]:g["c0"] + 128], identb, nmaskb, False, True, ["cstb"])
                        for g in gs:
                            zb_, sl, c0 = g["zb"], g["sl"], g["c0"]
                            op("act", lambda h: h.activation(out=ebuf[sl][:, c0:], in_=banks[zb_][:, c0:], func=AF.Exp),
                               writes=[BK(zb_), ("ebuf", sl)])
                            if not g["last"]:
                                op("dve", lambda h: h.tensor_copy(out=zrow[sl][0:1, c0:], in_=banks[zb_][0:1, c0:]),
                                   writes=[BK(zb_), ("zrow", sl)])
                            op("act", lambda h: h.activation(out=spb[sl][:, c0:], in_=ebuf[sl][:, c0:], func=AF.Ln,
                                                             bias=1.0, scale=1.0),
                               reads=[("ebuf", sl)], writes=[("spb", sl)])

                    def stageB(ii):
                        gs = [geo(i) for i in ii]
                        for g in gs:
                            mm(g["bb"], banks[g["bb"]][:, g["c0"]:], trib, spb[g["sl"]][:, g["c0"]:], True, False,
                               ["cstb", ("spb", g["sl"])])
                        for g in gs:
                            mm(g["bb"], banks[g["bb"]][:, g["c0"]:], nkT[g["pb"]:g["pb"] + 64, g["ch"], g["ks"]],
                               qT[g["pb"]:g["pb"] + 64, g["ch"], g["qs"]], False, False,
                               [("nkT", g["kb"] // 4), ("qT", g["qt"])])
                        for g in gs:
                            if g["diag"]:
                                mm(g["bb"], banks[g["bb"]][:, g["c0"]:g["c0"] + 128], identb, pmaskb, False, g["first"],
                                   ["cstb"])
                        for g in gs:
                            if not g["first"]:
                                mm(g["bb"], banks[g["bb"]][:, g["c0"]:], onesr[:], A16[g["st"]][0:1, g["c0"]:], False,
                                   True, ["onesr", ("A16", g["st"])])
                        for g in gs:
                            bb_, sl, c0, st = g["bb"], g["sl"], g["c0"], g["st"]
                            op("act", lambda h: h.activation(out=Pb[sl][:, c0:], in_=banks[bb_][:, c0:], func=AF.Exp,
                                                             scale=-1.0),
                               writes=[BK(bb_), ("Pb", sl)])
                            if g["first"]:
                                op("dve", lambda h: h.memset(A16[st][:], 0.0), writes=[("A16", st)])
                            if not g["last"]:
                                op("dve", lambda h: h.tensor_tensor(out=A16[st][0:1, c0:], in0=banks[bb_][0:1, c0:],
                                                                    in1=zrow[sl][0:1, c0:], op=ALU.add),
                                   reads=[("zrow", sl)], writes=[BK(bb_), ("A16", st)])

                    def stageC(ii):
                        gs = [geo(i) for i in ii]
                        for g in gs:
                            ob, sl, c0, pb, hd, kb = g["ob"], g["sl"], g["c0"], g["pb"], g["hd"], g["kb"]
                            mm(ob, banks[ob][pb:pb + 64, c0:], Vt[:, kb, hd * 64:(hd + 1) * 64], Pb[sl][:, c0:],
                               g["first"], g["last"], [("Pb", sl), ("Vt", kb)], skip_group_check=True)
                        for g in gs:
                            if g["last"]:
                                ob, pb, ch, qt = g["ob"], g["pb"], g["ch"], g["qt"]
                                op("dve", lambda h: h.tensor_copy(out=catB[pb:pb + 64, ch, qt * 512:(qt + 1) * 512],
                                                                  in_=banks[ob][pb:pb + 64, :]),
                                   writes=[BK(ob), ("catB", ch, qt, pb)])

                    n_st = len(items) // 2
                    for step in range(n_st + 2):
                        if step < n_st:
                            stageA((2 * step, 2 * step + 1))
                        if 0 <= step - 1 < n_st:
                            stageB((2 * (step - 1), 2 * (step - 1) + 1))
                        if 0 <= step - 2 < n_st:
                            stageC((2 * (step - 2), 2 * (step - 2) + 1))

                with c.scope():
                    N = 512
                    if dbg:
                        c.dma("sp", dC_v[:, 0:4, T0:T0 + L], catA[:], writes=["dC0"], key="k_dbg")
                        c.dma("sp", dC_v[:, 4:8, T0:T0 + L], catB[:], writes=["dC1"], key="k_dbg")
                    w_out = c.sb([128, 8, 1024], BF16, "w_out")
                    load_w(w_out, w_out_d, 8, 1024, "k_w_out", "w_out")
                    lnb = ln_scratch(N)
                    ztD = [c.sb([128, 8, N], F32, f"ztD{i}") for i in range(2)]
                    sin_t = c.sb([128, 8, N], F32, "sinD")
                    xbt = c.sb([128, 8, N], BF16, "xbtD")

                    def d_mm(tt_):
                        tok0 = T0 + tt_ * N
                        cs = slice(tt_ * N, (tt_ + 1) * N)
                        z = ztD[tt_ % 2]
                        zk = ("ztD", tt_ % 2)
                        c.dma("sp", sin_t[:], S_v[:, :, tok0:tok0 + N], writes=["sinD"], key="k_sinD")
                        for oc in range(8):
                            b_ = nb()
                            for kc in range(8):
                                src = catA[:, kc, cs] if kc < 4 else catB[:, kc - 4, cs]
                                mm(b_, banks[b_][:, :], w_out[:, kc, oc * 128:(oc + 1) * 128], src,
                                   kc == 0, kc == 7, wreads("w_out", kc, 1024))
                            op("dve", lambda h: h.tensor_tensor(out=z[:, oc, :], in0=banks[b_][:, :],
                                                                in1=sin_t[:, oc, :], op=ALU.add),
                               reads=["sinD"], writes=[BK(b_), zk])

                    def d_ln(tt_):
                        tok0 = T0 + tt_ * N
                        z = ztD[tt_ % 2]
                        zk = ("ztD", tt_ % 2)
                        layer_norm(z, N, zk, z, zk, lnb)
                        ln_epilogue_stream(z, zk, N, tok0, ga10, ba10, "ln1_g0", "ln1_b0", z, xbt, 0, sstkey=zk)

                    for tt_ in range(4):
                        d_mm(tt_)
                        if tt_ > 0:
                            d_ln(tt_ - 1)
                    d_ln(3)


    def dbg_stop(tag):
        if dbg and dbg.get("stop") == tag:
            c.barrier()
            c.dma("sp", dS_d, S_d, writes=["dS"], key="k_dbg")
            c.dma("sp", dX_d, XB_d, writes=["dX"], key="k_dbg")
            c.barrier()
            c.emit()
            c.close()
            return True
        return False

    if dbg_stop("D"):
        return nc
    H2_d = nc.dram_tensor("H2_scr", [D, T], BF16, kind="Internal").ap()
    H2_v = H2_d.rearrange("(c p) t -> p c t", p=128)

    def ffn_phase(layer, final, ga, ba, gname, bname):
        N = 512
        NTL = T // N
        NQ = 4
        b1n = f"fb1_{layer}"
        with c.scope():
            w1s = [c.sb([128, 8, 1024], BF16, f"w1s{i}") for i in range(2)]
            w2s = [c.sb([128, 8, 1024], BF16, f"w2s{i}") for i in range(2)]
            zt = [c.sb([128, 8, N], F32, f"ztE{i}") for i in range(2)]
            xin = [c.sb([128, 8, N], BF16, f"xinE{i}") for i in range(2)]
            sin2 = [c.sb([128, 8, N], F32, f"sinE{i}") for i in range(2)]
            hb = c.sb([128, 8, N], BF16, "hb")
            rl = [c.sb([128, N], F32, f"rl{i}") for i in range(2)]
            lnb = ln_scratch(N)
            xbt = c.sb([128, 8, N], BF16, "xbtE")
            yo = c.sb([128, D], F32, "yo")

            def load_q(qr):
                sl = qr % 2
                load_w(w1s[sl], w1_d[layer], 8, 1024, f"k_w1s{sl}", ("w1s", sl), cs0=qr * 1024)
                load_w(w2s[sl], w2_d[layer][qr * 1024:(qr + 1) * 1024, :], 8, 1024, f"k_w2s{sl}", ("w2s", sl))

            def part_h(qr, tt_):
                sl = tt_ % 2
                ws = qr % 2
                tok0 = tt_ * N
                for hc in range(8):
                    b_ = nb()
                    s2 = hc % 2
                    hcg = qr * 8 + hc
                    for kc in range(8):
                        mm(b_, banks[b_][:, :], w1s[ws][:, kc, hc * 128:(hc + 1) * 128], xin[sl][:, kc, :],
                           kc == 0, kc == 7, wreads(("w1s", ws), kc, 1024) + [("xinE", sl)])
                    op("act", lambda h: h.activation(out=rl[s2][:], in_=banks[b_][:, :], func=AF.Relu,
                                                     bias=V(b1n, hcg, 1), scale=1.0),
                       reads=["vec"], writes=[BK(b_), ("rl", s2)])
                    op("dve", lambda h: h.scalar_tensor_tensor(
                        out=hb[:, hc, :], in0=banks[b_][:, :], scalar=V(b1n, hcg, 1), in1=rl[s2][:],
                        op0=ALU.add, op1=ALU.mult), reads=["vec", ("rl", s2)], writes=[BK(b_), ("hb", hc)])

            def part_out(qr, tt_, mode="store"):
                ws = qr % 2
                tok0 = tt_ * N
                z = zt[tt_ % 2]
                zk = ("ztE", tt_ % 2)
                sin_t = sin2[tt_ % 2]
                sk_ = ("sinE", tt_ % 2)
                for oc in range(8):
                    b_ = nb()
                    for hc in range(8):
                        mm(b_, banks[b_][:, :], w2s[ws][:, hc, oc * 128:(oc + 1) * 128], hb[:, hc, :],
                           hc == 0, hc == 7, wreads(("w2s", ws), hc, 1024) + [("hb", hc)])
                    if mode == "acc":
                        op("dve", lambda h: h.tensor_tensor(out=z[:, oc, :], in0=banks[b_][:, :], in1=z[:, oc, :],
                                                            op=ALU.add), reads=[], writes=[BK(b_), zk])
                    else:
                        op("dve", lambda h: h.tensor_tensor(out=z[:, oc, :], in0=banks[b_][:, :], in1=sin_t[:, oc, :],
                                                            op=ALU.add), reads=[sk_], writes=[BK(b_), zk])
                if mode == "store":
                    c.dma("pool", S_v[:, :, tok0:tok0 + N], z[:], reads=[zk], writes=[("S_d", tok0)],
                          key=f"k_zpart{tt_ % 2}")

            def loads(qr, tt_):
                sl = tt_ % 2
                tok0 = tt_ * N
                c.dma("sp", xin[sl][:], XB_v[:, :, tok0:tok0 + N], writes=[("xinE", sl)], key=f"k_xinE{sl}")
                c.dma("sp", sin2[sl][:], S_v[:, :, tok0:tok0 + N], reads=[("S_d", tok0)], writes=[("sinE", sl)],
                      key=f"k_sinE{sl}")

            def part_ln(tt_):
                tok0 = tt_ * N
                z = zt[tt_ % 2]
                zk = ("ztE", tt_ % 2)
                layer_norm(z, N, zk, z, zk, lnb)
                if not final:
                    ln_epilogue_stream(z, zk, N, tok0, ga, ba, gname, bname, z, xbt, 1, sstkey=zk)
                else:
                    for kc in range(8):
                        op("act", lambda h, kc=kc: h.activation(out=z[:, kc, :], in_=z[:, kc, :], func=AF.Identity,
                                                                bias=V(bname, kc, 1), scale=V(gname, kc, 1)),
                           reads=["vec"], writes=[zk])
                    for bl in range(N // 128):
                        b0, b1 = nb(), nb()
                        for fc in range(8):
                            bq_ = b0 if fc < 4 else b1
                            op("pe", lambda h, fc=fc, bq_=bq_: h.transpose(
                                out=banks[bq_][:, (fc % 4) * 128:(fc % 4) * 128 + 128],
                                in_=z[:, fc, bl * 128:(bl + 1) * 128], identity=ident),
                                reads=[zk, "cst"], writes=[BK(bq_)])
                        op("act", lambda h: h.activation(out=yo[:, 0:512], in_=banks[b0][:, :], func=AF.Copy),
                           writes=[BK(b0), "yo"])
                        op("dve", lambda h: h.tensor_copy(out=yo[:, 512:1024], in_=banks[b1][:, :]),
                           writes=[BK(b1), "yo"])
                        c.dma("sp", y_d[tok0 + bl * 128:tok0 + (bl + 1) * 128, :], yo[:],
                              reads=["yo"], writes=[("y", tok0, bl)], key="k_yo")

            load_q(0)
            load_q(1)
            for qr in range(2):
                loads(qr, 0)
                for tt_ in range(NTL):
                    if tt_ + 1 < NTL:
                        loads(qr, tt_ + 1)
                    part_h(qr, tt_)
                    part_out(qr, tt_, "store")
                    if qr == 0 and tt_ == NTL - 1:
                        load_q(2)
            load_q(3)
            loads(2, 0)
            for tt_ in range(NTL):
                if tt_ + 1 < NTL:
                    loads(2, tt_ + 1)
                part_h(2, tt_)
                part_out(2, tt_, "keep")
                part_h(3, tt_)
                if tt_ > 0:
                    part_ln(tt_ - 1)
                part_out(3, tt_, "acc")
            part_ln(NTL - 1)

    ffn_phase(0, False, ga20, ba20, "ln2_g0", "ln2_b0")
    if dbg_stop("E0"):
        return nc

    N = 512
    NTL = T // N
    with c.scope():
        pw1 = c.sb([128, 8, 2048], BF16, "pw1")
        load_w(pw1, pw1_d, 8, 2048, "k_pw1", "pw1")
        dg = c.sb([128, 8, 31, 128], BF16, "dg")
        for oc in range(8):
            for k in range(31):
                op("dve", lambda h, oc=oc, k=k: h.tensor_scalar(out=dg[:, oc, k, :], in0=ident,
                                                                scalar1=V("w_dw", k * 8 + oc, 1), scalar2=None,
                                                                op0=ALU.mult), reads=["cst", "vec"], writes=[("dg", oc)])
        lnb = ln_scratch(N)
        xin = [c.sb([128, 8, N], BF16, f"xinF{i}") for i in range(2)]
        hbuf = c.sb([128, 8, 30 + N], BF16, "hbuf")
        sgF = [c.sb([128, N], F32, f"sgF{i}") for i in range(2)]
        cvs = [c.sb([128, 8, N], F32, f"cv{i}") for i in range(2)]
        h2 = c.sb([128, 8, N], BF16, "h2")

        def f1_load(tt_):
            sl = tt_ % 2
            c.dma("sp", xin[sl][:], XB_v[:, :, tt_ * N:(tt_ + 1) * N], writes=[("xinF", sl)], key=f"k_xinF{sl}")

        def f1_glu(tt_):
            sl = tt_ % 2
            if tt_ % 4 == 0:
                op("dve", lambda h: h.memset(hbuf[:, :, 0:30], 0.0), writes=[("hbuf", i) for i in range(8)])
            else:
                for oc in range(8):
                    op("dve", lambda h, oc=oc: h.tensor_copy(out=hbuf[:, oc, 0:30], in_=hbuf[:, oc, N:N + 30]),
                       reads=[], writes=[("hbuf", oc)])
            for oc in range(8):
                bv_, bg_ = nb(), nb()
                s2 = oc % 2
                for kc in range(8):
                    mm(bv_, banks[bv_][:, :], pw1[:, kc, oc * 128:(oc + 1) * 128], xin[sl][:, kc, :],
                       kc == 0, kc == 7, wreads("pw1", kc, 2048) + [("xinF", sl)])
                for kc in range(8):
                    mm(bg_, banks[bg_][:, :], pw1[:, kc, 1024 + oc * 128:1024 + (oc + 1) * 128], xin[sl][:, kc, :],
                       kc == 0, kc == 7, wreads("pw1", kc, 2048) + [("xinF", sl)])
                op("act", lambda h: h.activation(out=sgF[s2][:], in_=banks[bg_][:, :], func=AF.Sigmoid,
                                                 bias=V("b_pw1", 8 + oc, 1), scale=1.0),
                   reads=["vec"], writes=[BK(bg_), ("sgF", s2)])
                op("dve", lambda h: h.scalar_tensor_tensor(
                    out=hbuf[:, oc, 30:30 + N], in0=banks[bv_][:, :], scalar=V("b_pw1", oc, 1), in1=sgF[s2][:],
                    op0=ALU.add, op1=ALU.mult), reads=["vec", ("sgF", s2)], writes=[BK(bv_), ("hbuf", oc)])

        def f1_conv(tt_):
            cv = cvs[tt_ % 2]
            for oc in range(8):
                b_ = nb()
                for k in range(31):
                    mm(b_, banks[b_][:, :], dg[:, oc, k, :], hbuf[:, oc, k:k + N], k == 0, k == 30,
                       [("dg", oc), ("hbuf", oc)])
                op("act", lambda h: h.activation(out=cv[:, oc, :], in_=banks[b_][:, :], func=AF.Identity,
                                                 bias=V("b_dw", oc, 1), scale=1.0),
                   reads=["vec"], writes=[BK(b_), ("cv", tt_ % 2)])

        def f1_ln(tt_):
            cv = cvs[tt_ % 2]
            ck = ("cv", tt_ % 2)
            layer_norm(cv, N, ck, cv, ck, lnb)
            for oc in range(8):
                op("act", lambda h, oc=oc: h.activation(out=h2[:, oc, :], in_=cv[:, oc, :], func=AF.Silu,
                                                        bias=V("cln_b", oc, 1), scale=V("cln_g", oc, 1)),
                   reads=[ck, "vec"], writes=["h2"])
            c.dma("pool", H2_v[:, :, tt_ * N:(tt_ + 1) * N], h2[:], reads=["h2"], writes=[("H2_d", tt_)], key="k_h2")

        f1_load(0)
        for tt_ in range(NTL):
            if tt_ + 1 < NTL:
                f1_load(tt_ + 1)
            f1_glu(tt_)
            if tt_ > 0:
                f1_ln(tt_ - 1)
            f1_conv(tt_)
        f1_ln(NTL - 1)
    with c.scope():
        pw2 = c.sb([128, 8, 1024], BF16, "pw2")
        load_w(pw2, pw2_d, 8, 1024, "k_pw2", "pw2")
        lnb = ln_scratch(N)
        h2i = [c.sb([128, 8, N], BF16, f"h2i{i}") for i in range(2)]
        sin2 = [c.sb([128, 8, N], F32, f"sinF{i}") for i in range(2)]
        zts = [c.sb([128, 8, N], F32, f"ztF{i}") for i in range(2)]
        xbt = c.sb([128, 8, N], BF16, "xbtF")

        def f2_load(tt_):
            sl = tt_ % 2
            tok0 = tt_ * N
            c.dma("sp", h2i[sl][:], H2_v[:, :, tok0:tok0 + N], writes=[("h2i", sl)], key=f"k_h2i{sl}")
            c.dma("sp", sin2[sl][:], S_v[:, :, tok0:tok0 + N], writes=[("sinF", sl)], key=f"k_sinF{sl}")

        def f2_mm(tt_):
            sl = tt_ % 2
            z = zts[sl]
            for oc in range(8):
                b_ = nb()
                for kc in range(8):
                    mm(b_, banks[b_][:, :], pw2[:, kc, oc * 128:(oc + 1) * 128], h2i[sl][:, kc, :], kc == 0, kc == 7,
                       wreads("pw2", kc, 1024) + [("h2i", sl)])
                op("dve", lambda h: h.tensor_tensor(out=z[:, oc, :], in0=banks[b_][:, :], in1=sin2[sl][:, oc, :],
                                                    op=ALU.add), reads=[("sinF", sl)], writes=[BK(b_), ("ztF", sl)])

        def f2_ln(tt_):
            sl = tt_ % 2
            z = zts[sl]
            zk = ("ztF", sl)
            layer_norm(z, N, zk, z, zk, lnb)
            ln_epilogue_stream(z, zk, N, tt_ * N, ga11, ba11, "ln1_g1", "ln1_b1", z, xbt, 2, sstkey=zk)

        f2_load(0)
        for tt_ in range(NTL):
            if tt_ + 1 < NTL:
                f2_load(tt_ + 1)
            f2_mm(tt_)
            if tt_ > 0:
                f2_ln(tt_ - 1)
        f2_ln(NTL - 1)

    if dbg and dbg.get("stop") == "F":
        c.barrier()
        c.dma("sp", dC_d, H2_d, writes=["dC0"], key="k_dbg")
    if dbg_stop("F"):
        return nc
    ffn_phase(1, True, None, None, "ln2_g1", "ln2_b1")
    if dbg:
        dbg_stop(dbg.get("stop"))
        return nc

    c.barrier()
    c.emit()
    c.close()
    return nc


def _col(v):
    v = np.asarray(v, np.float32).reshape(-1, 128)
    return np.ascontiguousarray(v.T)


def _host_layout(inp):
    f = lambda a: np.ascontiguousarray(np.asarray(a, np.float32))
    vec = np.zeros((128, NV), np.float32)

    def put(name, arr):
        o, w = VOFF[name]
        vec[:, o:o + w] = arr

    for l in range(2):
        put(f"ln1_g{l}", _col(inp["ln1_g"][l])); put(f"ln1_b{l}", _col(inp["ln1_b"][l]))
        put(f"ln2_g{l}", _col(inp["ln2_g"][l])); put(f"ln2_b{l}", _col(inp["ln2_b"][l]))
        put(f"fb1_{l}", _col(inp["ffn_b1"][l])); put(f"fb2_{l}", _col(inp["ffn_b2"][l]))
    put("b_in", _col(inp["mix_b_in"][0][:1536]))
    put("s5_d", _col(inp["s5_d"][0])); put("b_glu", _col(inp["s5_b_glu"][0])); put("b_out", _col(inp["mix_b_out"][0]))
    put("b_pw1", _col(inp["conv_b_pw1"][0])); put("b_dw", _col(inp["conv_b_dw"][0]))
    put("cln_g", _col(inp["conv_ln_g"][0])); put("cln_b", _col(inp["conv_ln_b"][0]))
    put("b_pw2", _col(inp["conv_b_pw2"][0]))
    wd = np.asarray(inp["conv_w_dw"][0], np.float32)
    put("w_dw", np.concatenate([_col(wd[k]) for k in range(31)], axis=1))
    bv = np.ascontiguousarray(np.broadcast_to(np.asarray(inp["mix_b_in"][0][1536:], np.float32)[None, :], (128, 512)))
    cst = np.zeros((128, 640), np.float32)
    cst[:, 0:128] = np.eye(128)
    j = np.arange(128)[:, None]; s = np.arange(128)[None, :]
    cst[:, 128:256] = (j >= s)
    cst[:, 256:384] = np.where(j >= s, -30000.0, 0.0)
    cst[:, 384:512] = np.where(j >= s, 30000.0, 0.0)
    cst[:, 512:640] = 1.0 / 1024.0
    lr = np.asarray(inp["s5_lambda_re"][0], np.float32); li = np.asarray(inp["s5_lambda_im"][0], np.float32)
    ld = np.asarray(inp["s5_log_dt"][0], np.float32)
    lam = np.zeros((128, 3, 16), np.float32)
    for k in range(16):
        for g2 in range(2):
            g = 2 * k + g2
            lam[g2 * 64:(g2 + 1) * 64, 0, k] = lr[g]
            lam[g2 * 64:(g2 + 1) * 64, 1, k] = li[g]
            lam[g2 * 64:(g2 + 1) * 64, 2, k] = ld[g]

    def pad_layout(arr_gnp):
        out = np.zeros((128, 16, 128), np.float32)
        for k in range(16):
            for g2 in range(2):
                g = 2 * k + g2
                c0 = 16 * (g % 8)
                out[g2 * 64:(g2 + 1) * 64, k, c0:c0 + 16] = arr_gnp[g]
        return out

    bre = np.asarray(inp["s5_b_re"][0], np.float32); bim = np.asarray(inp["s5_b_im"][0], np.float32)
    cre = np.asarray(inp["s5_c_re"][0], np.float32).transpose(0, 2, 1)
    cim = np.asarray(inp["s5_c_im"][0], np.float32).transpose(0, 2, 1)
    shared = {
        "w_in": f(inp["mix_w_in"][0]), "w_glu": f(inp["s5_w_glu"][0]), "w_out": f(inp["mix_w_out"][0]),
        "pw1": f(inp["conv_w_pw1"][0]), "pw2": f(inp["conv_w_pw2"][0]),
        "w1_0": f(inp["ffn_w1"][0]), "w1_1": f(inp["ffn_w1"][1]),
        "w2_0": f(inp["ffn_w2"][0]), "w2_1": f(inp["ffn_w2"][1]),
        "vecs": vec, "bv_bc": bv, "consts": cst, "lamll": lam,
        "bt_re": pad_layout(bre), "bt_im": pad_layout(bim), "cp_re": pad_layout(cre), "cp_im": pad_layout(cim),
    }
    return shared


def kernel(**inputs):
    x = np.asarray(inputs["x"], np.float32)
    shared = _host_layout(inputs)
    nc = build()
    in_maps = []
    for i in range(8):
        m = dict(shared)
        m["x"] = np.ascontiguousarray(x[2 * i:2 * i + 2].reshape(T, D))
        in_maps.append(m)
    res = run_bass_kernel_spmd(nc, in_maps, core_ids=list(range(8)))
    out = np.concatenate([r["y"].reshape(2, L, D) for r in res.results], axis=0)
    return out.astype(np.float32)
```

```python
import contextlib
import numpy as np
import concourse.bass as bass
import concourse.mybir as mybir
from concourse.bass_utils import run_bass_kernel_spmd

F32 = mybir.dt.float32
BF16 = mybir.dt.bfloat16
I32 = mybir.dt.int32
AF = mybir.ActivationFunctionType
ALU = mybir.AluOpType

D = 1024
L = 2048
NSEQ = 2
T = NSEQ * L
ALPHA = 4.0 ** 0.25
EPS = 1e-5
PI = float(np.pi)
PAD = 1024


class _Rec:
    def __init__(self):
        self.call = None

    def __getattr__(self, name):
        def f(*a, **kw):
            self.call = (name, a, kw)
            return self
        return f


class Ctx:
    def __init__(self, nc):
        self.nc = nc
        self.es = contextlib.ExitStack()
        self.stacks = [self.es]
        self.engs = ["pe", "act", "dve", "pool", "sp"]
        self.sem = {}
        self.cnt = {}
        for k in self.engs:
            self.sem[k] = self.es.enter_context(nc.semaphore("s_" + k))
            self.cnt[k] = 0
        self.waited = {k: {} for k in self.engs}
        self.lastw = {}
        self.reads = {}
        self.dsem = {}
        self.dcnt = {}
        self.nsb = 0
        self.prog = {k: [] for k in self.engs}

    def sb(self, shape, dt, name=None):
        self.nsb += 1
        return self.stacks[-1].enter_context(
            self.nc.sbuf_tensor(f"{name or 'sb'}_{self.nsb}", list(shape), dt))

    def ps(self, shape, dt, name=None):
        self.nsb += 1
        return self.stacks[-1].enter_context(
            self.nc.psum_tensor(f"{name or 'ps'}_{self.nsb}", list(shape), dt))

    @contextlib.contextmanager
    def scope(self):
        st = contextlib.ExitStack()
        self.stacks.append(st)
        try:
            yield
        finally:
            self.barrier()
            self.stacks.pop()
            st.close()

    def _semobj(self, key):
        return self.sem[key] if key in self.sem else self.dsem[key]

    def _deps(self, reads, writes):
        deps = []
        for r in reads:
            if r in self.lastw:
                deps.append(self.lastw[r])
        for w in writes:
            if w in self.lastw:
                deps.append(self.lastw[w])
            deps.extend(self.reads.get(w, []))
        return deps

    def _wait(self, e, deps):
        best = {}
        for (k, v) in deps:
            if e == "pe" and k == "pe":
                continue
            if v > best.get(k, 0):
                best[k] = v
        for k, v in best.items():
            if self.waited[e].get(k, 0) >= v:
                continue
            so = self._semobj(k)
            self.prog[e].append(lambda h, so=so, v=v: h.wait_ge(so, v))
            self.waited[e][k] = v

    def _record(self, ticket, reads, writes):
        for r in reads:
            self.reads.setdefault(r, []).append(ticket)
        for w in writes:
            self.lastw[w] = ticket
            self.reads[w] = []

    def op(self, e, fn, reads=(), writes=()):
        self._wait(e, self._deps(reads, writes))
        self.cnt[e] += 1
        so = self.sem[e]
        rec = _Rec()
        fn(rec)
        name, a, kw = rec.call
        self.prog[e].append(lambda h, name=name, a=a, kw=kw, so=so: getattr(h, name)(*a, **kw).then_inc(so, 1))
        t = (e, self.cnt[e])
        self._record(t, reads, writes)
        return t

    def dma(self, q, out, in_, reads=(), writes=(), key=None, **kw):
        if key not in self.dsem:
            self.dsem[key] = self.es.enter_context(self.nc.semaphore(f"d{len(self.dsem)}"))
            self.dcnt[key] = 0
        self._wait(q, self._deps(reads, writes))
        so = self.dsem[key]
        self.prog[q].append(lambda h, out=out, in_=in_, kw=kw, so=so:
                            h.dma_start(out=out, in_=in_, **kw).then_inc(so, 16))
        self.dcnt[key] += 16
        t = (key, self.dcnt[key])
        self._record(t, reads, writes)
        return t

    def barrier(self):
        deps = [(k, self.cnt[k]) for k in self.engs if self.cnt[k] > 0]
        deps += [(k, v) for k, v in self.dcnt.items() if v > 0]
        for e in self.engs:
            self._wait(e, deps)
        self.lastw = {}
        self.reads = {}

    def emit(self):
        with self.nc.Block() as block:
            def mk(e):
                def body(h):
                    for f in self.prog[e]:
                        f(h)
                return body
            block.tensor(mk("pe"))
            block.scalar(mk("act"))
            block.vector(mk("dve"))
            block.gpsimd(mk("pool"))
            block.sync(mk("sp"))

    def close(self):
        self.es.close()


VEC_SPEC = [("ln1_g0", 8), ("ln1_b0", 8), ("ln2_g0", 8), ("ln2_b0", 8),
            ("ln1_g1", 8), ("ln1_b1", 8), ("ln2_g1", 8), ("ln2_b1", 8),
            ("fb1_0", 32), ("fb1_1", 32), ("fb2_0", 8), ("fb2_1", 8),
            ("b_in", 12), ("s5_d", 4), ("b_glu", 8), ("b_out", 8),
            ("b_pw1", 16), ("b_dw", 8), ("cln_g", 8), ("cln_b", 8), ("b_pw2", 8),
            ("w_dw", 31 * 8)]
VOFF = {}
_o = 0
for _n, _w in VEC_SPEC:
    VOFF[_n] = (_o, _w)
    _o += _w
NV = _o


def build(dbg=None):
    nc = bass.Bass("TRN2", target_bir_lowering=False)

    def din(name, shape, dt=F32):
        return nc.dram_tensor(name, list(shape), dt, kind="ExternalInput").ap()

    x_d = din("x", [T, D])
    w_in_d = din("w_in", [D, 2048])
    w_glu_d = din("w_glu", [512, 1024])
    w_out_d = din("w_out", [D, D])
    pw1_d = din("pw1", [D, 2048])
    pw2_d = din("pw2", [D, D])
    w1_d = [din("w1_0", [D, 4096]), din("w1_1", [D, 4096])]
    w2_d = [din("w2_0", [4096, D]), din("w2_1", [4096, D])]
    vec_d = din("vecs", [128, NV])
    bv_d = din("bv_bc", [128, 512])
    cst_d = din("consts", [128, 640])
    lam_d = din("lamll", [128, 3, 16])
    bt_d = [din("bt_re", [128, 16, 128]), din("bt_im", [128, 16, 128])]
    cp_d = [din("cp_re", [128, 16, 128]), din("cp_im", [128, 16, 128])]
    y_d = nc.dram_tensor("y", [T, D], F32, kind="ExternalOutput").ap()
    S_d = nc.dram_tensor("S_scr", [D, T], F32, kind="Internal").ap()
    XB_d = nc.dram_tensor("XB_scr", [D, T], BF16, kind="Internal").ap()
    if dbg:
        dS_d = nc.dram_tensor("dbgS", [D, T], F32, kind="ExternalOutput").ap()
        dX_d = nc.dram_tensor("dbgX", [D, T], BF16, kind="ExternalOutput").ap()
        dC_d = nc.dram_tensor("dbgC", [D, T], BF16, kind="ExternalOutput").ap()
        dC_v = dC_d.rearrange("(c p) t -> p c t", p=128)
    HG_d = nc.dram_tensor("HG_scr", [16, 2, 128, 2048], BF16, kind="Internal").ap()
    S_v = S_d.rearrange("(c p) t -> p c t", p=128)
    XB_v = XB_d.rearrange("(c p) t -> p c t", p=128)

    c = Ctx(nc)
    op = c.op

    vec = c.sb([128, NV], F32, "vec")
    cst = c.sb([128, 640], F32, "cst")
    cstb = c.sb([128, 640], BF16, "cstb")
    der = c.sb([128, 160], F32, "der")
    banks = [c.ps([128, 512], F32, f"bank{i}") for i in range(8)]
    ident = cst[:, 0:128]
    identb = cstb[:, 0:128]
    trib = cstb[:, 128:256]
    nmaskb = cstb[:, 256:384]
    pmaskb = cstb[:, 384:512]
    onesb = cstb[:, 512:640]

    def V(name, i=0, n=1):
        o, w = VOFF[name]
        return vec[:, o + i:o + i + n]

    c.dma("sp", vec[:], vec_d, writes=["vec"], key="k_vec")
    c.dma("sp", cst[:], cst_d, writes=["cst"], key="k_cst")
    op("dve", lambda h: h.tensor_copy(out=cstb[:], in_=cst[:]), reads=["cst"], writes=["cstb"])

    DER = {}
    _do = [0]

    def dalloc(name, n):
        DER[name] = _do[0]
        _do[0] += n
        return der[:, DER[name]:DER[name] + n]

    bq8 = dalloc("bq8", 4)
    op("dve", lambda h: h.tensor_scalar(out=bq8, in0=V("b_in", 4, 4), scalar1=0.125, scalar2=None,
                                        op0=ALU.mult), reads=["vec"], writes=["der"])

    def ln_consts(tag, gname, bname, nextb):
        ga = dalloc("ga" + tag, 8)
        ba = dalloc("ba" + tag, 8)
        op("dve", lambda h: h.tensor_scalar(out=ga, in0=V(gname, 0, 8), scalar1=ALPHA, scalar2=None,
                                            op0=ALU.mult), reads=["vec"], writes=["der"])
        op("dve", lambda h: h.scalar_tensor_tensor(out=ba, in0=V(bname, 0, 8), scalar=ALPHA,
                                                   in1=V(nextb, 0, 8), op0=ALU.mult, op1=ALU.add),
           reads=["vec"], writes=["der"])
        return ga, ba

    ga10, ba10 = ln_consts("10", "ln1_g0", "ln1_b0", "fb2_0")
    ga20, ba20 = ln_consts("20", "ln2_g0", "ln2_b0", "b_pw2")
    ga11, ba11 = ln_consts("11", "ln1_g1", "ln1_b1", "fb2_1")

    bkrr = [0]

    def nb(pool=(0, 1, 2, 3, 4, 5, 6, 7)):
        bkrr[0] += 1
        return pool[bkrr[0] % len(pool)]

    def BK(i):
        return ("bk", i)

    def mm(bank, out_ap, lhsT, rhs, start, stop, reads, **kw):
        op("pe", lambda h: h.matmul(out_ap, lhsT=lhsT, rhs=rhs, start=start, stop=stop, **kw),
           reads=reads, writes=[BK(bank)])

    def load_w(dst, src_d, kc_n, ncols, key, rkey, cs0=0, c0b=0):
        sv = src_d.rearrange("(c p) f -> p c f", p=128)
        step = 2048
        for kc in range(kc_n):
            for c0 in range(0, ncols, step):
                c1 = min(ncols, c0 + step)
                c.dma("pool", dst[:, kc, c0b + c0:c0b + c1], sv[:, kc, cs0 + c0:cs0 + c1],
                      writes=[(rkey, kc, c0)], key=key)
        fin = (key, c.dcnt[key])
        for kc in range(kc_n):
            for c0 in range(0, ncols, step):
                c.lastw[(rkey, kc, c0)] = fin

    def wreads(rkey, kc, ncols):
        return [(rkey, kc, c0) for c0 in range(0, ncols, 2048)]

    def layer_norm(zt, N, zkey, xh, xhkey, lnb):
        zb, zq, msq, var, rstd, nmr = lnb["zb"], lnb["zq"], lnb["msq"], lnb["var"], lnb["rstd"], lnb["nmr"]
        op("act", lambda h: h.activation(out=zb[:, :, 0:N], in_=zt[:, :, 0:N], func=AF.Copy),
           reads=[zkey], writes=["ln_zb"])
        op("act", lambda h: h.activation(out=zq[:, :, 0:N], in_=zt[:, :, 0:N], func=AF.Square),
           reads=[zkey], writes=["ln_zq"])
        bm, bq = nb(), nb()
        for kc in range(8):
            mm(bm, banks[bm][:, 0:N], onesb, zb[:, kc, 0:N], kc == 0, kc == 7, ["ln_zb", "cstb"])
        for kc in range(8):
            mm(bq, banks[bq][:, 0:N], onesb, zq[:, kc, 0:N], kc == 0, kc == 7, ["ln_zq", "cstb"])
        op("act", lambda h: h.activation(out=msq[:, 0:N], in_=banks[bm][:, 0:N], func=AF.Square),
           writes=[BK(bm), "ln_msq"])
        op("dve", lambda h: h.tensor_tensor(out=var[:, 0:N], in0=banks[bq][:, 0:N], in1=msq[:, 0:N],
                                            op=ALU.subtract), reads=["ln_msq"], writes=[BK(bq), "ln_var"])
        op("act", lambda h: h.activation(out=var[:, 0:N], in_=var[:, 0:N], func=AF.Ln, bias=EPS, scale=1.0),
           reads=["ln_var"], writes=["ln_var"])
        op("act", lambda h: h.activation(out=banks[bq][:, 0:N], in_=var[:, 0:N], func=AF.Exp, scale=-0.5),
           reads=["ln_var"], writes=[BK(bq)])
        mb = banks[bm][:, 0:N].unsqueeze(1).broadcast_to([128, 8, N])
        rb = banks[bq][:, 0:N].unsqueeze(1).broadcast_to([128, 8, N])
        op("dve", lambda h: h.tensor_tensor(out=xh[:, :, 0:N], in0=zt[:, :, 0:N], in1=mb, op=ALU.subtract),
           reads=[zkey], writes=[BK(bm), xhkey])
        op("dve", lambda h: h.tensor_tensor(out=xh[:, :, 0:N], in0=xh[:, :, 0:N], in1=rb, op=ALU.mult),
           reads=[], writes=[BK(bq), xhkey])

    def ln_scratch(N):
        return {"zb": c.sb([128, 8, N], BF16, "zb"), "zq": c.sb([128, 8, N], BF16, "zq"),
                "msq": c.sb([128, N], F32, "msq"), "var": c.sb([128, N], F32, "var"),
                "rstd": c.sb([128, N], F32, "rstd"), "nmr": c.sb([128, N], F32, "nmr")}

    def ln_epilogue_stream(xh, xhkey, N, tok0, ga, ba, gname, bname, sst, xbt, slot, sstkey=None):
        sk = sstkey or ("sst", slot)
        for kc in range(8):
            op("act", lambda h, kc=kc: h.activation(out=xbt[:, kc, 0:N], in_=xh[:, kc, 0:N], func=AF.Identity,
                                                    bias=V(bname, kc, 1), scale=V(gname, kc, 1)),
               reads=[xhkey, "vec"], writes=[("xbt", slot)])
        for kc in range(8):
            op("act", lambda h, kc=kc: h.activation(out=sst[:, kc, 0:N], in_=xh[:, kc, 0:N], func=AF.Identity,
                                                    bias=ba[:, kc:kc + 1], scale=ga[:, kc:kc + 1]),
               reads=[xhkey, "der"], writes=[sk])
        c.dma("pool", S_v[:, :, tok0:tok0 + N], sst[:, :, 0:N], reads=[sk],
              writes=[("S_d", tok0)], key=f"k_sst{slot}")
        c.dma("pool", XB_v[:, :, tok0:tok0 + N], xbt[:, :, 0:N], reads=[("xbt", slot)],
              writes=[("XB_d", tok0)], key=f"k_xbt{slot}")

    with c.scope():
        bv = c.sb([128, 512], F32, "bv")
        c.dma("sp", bv[:], bv_d, writes=["bv"], key="k_bv")

        lam = c.sb([128, 3, 16], F32, "lam")
        c.dma("sp", lam[:], lam_d, writes=["lam"], key="k_lam")
        tb = c.sb([128, 24, 16], F32, "s5tmp")
        tbi = c.sb([128, 16], I32, "s5tmpi")
        AR = c.sb([128, 11, 16], F32, "AR")
        AI = c.sb([128, 11, 16], F32, "AI")
        NAI = c.sb([128, 11, 16], F32, "NAI")
        TK = "s5t"

        def tt(out, a, b, o, e="dve"):
            op(e, lambda h: h.tensor_tensor(out=out, in0=a, in1=b, op=o), reads=[TK, "lam"], writes=[TK])

        def ts(out, a, s1, o1, s2=None, o2=None):
            if o2 is None:
                op("dve", lambda h: h.tensor_scalar(out=out, in0=a, scalar1=s1, scalar2=None, op0=o1),
                   reads=[TK, "lam"], writes=[TK])
            else:
                op("dve", lambda h: h.tensor_scalar(out=out, in0=a, scalar1=s1, scalar2=s2, op0=o1, op1=o2),
                   reads=[TK, "lam"], writes=[TK])

        def act(out, a, f, **kw):
            op("act", lambda h: h.activation(out=out, in_=a, func=f, **kw), reads=[TK, "lam"], writes=[TK])

        dt_, lr_, a_, th_, mag_ = tb[:, 0, :], tb[:, 1, :], tb[:, 2, :], tb[:, 3, :], tb[:, 4, :]
        act(dt_, lam[:, 2, :], AF.Exp)
        ts(lr_, lam[:, 0, :], -1e-4, ALU.min)
        tt(a_, lr_, dt_, ALU.mult)
        tt(th_, lam[:, 1, :], dt_, ALU.mult)
        act(mag_, a_, AF.Exp)

        def sin_of(out, src, shift):
            u, kf, r, g = tb[:, 5, :], tb[:, 6, :], tb[:, 7, :], tb[:, 8, :]
            ts(u, src, shift, ALU.add, 1.0 / (2 * PI), ALU.mult)
            op("dve", lambda h: h.tensor_copy(out=tbi[:], in_=u), reads=[TK], writes=[TK])
            op("dve", lambda h: h.tensor_copy(out=kf, in_=tbi[:]), reads=[TK], writes=[TK])
            ts(kf, kf, -2 * PI, ALU.mult, shift, ALU.add)
            tt(r, src, kf, ALU.add)
            ts(g, r, PI, ALU.is_gt, -2 * PI, ALU.mult)
            tt(r, r, g, ALU.add)
            ts(g, r, -PI, ALU.is_lt, 2 * PI, ALU.mult)
            tt(r, r, g, ALU.add)
            act(out, r, AF.Sin)

        sn_, cs_ = tb[:, 9, :], tb[:, 10, :]
        sin_of(sn_, th_, 0.0)
        sin_of(cs_, th_, PI / 2)
        tt(AR[:, 0, :], mag_, cs_, ALU.mult)
        tt(AI[:, 0, :], mag_, sn_, ALU.mult)
        for s in range(10):
            t1, t2 = tb[:, 11, :], tb[:, 12, :]
            tt(t1, AR[:, s, :], AR[:, s, :], ALU.mult)
            tt(t2, AI[:, s, :], AI[:, s, :], ALU.mult)
            tt(AR[:, s + 1, :], t1, t2, ALU.subtract)
            tt(t1, AR[:, s, :], AI[:, s, :], ALU.mult)
            ts(AI[:, s + 1, :], t1, 2.0, ALU.mult)
        ts(NAI[:], AI[:], -1.0, ALU.mult)
        den, nr, Fr, Fi = tb[:, 13, :], tb[:, 14, :], tb[:, 15, :], tb[:, 16, :]
        t1, t2 = tb[:, 11, :], tb[:, 12, :]
        tt(t1, lr_, lr_, ALU.mult)
        tt(t2, lam[:, 1, :], lam[:, 1, :], ALU.mult)
        tt(den, t1, t2, ALU.add)
        op("dve", lambda h: h.reciprocal(out=den, in_=den), reads=[TK], writes=[TK])
        ts(nr, AR[:, 0, :], -1.0, ALU.add)
        tt(t1, nr, lr_, ALU.mult)
        tt(t2, AI[:, 0, :], lam[:, 1, :], ALU.mult)
        tt(t1, t1, t2, ALU.add)
        tt(Fr, t1, den, ALU.mult)
        tt(t1, AI[:, 0, :], lr_, ALU.mult)
        tt(t2, nr, lam[:, 1, :], ALU.mult)
        tt(t1, t1, t2, ALU.subtract)
        tt(Fi, t1, den, ALU.mult)

        bb = [c.sb([128, 16, 128], F32, "bb_re"), c.sb([128, 16, 128], F32, "bb_im")]
        cp = [c.sb([128, 16, 128], F32, "cp_re"), c.sb([128, 16, 128], F32, "cp_im")]
        PR = c.sb([128, 9, 16], F32, "PR")
        PI_ = c.sb([128, 9, 16], F32, "PI")
        A8R = c.sb([128, 8, 16], F32, "A8R")
        A8I = c.sb([128, 8, 16], F32, "A8I")
        NA8I = c.sb([128, 8, 16], F32, "NA8I")
        Kblk = c.sb([128, 4, 8, 128], BF16, "Kblk")
        for i in range(2):
            c.dma("sp", cp[i][:], cp_d[i], writes=[("cp", i)], key=f"k_cp{i}")
        op("dve", lambda h: h.memset(PR[:, 0, :], 1.0), reads=[TK], writes=[TK])
        op("dve", lambda h: h.memset(PI_[:, 0, :], 0.0), reads=[TK], writes=[TK])
        ts(PR[:, 1, :], AR[:, 0, :], 1.0, ALU.mult)
        ts(PI_[:, 1, :], AI[:, 0, :], 1.0, ALU.mult)
        for j in range(1, 8):
            t1, t2 = tb[:, 11, :], tb[:, 12, :]
            tt(t1, PR[:, j, :], AR[:, 0, :], ALU.mult)
            tt(t2, PI_[:, j, :], AI[:, 0, :], ALU.mult)
            tt(PR[:, j + 1, :], t1, t2, ALU.subtract)
            tt(t1, PR[:, j, :], AI[:, 0, :], ALU.mult)
            tt(t2, PI_[:, j, :], AR[:, 0, :], ALU.mult)
            tt(PI_[:, j + 1, :], t1, t2, ALU.add)
        ts(A8R[:, 0, :], AR[:, 3, :], 1.0, ALU.mult)
        ts(A8I[:, 0, :], AI[:, 3, :], 1.0, ALU.mult)
        for s_ in range(1, 8):
            ts(A8R[:, s_, :], AR[:, 3 + s_, :], 1.0, ALU.mult)
            ts(A8I[:, s_, :], AI[:, 3 + s_, :], 1.0, ALU.mult)
        ts(NA8I[:], A8I[:], -1.0, ALU.mult)
        NPI = c.sb([128, 9, 16], F32, "NPI")
        ts(NPI[:], PI_[:], -1.0, ALU.mult)
        with c.scope():
            bt = [c.sb([128, 16, 128], F32, "bt_re"), c.sb([128, 16, 128], F32, "bt_im")]
            w1t = c.sb([128, 16, 128], F32, "w1t")
            w2t = c.sb([128, 16, 128], F32, "w2t")
            sre = c.sb([128, 16, 128], F32, "sre")
            sim = c.sb([128, 16, 128], F32, "sim")
            for i in range(2):
                c.dma("sp", bt[i][:], bt_d[i], writes=[("bt", i)], key=f"k_bt{i}")
            Frb = Fr.unsqueeze(2).broadcast_to([128, 16, 128])
            Fib = Fi.unsqueeze(2).broadcast_to([128, 16, 128])

            def t3(out, a, b, o, rk, wk, e="dve"):
                op(e, lambda h: h.tensor_tensor(out=out, in0=a, in1=b, op=o), reads=rk + [TK], writes=wk)

            t3(w1t[:], bt[0][:], Frb, ALU.mult, [("bt", 0)], ["w1t"])
            t3(w2t[:], bt[1][:], Fib, ALU.mult, [("bt", 1)], ["w2t"])
            t3(bb[0][:], w1t[:], w2t[:], ALU.subtract, ["w1t", "w2t"], [("bb", 0)])
            t3(w1t[:], bt[1][:], Frb, ALU.mult, [("bt", 1)], ["w1t"])
            t3(w2t[:], bt[0][:], Fib, ALU.mult, [("bt", 0)], ["w2t"])
            t3(bb[1][:], w1t[:], w2t[:], ALU.add, ["w1t", "w2t"], [("bb", 1)])
            for tau in range(8):
                prb = PR[:, tau, :].unsqueeze(2).broadcast_to([128, 16, 128])
                pib = PI_[:, tau, :].unsqueeze(2).broadcast_to([128, 16, 128])
                t3(w1t[:], bb[0][:], prb, ALU.mult, [("bb", 0)], ["w1t"])
                t3(w2t[:], bb[1][:], pib, ALU.mult, [("bb", 1)], ["w2t"])
                t3(sre[:], w1t[:], w2t[:], ALU.subtract, ["w1t", "w2t"], ["sre"])
                t3(w1t[:], bb[0][:], pib, ALU.mult, [("bb", 0)], ["w1t"])
                t3(w2t[:], bb[1][:], prb, ALU.mult, [("bb", 1)], ["w2t"])
                t3(sim[:], w1t[:], w2t[:], ALU.add, ["w1t", "w2t"], ["sim"])
                op("dve", lambda h: h.tensor_scalar(out=sim[:], in0=sim[:], scalar1=-1.0, scalar2=None, op0=ALU.mult),
                   reads=["sim"], writes=["sim"])
                for q in range(4):
                    b_ = nb()
                    n_ = 0
                    for kk in range(4):
                        for (a_, c_, ak) in ((sre, cp[0], "sre"), (sim, cp[1], "sim")):
                            mm(b_, banks[b_][:, 0:128], a_[:, 4 * q + kk, :], c_[:, 4 * q + kk, :], n_ == 0, n_ == 7,
                               [ak, ("cp", 0), ("cp", 1)])
                            n_ += 1
                    op("act", lambda h, q=q, tau=tau, b_=b_: h.activation(out=Kblk[:, q, tau, :],
                                                                          in_=banks[b_][:, 0:128], func=AF.Copy),
                       writes=[BK(b_), "Kblk"])


        for sq in range(NSEQ):
            with c.scope():
                T0 = sq * L
                catA = c.sb([128, 4, L], BF16, "catA")

                def phaseA(part, outs):
                    with c.scope():
                        ncol = 512 if part == 0 else 1536
                        w_in = c.sb([128, 8, ncol], BF16, "w_in")
                        load_w(w_in, w_in_d, 8, ncol, f"k_w_in{part}", "w_in", cs0=(0 if part == 0 else 512))
                        XBt = [c.sb([128, 8, 512], BF16, "XBt0"), c.sb([128, 8, 512], BF16, "XBt1")]
                        xt = [c.sb([128, D], F32, f"xt{i}") for i in range(4)]
                        if part == 0:
                            sst = [c.sb([128, 8, 128], F32, f"sstA{i}") for i in range(4)]
                        for tt_ in range(4):
                            XB = XBt[tt_ % 2]
                            xk = ("XB", tt_ % 2)
                            for bl in range(4):
                                tbk = tt_ * 4 + bl
                                sl = tbk % 4
                                tok = T0 + tbk * 128
                                c.dma("sp", xt[sl][:], x_d[tok:tok + 128, :], writes=[("xt", sl)], key=f"k_xt{sl}")
                                b0, b1 = nb(), nb()
                                for fc in range(8):
                                    bq_ = b0 if fc < 4 else b1
                                    op("pe", lambda h, fc=fc, bq_=bq_, sl=sl: h.transpose(
                                        out=banks[bq_][:, (fc % 4) * 128:(fc % 4) * 128 + 128],
                                        in_=xt[sl][:, fc * 128:(fc + 1) * 128], identity=ident),
                                        reads=[("xt", sl), "cst"], writes=[BK(bq_)])
                                for fc in range(8):
                                    bq_ = b0 if fc < 4 else b1
                                    src = banks[bq_][:, (fc % 4) * 128:(fc % 4) * 128 + 128]
                                    if part == 0:
                                        op("act", lambda h, fc=fc, src=src, sl=sl: h.activation(
                                            out=sst[sl][:, fc, :], in_=src, func=AF.Identity,
                                            bias=V("b_out", fc, 1), scale=ALPHA),
                                            reads=["vec"], writes=[BK(bq_), ("sstA", sl)])
                                    op("dve", lambda h, fc=fc, src=src, bl=bl: h.tensor_copy(
                                        out=XB[:, fc, bl * 128:(bl + 1) * 128], in_=src),
                                        writes=[BK(bq_), xk])
                                if part == 0:
                                    c.dma("pool", S_v[:, :, tok:tok + 128], sst[sl][:], reads=[("sstA", sl)],
                                          writes=[("S_d", tok)], key=f"k_sstA{sl}")
                            cs = slice(tt_ * 512, (tt_ + 1) * 512)
                            for oc in range(4 if part == 0 else 8):
                                b_ = nb()
                                for kc in range(8):
                                    mm(b_, banks[b_][:, :], w_in[:, kc, oc * 128:(oc + 1) * 128], XB[:, kc, :],
                                       kc == 0, kc == 7, wreads("w_in", kc, ncol) + [xk])
                                if part == 0:
                                    u_f = outs[0]
                                    op("act", lambda h, oc=oc, b_=b_, cs=cs: h.activation(
                                        out=u_f[:, oc, cs], in_=banks[b_][:, :], func=AF.Identity,
                                        bias=V("b_in", oc, 1), scale=1.0), reads=["vec"],
                                        writes=[BK(b_), ("u_f", tt_)])
                                elif oc < 4:
                                    qT = outs[0]
                                    op("act", lambda h, oc=oc, b_=b_, cs=cs: h.activation(
                                        out=qT[:, oc, cs], in_=banks[b_][:, :], func=AF.Identity,
                                        bias=bq8[:, oc:oc + 1], scale=0.125), reads=["der"],
                                        writes=[BK(b_), ("qT", tt_)])
                                else:
                                    kT, nkT = outs[1], outs[2]
                                    op("act", lambda h, oc=oc, b_=b_, cs=cs: h.activation(
                                        out=kT[:, oc - 4, cs], in_=banks[b_][:, :], func=AF.Identity,
                                        bias=V("b_in", 4 + oc, 1), scale=1.0), reads=["vec"],
                                        writes=[BK(b_), ("kT", tt_)])
                                    op("dve", lambda h, oc=oc, cs=cs: h.tensor_scalar(
                                        out=nkT[:, oc - 4, cs], in0=kT[:, oc - 4, cs], scalar1=-1.0, scalar2=None,
                                        op0=ALU.mult), reads=[("kT", tt_)], writes=[("nkT", tt_)])
                            if part == 1:
                                Vt = outs[3]
                                for bl in range(4):
                                    tbk = tt_ * 4 + bl
                                    b_ = nb()
                                    for kc in range(8):
                                        mm(b_, banks[b_][:, :], XB[:, kc, bl * 128:(bl + 1) * 128],
                                           w_in[:, kc, 1024:1536], kc == 0, kc == 7,
                                           wreads("w_in", kc, ncol) + [xk])
                                    op("dve", lambda h, tbk=tbk, b_=b_: h.tensor_tensor(
                                        out=Vt[:, tbk, :], in0=banks[b_][:, :], in1=bv[:], op=ALU.add),
                                        reads=["bv"], writes=[BK(b_), ("Vt", tbk)])

                with c.scope():
                    u_f = c.sb([128, 4, L], BF16, "u_f")
                    phaseA(0, [u_f])
                    w_glu = c.sb([128, 4, 1024], BF16, "w_glu")
                    load_w(w_glu, w_glu_d, 4, 1024, "k_w_glu", "w_glu")
                    YB = (0, 1, 2, 3)
                    WB = (4, 5, 6, 7)
                    XAs = [[c.sb([128, 2, 256], F32, f"XA{a}{b}") for b in range(2)] for a in range(2)]
                    Xc = [c.sb([128, 2, 256], BF16, f"Xc{i}") for i in range(2)]
                    Hf = [c.sb([128, 8, 2, 128], F32, f"Hf{i}") for i in range(2)]
                    Hp = [c.sb([128, 8, 2, 128], BF16, f"Hp{i}") for i in range(2)]
                    Gp = [c.sb([128, 8, 2, 128], BF16, f"Gp{i}") for i in range(4)]
                    tmpg = c.sb([128, 128], F32, "tmpg")
                    zfull = c.sb([128, L], F32, "zfull")
                    gfull = c.sb([128, L], F32, "gfull")
                    yg = c.sb([128, 4, L], BF16, "yg")
                    sg_ = [c.sb([128, 512], F32, "sg0"), c.sb([128, 512], F32, "sg1")]
                    tmpA = c.sb([128, 8, 128], F32, "tmpA")
                    tmpB = c.sb([128, 8, 128], F32, "tmpB")

                    def uq_of(q):
                        return u_f[:, q, :].rearrange("p (c i) -> p i c", i=8)

                    UK = [("u_f", t_) for t_ in range(4)]

                    def tab_ops(p):
                        k, sl = p, p % 2
                        th = []
                        rk = [TK, ("bb", 0), ("bb", 1), ("cp", 0), ("cp", 1)]
                        if sq == 1:
                            th.append(lambda: c.dma("sp", Hp[sl][:].rearrange("p a b n -> p (a b n)"), HG_d[p, 0],
                                                    reads=[("HG_d", p, 0)],
                                                    writes=[("Hp", sl, jj) for jj in range(8)], key=f"k_hpl{sl}"))
                            th.append(lambda: c.dma("sp", Gp[p % 4][:].rearrange("p a b n -> p (a b n)"), HG_d[p, 1],
                                                    reads=[("HG_d", p, 1)],
                                                    writes=[(("Gp", p % 4), i, r) for i in range(8) for r in range(2)],
                                                    key=f"k_gpl{p % 4}"))
                            return th

                        def add(fn, reads, writes):
                            th.append(lambda: op("dve", fn, reads=reads, writes=writes))

                        for (o, a_re, a_im, j0, wk, neg) in ((Hf[sl], bb[0][:, k, :], bb[1][:, k, :], 0, ("Hf", sl), False),
                                                             (Gp[p % 4], cp[0][:, k, :], cp[1][:, k, :], 1, ("Gp", p % 4),
                                                              True)):
                            tA = ("tmpA", neg)
                            tB = ("tmpB", neg)
                            ta = tmpA if not neg else tmpB
                            for j in range(8):
                                si = PI_[:, j0 + j, k:k + 1]
                                add(lambda h, j=j, si=si, ta=ta, a_im=a_im: h.tensor_scalar(
                                    out=ta[:, j, :], in0=a_im, scalar1=si, scalar2=None, op0=ALU.mult), rk, [(tA, j)])
                            for j in range(8):
                                sr = PR[:, j0 + j, k:k + 1]
                                add(lambda h, j=j, sr=sr, ta=ta, a_re=a_re, o=o: h.scalar_tensor_tensor(
                                    out=o[:, j, 0, :], in0=a_re, scalar=sr, in1=ta[:, j, :], op0=ALU.mult,
                                    op1=ALU.subtract), rk + [(tA, j)], [(wk, j, 0)])
                            for j in range(8):
                                sr = PR[:, j0 + j, k:k + 1]
                                if neg:
                                    add(lambda h, j=j, sr=sr, ta=ta, a_im=a_im: h.tensor_scalar(
                                        out=ta[:, j, :], in0=a_im, scalar1=sr, scalar2=-1.0, op0=ALU.mult, op1=ALU.mult),
                                        rk, [(tA, j)])
                                else:
                                    add(lambda h, j=j, sr=sr, ta=ta, a_im=a_im: h.tensor_scalar(
                                        out=ta[:, j, :], in0=a_im, scalar1=sr, scalar2=None, op0=ALU.mult), rk, [(tA, j)])
                            for j in range(8):
                                si = (NPI if neg else PI_)[:, j0 + j, k:k + 1]
                                add(lambda h, j=j, si=si, ta=ta, a_re=a_re, o=o: h.scalar_tensor_tensor(
                                    out=o[:, j, 1, :], in0=a_re, scalar=si, in1=ta[:, j, :], op0=ALU.mult, op1=ALU.add),
                                    rk + [(tA, j)], [(wk, j, 1)])
                        return th

                    def emit_transposes_and_V(p):
                        k, sl, q = p, p % 2, p // 4
                        uq = uq_of(q)
                        for jj in range(8 if sq == 0 else 0):
                            b_ = nb(WB)
                            for r in range(2):
                                op("pe", lambda h, jj=jj, r=r, b_=b_: h.transpose(
                                    out=banks[b_][:, r * 128:(r + 1) * 128], in_=Hf[sl][:, jj, r, :], identity=ident),
                                    reads=[(("Hf", sl), jj, r), "cst"], writes=[BK(b_)])
                            op("act", lambda h, jj=jj, b_=b_: h.activation(
                                out=Hp[sl][:, jj, :, :], in_=banks[b_][:, 0:256].rearrange("p (r n) -> p r n", r=2),
                                func=AF.Copy), writes=[BK(b_), ("Hp", sl, jj)])
                        if sq == 0:
                            c.dma("pool", HG_d[p, 0], Hp[sl][:].rearrange("p a b n -> p (a b n)"),
                                  reads=[("Hp", sl, jj) for jj in range(8)], writes=[("HG_d", p, 0)], key=f"k_hps{sl}")
                            c.dma("pool", HG_d[p, 1], Gp[p % 4][:].rearrange("p a b n -> p (a b n)"),
                                  reads=[(("Gp", p % 4), i, r) for i in range(8) for r in range(2)],
                                  writes=[("HG_d", p, 1)], key=f"k_gps{p % 4}")
                        b_ = nb(WB)
                        for r in range(2):
                            for j in range(8):
                                mm(b_, banks[b_][:, r * 256:(r + 1) * 256], Hp[sl][:, 7 - j, r, :], uq[:, j, :],
                                   j == 0 and r == 0, j == 7 and r == 1, [("Hp", sl, 7 - j)] + UK)
                        op("act", lambda h, b_=b_: h.activation(
                            out=XAs[sl][0][:], in_=banks[b_][:, :].rearrange("p (r n) -> p r n", r=2), func=AF.Copy),
                            writes=[BK(b_), ("XA", sl, 0)])

                    def ks_ops(p):
                        k, sl = p, p % 2
                        th = []
                        for s in range(8):
                            d = 1 << s
                            sa, da = s % 2, 1 - (s % 2)
                            src, dst = XAs[sl][sa], XAs[sl][da]
                            last = (s == 7)
                            o = Xc[sl] if last else dst
                            ok = ("Xc", sl) if last else ("XA", sl, da)
                            kr = [("XA", sl, sa)]

                            def stt(out, in0, sc, in1, rk, wk):
                                th.append(lambda: op("dve", lambda h: h.scalar_tensor_tensor(
                                    out=out, in0=in0, scalar=sc, in1=in1, op0=ALU.mult, op1=ALU.add),
                                    reads=rk + [TK], writes=wk))
                            th.append(lambda o=o, src=src, d=d, kr=kr, ok=ok: op(
                                "pool", lambda h: h.tensor_copy(out=o[:, :, 0:d], in_=src[:, :, 0:d]), reads=kr,
                                writes=[ok]))
                            stt(dst[:, 0, d:], src[:, 0, 0:256 - d], A8R[:, s, k:k + 1], src[:, 0, d:], kr, [("XA", sl, da)])
                            stt(dst[:, 1, d:], src[:, 0, 0:256 - d], A8I[:, s, k:k + 1], src[:, 1, d:], kr, [("XA", sl, da)])
                            stt(o[:, 0, d:], src[:, 1, 0:256 - d], NA8I[:, s, k:k + 1], dst[:, 0, d:],
                                kr + [("XA", sl, da)], [ok])
                            stt(o[:, 1, d:], src[:, 1, 0:256 - d], A8R[:, s, k:k + 1], dst[:, 1, d:],
                                kr + [("XA", sl, da)], [ok])
                        return th

                    def toeplitz(q):
                        uq = uq_of(q)
                        for b in range(4):
                            first = True
                            for i in (2 * b, 2 * b + 1):
                                for tau in range(i + 1):
                                    mm(YB[b], banks[YB[b]][:, (i % 2) * 256:(i % 2) * 256 + 256], Kblk[:, q, tau, :],
                                       uq[:, i - tau, :], first, False, ["Kblk"] + UK)
                                    first = False

                    def farfield(p):
                        sl, kk = p % 2, p % 4
                        for i in range(8):
                            for r in range(2):
                                mm(YB[i // 2], banks[YB[i // 2]][:, (i % 2) * 256 + 1:(i % 2) * 256 + 256],
                                   Gp[p % 4][:, i, r, :], Xc[sl][:, r, 0:255], False, (kk == 3 and i % 2 == 1 and r == 1),
                                   [(("Gp", p % 4), i, r), ("Xc", sl)])

                    def zgelu(q):
                        uq = uq_of(q)
                        zv = zfull[:, :].rearrange("p (c i) -> p i c", i=8)
                        for b in range(4):
                            op("dve", lambda h, b=b: h.scalar_tensor_tensor(
                                out=zv[:, 2 * b:2 * b + 2, :], in0=uq[:, 2 * b:2 * b + 2, :], scalar=V("s5_d", q, 1),
                                in1=banks[YB[b]][:, :].rearrange("p (i c) -> p i c", i=2), op0=ALU.mult, op1=ALU.add),
                                reads=UK + ["vec"], writes=[BK(YB[b]), "zfull"])
                        op("act", lambda h: h.activation(out=gfull[:], in_=zfull[:], func=AF.Square),
                           reads=["zfull"], writes=["gfull"])
                        op("dve", lambda h: h.tensor_scalar(out=gfull[:], in0=gfull[:], scalar1=0.044715, scalar2=1.0,
                                                            op0=ALU.mult, op1=ALU.add), reads=["gfull"], writes=["gfull"])
                        op("dve", lambda h: h.tensor_tensor(out=gfull[:], in0=gfull[:], in1=zfull[:], op=ALU.mult),
                           reads=["gfull", "zfull"], writes=["gfull"])
                        op("act", lambda h: h.activation(out=gfull[:], in_=gfull[:], func=AF.Sigmoid,
                                                         scale=1.5957691216057308), reads=["gfull"], writes=["gfull"])
                        op("dve", lambda h: h.tensor_tensor(out=yg[:, q, :], in0=gfull[:], in1=zfull[:], op=ALU.mult),
                           reads=["gfull", "zfull"], writes=[("yg", q, t_) for t_ in range(4)])

                    def interleave(*lists):
                        idx = [0] * len(lists)
                        more = True
                        while more:
                            more = False
                            for li, l_ in enumerate(lists):
                                if idx[li] < len(l_):
                                    l_[idx[li]]()
                                    idx[li] += 1
                                    more = True

                    interleave(tab_ops(0) + tab_ops(1))
                    for pp in range(8):
                        p0, p1 = 2 * pp, 2 * pp + 1
                        emit_transposes_and_V(p0)
                        emit_transposes_and_V(p1)
                        if p0 % 4 == 0:
                            toeplitz(p0 // 4)
                        nxt = (tab_ops(p0 + 2) + tab_ops(p1 + 2)) if p0 + 2 < 16 else []
                        interleave(ks_ops(p0), ks_ops(p1), nxt[0::2], nxt[1::2])
                        farfield(p0)
                        farfield(p1)
                        if p1 % 4 == 3:
                            zgelu(p1 // 4)
                    for tt_ in range(4):
                        cs = slice(tt_ * 512, (tt_ + 1) * 512)
                        for oc in range(4):
                            bv_, bg_ = nb(), nb()
                            sl = oc % 2
                            for kc in range(4):
                                mm(bv_, banks[bv_][:, :], w_glu[:, kc, oc * 128:(oc + 1) * 128], yg[:, kc, cs],
                                   kc == 0, kc == 3, wreads("w_glu", kc, 1024) + [("yg", kc, tt_)])
                            for kc in range(4):
                                mm(bg_, banks[bg_][:, :], w_glu[:, kc, 512 + oc * 128:512 + (oc + 1) * 128],
                                   yg[:, kc, cs], kc == 0, kc == 3,
                                   wreads("w_glu", kc, 1024) + [("yg", kc, tt_)])
                            op("act", lambda h, oc=oc, bg_=bg_, sl=sl: h.activation(
                                out=sg_[sl][:], in_=banks[bg_][:, :], func=AF.Sigmoid,
                                bias=V("b_glu", 4 + oc, 1), scale=1.0), reads=["vec"],
                                writes=[BK(bg_), ("sg", sl)])
                            op("dve", lambda h, oc=oc, bv_=bv_, sl=sl, cs=cs: h.scalar_tensor_tensor(
                                out=catA[:, oc, cs], in0=banks[bv_][:, :], scalar=V("b_glu", oc, 1), in1=sg_[sl][:],
                                op0=ALU.add, op1=ALU.mult), reads=["vec", ("sg", sl)],
                                writes=[BK(bv_), ("catA", oc, tt_)])

                catB = c.sb([128, 4, L], BF16, "catB")
                with c.scope():
                    qT = c.sb([128, 4, L], BF16, "qT")
                    kT = c.sb([128, 4, L], BF16, "kT")
                    nkT = c.sb([128, 4, L], BF16, "nkT")
                    Vt = c.sb([128, 16, 512], BF16, "Vt")
                    phaseA(1, [qT, kT, nkT, Vt])
                    ZB = (0, 1, 2)
                    BB = (3, 4, 5)
                    OB2 = (6, 7)
                    NBUF = 4
                    FP16 = mybir.dt.float16
                    ebuf = [c.sb([128, 512], F32, f"ebuf{i}") for i in range(NBUF)]
                    spb = [c.sb([128, 512], BF16, f"spb{i}") for i in range(NBUF)]
                    Pb = [c.sb([128, 512], BF16, f"Pb{i}") for i in range(NBUF)]
                    zrow = [c.sb([1, 512], F32, f"zrow{i}") for i in range(NBUF)]
                    A16 = [c.sb([1, 512], FP16, f"A16_{i}") for i in range(2)]
                    onesr = c.sb([1, 128], BF16, "onesr")
                    op("dve", lambda h: h.memset(onesr[:], 1.0), writes=["onesr"])
                    items = []
                    for m_ in range(4):
                        for qt in range(4):
                            for kb in range(4 * qt + 3, -1, -1):
                                for st in range(2):
                                    items.append((2 * m_ + st, qt, kb, st))

                    def geo(i):
                        hd, qt, kb, st = items[i]
                        r = max(0, kb - 4 * qt)
                        return dict(hd=hd, qt=qt, kb=kb, st=st, ch=hd // 2, pb=(hd % 2) * 64, r=r, diag=kb >= 4 * qt,
                                    c0=r * 128, qs=slice(qt * 512 + r * 128, (qt + 1) * 512),
                                    ks=slice(kb * 128, (kb + 1) * 128), sl=i % NBUF,
                                    zb=ZB[i % 3], bb=BB[i % 3], ob=OB2[st],
                                    first=(kb == 4 * qt + 3), last=(kb == 0))

                    def stageA(ii):
                        gs = [geo(i) for i in ii]
                        for g in gs:
                            mm(g["zb"], banks[g["zb"]][:, g["c0"]:], kT[g["pb"]:g["pb"] + 64, g["ch"], g["ks"]],
                               qT[g["pb"]:g["pb"] + 64, g["ch"], g["qs"]], True, not g["diag"],
                               [("kT", g["kb"] // 4), ("qT", g["qt"])])
                        for g in gs:
                            if g["diag"]:
                                mm(g["zb"], banks[g["zb"]][:, g["c0"]:g["c0"] + 128], identb, nmaskb, False, True, ["cstb"])
                        for g in gs:
                            zb_, sl, c0 = g["zb"], g["sl"], g["c0"]
                            op("act", lambda h: h.activation(out=ebuf[sl][:, c0:], in_=banks[zb_][:, c0:], func=AF.Exp),
                               writes=[BK(zb_), ("ebuf", sl)])
                            if not g["last"]:
                                op("dve", lambda h: h.tensor_copy(out=zrow[sl][0:1, c0:], in_=banks[zb_][0:1, c0:]),
                                   writes=[BK(zb_), ("zrow", sl)])
                            op("act", lambda h: h.activation(out=spb[sl][:, c0:], in_=ebuf[sl][:, c0:], func=AF.Ln,
                                                             bias=1.0, scale=1.0),
                               reads=[("ebuf", sl)], writes=[("spb", sl)])

                    def stageB(ii):
                        gs = [geo(i) for i in ii]
                        for g in gs:
                            mm(g["bb"], banks[g["bb"]][:, g["c0"]:], trib, spb[g["sl"]][:, g["c0"]:], True, False,
                               ["cstb", ("spb", g["sl"])])
                        for g in gs:
                            mm(g["bb"], banks[g["bb"]][:, g["c0"]:], nkT[g["pb"]:g["pb"] + 64, g["ch"], g["ks"]],
                               qT[g["pb"]:g["pb"] + 64, g["ch"], g["qs"]], False, False,
                               [("nkT", g["kb"] // 4), ("qT", g["qt"])])
                        for g in gs:
                            if g["diag"]:
                                mm(g["bb"], banks[g["bb"]][:, g["c0"]:g["c0"] + 128], identb, pmaskb, False, g["first"],
                                   ["cstb"])
                        for g in gs:
                            if not g["first"]:
                                mm(g["bb"], banks[g["bb"]][:, g["c0"]:], onesr[:], A16[g["st"]][0:1, g["c0"]:], False,
                                   True, ["onesr", ("A16", g["st"])])
                        for g in gs:
                            bb_, sl, c0, st = g["bb"], g["sl"], g["c0"], g["st"]
                            op("act", lambda h: h.activation(out=Pb[sl][:, c0:], in_=banks[bb_][:, c0:], func=AF.Exp,
                                                             scale=-1.0),
                               writes=[BK(bb_), ("Pb", sl)])
                            if g["first"]:
                                op("dve", lambda h: h.memset(A16[st][:], 0.0), writes=[("A16", st)])
                            if not g["last"]:
                                op("dve", lambda h: h.tensor_tensor(out=A16[st][0:1, c0:], in0=banks[bb_][0:1, c0:],
                                                                    in1=zrow[sl][0:1, c0:], op=ALU.add),
                                   reads=[("zrow", sl)], writes=[BK(bb_), ("A16", st)])

                    def stageC(ii):
                        gs = [geo(i) for i in ii]
                        for g in gs:
                            ob, sl, c0, pb, hd, kb = g["ob"], g["sl"], g["c0"], g["pb"], g["hd"], g["kb"]
                            mm(ob, banks[ob][pb:pb + 64, c0:], Vt[:, kb, hd * 64:(hd + 1) * 64], Pb[sl][:, c0:],
                               g["first"], g["last"], [("Pb", sl), ("Vt", kb)], skip_group_check=True)
                        for g in gs:
                            if g["last"]:
                                ob, pb, ch, qt = g["ob"], g["pb"], g["ch"], g["qt"]
                                op("dve", lambda h: h.tensor_copy(out=catB[pb:pb + 64, ch, qt * 512:(qt + 1) * 512],
                                                                  in_=banks[ob][pb:pb + 64, :]),
                                   writes=[BK(ob), ("catB", ch, qt, pb)])

                    n_st = len(items) // 2
                    for step in range(n_st + 2):
                        if step < n_st:
                            stageA((2 * step, 2 * step + 1))
                        if 0 <= step - 1 < n_st:
                            stageB((2 * (step - 1), 2 * (step - 1) + 1))
                        if 0 <= step - 2 < n_st:
                            stageC((2 * (step - 2), 2 * (step - 2) + 1))

                with c.scope():
                    N = 512
                    if dbg:
                        c.dma("sp", dC_v[:, 0:4, T0:T0 + L], catA[:], writes=["dC0"], key="k_dbg")
                        c.dma("sp", dC_v[:, 4:8, T0:T0 + L], catB[:], writes=["dC1"], key="k_dbg")
                    w_out = c.sb([128, 8, 1024], BF16, "w_out")
                    load_w(w_out, w_out_d, 8, 1024, "k_w_out", "w_out")
                    lnb = ln_scratch(N)
                    ztD = [c.sb([128, 8, N], F32, f"ztD{i}") for i in range(2)]
                    sin_t = c.sb([128, 8, N], F32, "sinD")
                    xbt = c.sb([128, 8, N], BF16, "xbtD")

                    def d_mm(tt_):
                        tok0 = T0 + tt_ * N
                        cs = slice(tt_ * N, (tt_ + 1) * N)
                        z = ztD[tt_ % 2]
                        zk = ("ztD", tt_ % 2)
                        c.dma("sp", sin_t[:], S_v[:, :, tok0:tok0 + N], writes=["sinD"], key="k_sinD")
                        for oc in range(8):
                            b_ = nb()
                            for kc in range(8):
                                src = catA[:, kc, cs] if kc < 4 else catB[:, kc - 4, cs]
                                mm(b_, banks[b_][:, :], w_out[:, kc, oc * 128:(oc + 1) * 128], src,
                                   kc == 0, kc == 7, wreads("w_out", kc, 1024))
                            op("dve", lambda h: h.tensor_tensor(out=z[:, oc, :], in0=banks[b_][:, :],
                                                                in1=sin_t[:, oc, :], op=ALU.add),
                               reads=["sinD"], writes=[BK(b_), zk])

                    def d_ln(tt_):
                        tok0 = T0 + tt_ * N
                        z = ztD[tt_ % 2]
                        zk = ("ztD", tt_ % 2)
                        layer_norm(z, N, zk, z, zk, lnb)
                        ln_epilogue_stream(z, zk, N, tok0, ga10, ba10, "ln1_g0", "ln1_b0", z, xbt, 0, sstkey=zk)

                    for tt_ in range(4):
                        d_mm(tt_)
                        if tt_ > 0:
                            d_ln(tt_ - 1)
                    d_ln(3)


    def dbg_stop(tag):
        if dbg and dbg.get("stop") == tag:
            c.barrier()
            c.dma("sp", dS_d, S_d, writes=["dS"], key="k_dbg")
            c.dma("sp", dX_d, XB_d, writes=["dX"], key="k_dbg")
            c.barrier()
            c.emit()
            c.close()
            return True
        return False

    if dbg_stop("D"):
        return nc
    H2_d = nc.dram_tensor("H2_scr", [D, T], BF16, kind="Internal").ap()
    H2_v = H2_d.rearrange("(c p) t -> p c t", p=128)

    def ffn_phase(layer, final, ga, ba, gname, bname):
        N = 512
        NTL = T // N
        NQ = 4
        b1n = f"fb1_{layer}"
        with c.scope():
            w1s = [c.sb([128, 8, 1024], BF16, f"w1s{i}") for i in range(2)]
            w2s = [c.sb([128, 8, 1024], BF16, f"w2s{i}") for i in range(2)]
            zt = [c.sb([128, 8, N], F32, f"ztE{i}") for i in range(2)]
            xin = [c.sb([128, 8, N], BF16, f"xinE{i}") for i in range(2)]
            sin2 = [c.sb([128, 8, N], F32, f"sinE{i}") for i in range(2)]
            hb = c.sb([128, 8, N], BF16, "hb")
            rl = [c.sb([128, N], F32, f"rl{i}") for i in range(2)]
            lnb = ln_scratch(N)
            xbt = c.sb([128, 8, N], BF16, "xbtE")
            yo = c.sb([128, D], F32, "yo")

            def load_q(qr):
                sl = qr % 2
                load_w(w1s[sl], w1_d[layer], 8, 1024, f"k_w1s{sl}", ("w1s", sl), cs0=qr * 1024)
                load_w(w2s[sl], w2_d[layer][qr * 1024:(qr + 1) * 1024, :], 8, 1024, f"k_w2s{sl}", ("w2s", sl))

            def part_h(qr, tt_):
                sl = tt_ % 2
                ws = qr % 2
                tok0 = tt_ * N
                for hc in range(8):
                    b_ = nb()
                    s2 = hc % 2
                    hcg = qr * 8 + hc
                    for kc in range(8):
                        mm(b_, banks[b_][:, :], w1s[ws][:, kc, hc * 128:(hc + 1) * 128], xin[sl][:, kc, :],
                           kc == 0, kc == 7, wreads(("w1s", ws), kc, 1024) + [("xinE", sl)])
                    op("act", lambda h: h.activation(out=rl[s2][:], in_=banks[b_][:, :], func=AF.Relu,
                                                     bias=V(b1n, hcg, 1), scale=1.0),
                       reads=["vec"], writes=[BK(b_), ("rl", s2)])
                    op("dve", lambda h: h.scalar_tensor_tensor(
                        out=hb[:, hc, :], in0=banks[b_][:, :], scalar=V(b1n, hcg, 1), in1=rl[s2][:],
                        op0=ALU.add, op1=ALU.mult), reads=["vec", ("rl", s2)], writes=[BK(b_), ("hb", hc)])

            def part_out(qr, tt_, mode="store"):
                ws = qr % 2
                tok0 = tt_ * N
                z = zt[tt_ % 2]
                zk = ("ztE", tt_ % 2)
                sin_t = sin2[tt_ % 2]
                sk_ = ("sinE", tt_ % 2)
                for oc in range(8):
                    b_ = nb()
                    for hc in range(8):
                        mm(b_, banks[b_][:, :], w2s[ws][:, hc, oc * 128:(oc + 1) * 128], hb[:, hc, :],
                           hc == 0, hc == 7, wreads(("w2s", ws), hc, 1024) + [("hb", hc)])
                    if mode == "acc":
                        op("dve", lambda h: h.tensor_tensor(out=z[:, oc, :], in0=banks[b_][:, :], in1=z[:, oc, :],
                                                            op=ALU.add), reads=[], writes=[BK(b_), zk])
                    else:
                        op("dve", lambda h: h.tensor_tensor(out=z[:, oc, :], in0=banks[b_][:, :], in1=sin_t[:, oc, :],
                                                            op=ALU.add), reads=[sk_], writes=[BK(b_), zk])
                if mode == "store":
                    c.dma("pool", S_v[:, :, tok0:tok0 + N], z[:], reads=[zk], writes=[("S_d", tok0)],
                          key=f"k_zpart{tt_ % 2}")

            def loads(qr, tt_):
                sl = tt_ % 2
                tok0 = tt_ * N
                c.dma("sp", xin[sl][:], XB_v[:, :, tok0:tok0 + N], writes=[("xinE", sl)], key=f"k_xinE{sl}")
                c.dma("sp", sin2[sl][:], S_v[:, :, tok0:tok0 + N], reads=[("S_d", tok0)], writes=[("sinE", sl)],
                      key=f"k_sinE{sl}")

            def part_ln(tt_):
                tok0 = tt_ * N
                z = zt[tt_ % 2]
                zk = ("ztE", tt_ % 2)
                layer_norm(z, N, zk, z, zk, lnb)
                if not final:
                    ln_epilogue_stream(z, zk, N, tok0, ga, ba, gname, bname, z, xbt, 1, sstkey=zk)
                else:
                    for kc in range(8):
                        op("act", lambda h, kc=kc: h.activation(out=z[:, kc, :], in_=z[:, kc, :], func=AF.Identity,
                                                                bias=V(bname, kc, 1), scale=V(gname, kc, 1)),
                           reads=["vec"], writes=[zk])
                    for bl in range(N // 128):
                        b0, b1 = nb(), nb()
                        for fc in range(8):
                            bq_ = b0 if fc < 4 else b1
                            op("pe", lambda h, fc=fc, bq_=bq_: h.transpose(
                                out=banks[bq_][:, (fc % 4) * 128:(fc % 4) * 128 + 128],
                                in_=z[:, fc, bl * 128:(bl + 1) * 128], identity=ident),
                                reads=[zk, "cst"], writes=[BK(bq_)])
                        op("act", lambda h: h.activation(out=yo[:, 0:512], in_=banks[b0][:, :], func=AF.Copy),
                           writes=[BK(b0), "yo"])
                        op("dve", lambda h: h.tensor_copy(out=yo[:, 512:1024], in_=banks[b1][:, :]),
                           writes=[BK(b1), "yo"])
                        c.dma("sp", y_d[tok0 + bl * 128:tok0 + (bl + 1) * 128, :], yo[:],
                              reads=["yo"], writes=[("y", tok0, bl)], key="k_yo")

            load_q(0)
            load_q(1)
            for qr in range(2):
                loads(qr, 0)
                for tt_ in range(NTL):
                    if tt_ + 1 < NTL:
                        loads(qr, tt_ + 1)
                    part_h(qr, tt_)
                    part_out(qr, tt_, "store")
                    if qr == 0 and tt_ == NTL - 1:
                        load_q(2)
            load_q(3)
            loads(2, 0)
            for tt_ in range(NTL):
                if tt_ + 1 < NTL:
                    loads(2, tt_ + 1)
                part_h(2, tt_)
                part_out(2, tt_, "keep")
                part_h(3, tt_)
                if tt_ > 0:
                    part_ln(tt_ - 1)
                part_out(3, tt_, "acc")
            part_ln(NTL - 1)

    ffn_phase(0, False, ga20, ba20, "ln2_g0", "ln2_b0")
    if dbg_stop("E0"):
        return nc

    N = 512
    NTL = T // N
    with c.scope():
        pw1 = c.sb([128, 8, 2048], BF16, "pw1")
        load_w(pw1, pw1_d, 8, 2048, "k_pw1", "pw1")
        dg = c.sb([128, 8, 31, 128], BF16, "dg")
        for oc in range(8):
            for k in range(31):
                op("dve", lambda h, oc=oc, k=k: h.tensor_scalar(out=dg[:, oc, k, :], in0=ident,
                                                                scalar1=V("w_dw", k * 8 + oc, 1), scalar2=None,
                                                                op0=ALU.mult), reads=["cst", "vec"], writes=[("dg", oc)])
        lnb = ln_scratch(N)
        xin = [c.sb([128, 8, N], BF16, f"xinF{i}") for i in range(2)]
        hbuf = c.sb([128, 8, 30 + N], BF16, "hbuf")
        sgF = [c.sb([128, N], F32, f"sgF{i}") for i in range(2)]
        cvs = [c.sb([128, 8, N], F32, f"cv{i}") for i in range(2)]
        h2 = c.sb([128, 8, N], BF16, "h2")

        def f1_load(tt_):
            sl = tt_ % 2
            c.dma("sp", xin[sl][:], XB_v[:, :, tt_ * N:(tt_ + 1) * N], writes=[("xinF", sl)], key=f"k_xinF{sl}")

        def f1_glu(tt_):
            sl = tt_ % 2
            if tt_ % 4 == 0:
                op("dve", lambda h: h.memset(hbuf[:, :, 0:30], 0.0), writes=[("hbuf", i) for i in range(8)])
            else:
                for oc in range(8):
                    op("dve", lambda h, oc=oc: h.tensor_copy(out=hbuf[:, oc, 0:30], in_=hbuf[:, oc, N:N + 30]),
                       reads=[], writes=[("hbuf", oc)])
            for oc in range(8):
                bv_, bg_ = nb(), nb()
                s2 = oc % 2
                for kc in range(8):
                    mm(bv_, banks[bv_][:, :], pw1[:, kc, oc * 128:(oc + 1) * 128], xin[sl][:, kc, :],
                       kc == 0, kc == 7, wreads("pw1", kc, 2048) + [("xinF", sl)])
                for kc in range(8):
                    mm(bg_, banks[bg_][:, :], pw1[:, kc, 1024 + oc * 128:1024 + (oc + 1) * 128], xin[sl][:, kc, :],
                       kc == 0, kc == 7, wreads("pw1", kc, 2048) + [("xinF", sl)])
                op("act", lambda h: h.activation(out=sgF[s2][:], in_=banks[bg_][:, :], func=AF.Sigmoid,
                                                 bias=V("b_pw1", 8 + oc, 1), scale=1.0),
                   reads=["vec"], writes=[BK(bg_), ("sgF", s2)])
                op("dve", lambda h: h.scalar_tensor_tensor(
                    out=hbuf[:, oc, 30:30 + N], in0=banks[bv_][:, :], scalar=V("b_pw1", oc, 1), in1=sgF[s2][:],
                    op0=ALU.add, op1=ALU.mult), reads=["vec", ("sgF", s2)], writes=[BK(bv_), ("hbuf", oc)])

        def f1_conv(tt_):
            cv = cvs[tt_ % 2]
            for oc in range(8):
                b_ = nb()
                for k in range(31):
                    mm(b_, banks[b_][:, :], dg[:, oc, k, :], hbuf[:, oc, k:k + N], k == 0, k == 30,
                       [("dg", oc), ("hbuf", oc)])
                op("act", lambda h: h.activation(out=cv[:, oc, :], in_=banks[b_][:, :], func=AF.Identity,
                                                 bias=V("b_dw", oc, 1), scale=1.0),
                   reads=["vec"], writes=[BK(b_), ("cv", tt_ % 2)])

        def f1_ln(tt_):
            cv = cvs[tt_ % 2]
            ck = ("cv", tt_ % 2)
            layer_norm(cv, N, ck, cv, ck, lnb)
            for oc in range(8):
                op("act", lambda h, oc=oc: h.activation(out=h2[:, oc, :], in_=cv[:, oc, :], func=AF.Silu,
                                                        bias=V("cln_b", oc, 1), scale=V("cln_g", oc, 1)),
                   reads=[ck, "vec"], writes=["h2"])
            c.dma("pool", H2_v[:, :, tt_ * N:(tt_ + 1) * N], h2[:], reads=["h2"], writes=[("H2_d", tt_)], key="k_h2")

        f1_load(0)
        for tt_ in range(NTL):
            if tt_ + 1 < NTL:
                f1_load(tt_ + 1)
            f1_glu(tt_)
            if tt_ > 0:
                f1_ln(tt_ - 1)
            f1_conv(tt_)
        f1_ln(NTL - 1)
    with c.scope():
        pw2 = c.sb([128, 8, 1024], BF16, "pw2")
        load_w(pw2, pw2_d, 8, 1024, "k_pw2", "pw2")
        lnb = ln_scratch(N)
        h2i = [c.sb([128, 8, N], BF16, f"h2i{i}") for i in range(2)]
        sin2 = [c.sb([128, 8, N], F32, f"sinF{i}") for i in range(2)]
        zts = [c.sb([128, 8, N], F32, f"ztF{i}") for i in range(2)]
        xbt = c.sb([128, 8, N], BF16, "xbtF")

        def f2_load(tt_):
            sl = tt_ % 2
            tok0 = tt_ * N
            c.dma("sp", h2i[sl][:], H2_v[:, :, tok0:tok0 + N], writes=[("h2i", sl)], key=f"k_h2i{sl}")
            c.dma("sp", sin2[sl][:], S_v[:, :, tok0:tok0 + N], writes=[("sinF", sl)], key=f"k_sinF{sl}")

        def f2_mm(tt_):
            sl = tt_ % 2
            z = zts[sl]
            for oc in range(8):
                b_ = nb()
                for kc in range(8):
                    mm(b_, banks[b_][:, :], pw2[:, kc, oc * 128:(oc + 1) * 128], h2i[sl][:, kc, :], kc == 0, kc == 7,
                       wreads("pw2", kc, 1024) + [("h2i", sl)])
                op("dve", lambda h: h.tensor_tensor(out=z[:, oc, :], in0=banks[b_][:, :], in1=sin2[sl][:, oc, :],
                                                    op=ALU.add), reads=[("sinF", sl)], writes=[BK(b_), ("ztF", sl)])

        def f2_ln(tt_):
            sl = tt_ % 2
            z = zts[sl]
            zk = ("ztF", sl)
            layer_norm(z, N, zk, z, zk, lnb)
            ln_epilogue_stream(z, zk, N, tt_ * N, ga11, ba11, "ln1_g1", "ln1_b1", z, xbt, 2, sstkey=zk)

        f2_load(0)
        for tt_ in range(NTL):
            if tt_ + 1 < NTL:
                f2_load(tt_ + 1)
            f2_mm(tt_)
            if tt_ > 0:
                f2_ln(tt_ - 1)
        f2_ln(NTL - 1)

    if dbg and dbg.get("stop") == "F":
        c.barrier()
        c.dma("sp", dC_d, H2_d, writes=["dC0"], key="k_dbg")
    if dbg_stop("F"):
        return nc
    ffn_phase(1, True, None, None, "ln2_g1", "ln2_b1")
    if dbg:
        dbg_stop(dbg.get("stop"))
        return nc

    c.barrier()
    c.emit()
    c.close()
    return nc


def _col(v):
    v = np.asarray(v, np.float32).reshape(-1, 128)
    return np.ascontiguousarray(v.T)


def _host_layout(inp):
    f = lambda a: np.ascontiguousarray(np.asarray(a, np.float32))
    vec = np.zeros((128, NV), np.float32)

    def put(name, arr):
        o, w = VOFF[name]
        vec[:, o:o + w] = arr

    for l in range(2):
        put(f"ln1_g{l}", _col(inp["ln1_g"][l])); put(f"ln1_b{l}", _col(inp["ln1_b"][l]))
        put(f"ln2_g{l}", _col(inp["ln2_g"][l])); put(f"ln2_b{l}", _col(inp["ln2_b"][l]))
        put(f"fb1_{l}", _col(inp["ffn_b1"][l])); put(f"fb2_{l}", _col(inp["ffn_b2"][l]))
    put("b_in", _col(inp["mix_b_in"][0][:1536]))
    put("s5_d", _col(inp["s5_d"][0])); put("b_glu", _col(inp["s5_b_glu"][0])); put("b_out", _col(inp["mix_b_out"][0]))
    put("b_pw1", _col(inp["conv_b_pw1"][0])); put("b_dw", _col(inp["conv_b_dw"][0]))
    put("cln_g", _col(inp["conv_ln_g"][0])); put("cln_b", _col(inp["conv_ln_b"][0]))
    put("b_pw2", _col(inp["conv_b_pw2"][0]))
    wd = np.asarray(inp["conv_w_dw"][0], np.float32)
    put("w_dw", np.concatenate([_col(wd[k]) for k in range(31)], axis=1))
    bv = np.ascontiguousarray(np.broadcast_to(np.asarray(inp["mix_b_in"][0][1536:], np.float32)[None, :], (128, 512)))
    cst = np.zeros((128, 640), np.float32)
    cst[:, 0:128] = np.eye(128)
    j = np.arange(128)[:, None]; s = np.arange(128)[None, :]
    cst[:, 128:256] = (j >= s)
    cst[:, 256:384] = np.where(j >= s, -30000.0, 0.0)
    cst[:, 384:512] = np.where(j >= s, 30000.0, 0.0)
    cst[:, 512:640] = 1.0 / 1024.0
    lr = np.asarray(inp["s5_lambda_re"][0], np.float32); li = np.asarray(inp["s5_lambda_im"][0], np.float32)
    ld = np.asarray(inp["s5_log_dt"][0], np.float32)
    lam = np.zeros((128, 3, 16), np.float32)
    for k in range(16):
        for g2 in range(2):
            g = 2 * k + g2
            lam[g2 * 64:(g2 + 1) * 64, 0, k] = lr[g]
            lam[g2 * 64:(g2 + 1) * 64, 1, k] = li[g]
            lam[g2 * 64:(g2 + 1) * 64, 2, k] = ld[g]

    def pad_layout(arr_gnp):
        out = np.zeros((128, 16, 128), np.float32)
        for k in range(16):
            for g2 in range(2):
                g = 2 * k + g2
                c0 = 16 * (g % 8)
                out[g2 * 64:(g2 + 1) * 64, k, c0:c0 + 16] = arr_gnp[g]
        return out

    bre = np.asarray(inp["s5_b_re"][0], np.float32); bim = np.asarray(inp["s5_b_im"][0], np.float32)
    cre = np.asarray(inp["s5_c_re"][0], np.float32).transpose(0, 2, 1)
    cim = np.asarray(inp["s5_c_im"][0], np.float32).transpose(0, 2, 1)
    shared = {
        "w_in": f(inp["mix_w_in"][0]), "w_glu": f(inp["s5_w_glu"][0]), "w_out": f(inp["mix_w_out"][0]),
        "pw1": f(inp["conv_w_pw1"][0]), "pw2": f(inp["conv_w_pw2"][0]),
        "w1_0": f(inp["ffn_w1"][0]), "w1_1": f(inp["ffn_w1"][1]),
        "w2_0": f(inp["ffn_w2"][0]), "w2_1": f(inp["ffn_w2"][1]),
        "vecs": vec, "bv_bc": bv, "consts": cst, "lamll": lam,
        "bt_re": pad_layout(bre), "bt_im": pad_layout(bim), "cp_re": pad_layout(cre), "cp_im": pad_layout(cim),
    }
    return shared


def kernel(**inputs):
    x = np.asarray(inputs["x"], np.float32)
    shared = _host_layout(inputs)
    nc = build()
    in_maps = []
    for i in range(8):
        m = dict(shared)
        m["x"] = np.ascontiguousarray(x[2 * i:2 * i + 2].reshape(T, D))
        in_maps.append(m)
    res = run_bass_kernel_spmd(nc, in_maps, core_ids=list(range(8)))
    out = np.concatenate([r["y"].reshape(2, L, D) for r in res.results], axis=0)
    return out.astype(np.float32)
```

```python
import contextlib
import numpy as np
import concourse.bass as bass
import concourse.mybir as mybir
from concourse.bass_utils import run_bass_kernel_spmd

F32 = mybir.dt.float32
BF16 = mybir.dt.bfloat16
I32 = mybir.dt.int32
AF = mybir.ActivationFunctionType
ALU = mybir.AluOpType

D = 1024
L = 2048
NSEQ = 2
T = NSEQ * L
ALPHA = 4.0 ** 0.25
EPS = 1e-5
PI = float(np.pi)
PAD = 1024


class _Rec:
    def __init__(self):
        self.call = None

    def __getattr__(self, name):
        def f(*a, **kw):
            self.call = (name, a, kw)
            return self
        return f


class Ctx:
    def __init__(self, nc):
        self.nc = nc
        self.es = contextlib.ExitStack()
        self.stacks = [self.es]
        self.engs = ["pe", "act", "dve", "pool", "sp"]
        self.sem = {}
        self.cnt = {}
        for k in self.engs:
            self.sem[k] = self.es.enter_context(nc.semaphore("s_" + k))
            self.cnt[k] = 0
        self.waited = {k: {} for k in self.engs}
        self.lastw = {}
        self.reads = {}
        self.dsem = {}
        self.dcnt = {}
        self.nsb = 0
        self.prog = {k: [] for k in self.engs}

    def sb(self, shape, dt, name=None):
        self.nsb += 1
        return self.stacks[-1].enter_context(
            self.nc.sbuf_tensor(f"{name or 'sb'}_{self.nsb}", list(shape), dt))

    def ps(self, shape, dt, name=None):
        self.nsb += 1
        return self.stacks[-1].enter_context(
            self.nc.psum_tensor(f"{name or 'ps'}_{self.nsb}", list(shape), dt))

    @contextlib.contextmanager
    def scope(self):
        st = contextlib.ExitStack()
        self.stacks.append(st)
        try:
            yield
        finally:
            self.barrier()
            self.stacks.pop()
            st.close()

    def _semobj(self, key):
        return self.sem[key] if key in self.sem else self.dsem[key]

    def _deps(self, reads, writes):
        deps = []
        for r in reads:
            if r in self.lastw:
                deps.append(self.lastw[r])
        for w in writes:
            if w in self.lastw:
                deps.append(self.lastw[w])
            deps.extend(self.reads.get(w, []))
        return deps

    def _wait(self, e, deps):
        best = {}
        for (k, v) in deps:
            if e == "pe" and k == "pe":
                continue
            if v > best.get(k, 0):
                best[k] = v
        for k, v in best.items():
            if self.waited[e].get(k, 0) >= v:
                continue
            so = self._semobj(k)
            self.prog[e].append(lambda h, so=so, v=v: h.wait_ge(so, v))
            self.waited[e][k] = v

    def _record(self, ticket, reads, writes):
        for r in reads:
            self.reads.setdefault(r, []).append(ticket)
        for w in writes:
            self.lastw[w] = ticket
            self.reads[w] = []

    def op(self, e, fn, reads=(), writes=()):
        self._wait(e, self._deps(reads, writes))
        self.cnt[e] += 1
        so = self.sem[e]
        rec = _Rec()
        fn(rec)
        name, a, kw = rec.call
        self.prog[e].append(lambda h, name=name, a=a, kw=kw, so=so: getattr(h, name)(*a, **kw).then_inc(so, 1))
        t = (e, self.cnt[e])
        self._record(t, reads, writes)
        return t

    def dma(self, q, out, in_, reads=(), writes=(), key=None, **kw):
        if key not in self.dsem:
            self.dsem[key] = self.es.enter_context(self.nc.semaphore(f"d{len(self.dsem)}"))
            self.dcnt[key] = 0
        self._wait(q, self._deps(reads, writes))
        so = self.dsem[key]
        self.prog[q].append(lambda h, out=out, in_=in_, kw=kw, so=so:
                            h.dma_start(out=out, in_=in_, **kw).then_inc(so, 16))
        self.dcnt[key] += 16
        t = (key, self.dcnt[key])
        self._record(t, reads, writes)
        return t

    def barrier(self):
        deps = [(k, self.cnt[k]) for k in self.engs if self.cnt[k] > 0]
        deps += [(k, v) for k, v in self.dcnt.items() if v > 0]
        for e in self.engs:
            self._wait(e, deps)
        self.lastw = {}
        self.reads = {}

    def emit(self):
        with self.nc.Block() as block:
            def mk(e):
                def body(h):
                    for f in self.prog[e]:
                        f(h)
                return body
            block.tensor(mk("pe"))
            block.scalar(mk("act"))
            block.vector(mk("dve"))
            block.gpsimd(mk("pool"))
            block.sync(mk("sp"))

    def close(self):
        self.es.close()


VEC_SPEC = [("ln1_g0", 8), ("ln1_b0", 8), ("ln2_g0", 8), ("ln2_b0", 8),
            ("ln1_g1", 8), ("ln1_b1", 8), ("ln2_g1", 8), ("ln2_b1", 8),
            ("fb1_0", 32), ("fb1_1", 32), ("fb2_0", 8), ("fb2_1", 8),
            ("b_in", 12), ("s5_d", 4), ("b_glu", 8), ("b_out", 8),
            ("b_pw1", 16), ("b_dw", 8), ("cln_g", 8), ("cln_b", 8), ("b_pw2", 8),
            ("w_dw", 31 * 8)]
VOFF = {}
_o = 0
for _n, _w in VEC_SPEC:
    VOFF[_n] = (_o, _w)
    _o += _w
NV = _o


def build(dbg=None):
    nc = bass.Bass("TRN2", target_bir_lowering=False)

    def din(name, shape, dt=F32):
        return nc.dram_tensor(name, list(shape), dt, kind="ExternalInput").ap()

    x_d = din("x", [T, D])
    w_in_d = din("w_in", [D, 2048])
    w_glu_d = din("w_glu", [512, 1024])
    w_out_d = din("w_out", [D, D])
    pw1_d = din("pw1", [D, 2048])
    pw2_d = din("pw2", [D, D])
    w1_d = [din("w1_0", [D, 4096]), din("w1_1", [D, 4096])]
    w2_d = [din("w2_0", [4096, D]), din("w2_1", [4096, D])]
    vec_d = din("vecs", [128, NV])
    bv_d = din("bv_bc", [128, 512])
    cst_d = din("consts", [128, 640])
    lam_d = din("lamll", [128, 3, 16])
    bt_d = [din("bt_re", [128, 16, 128]), din("bt_im", [128, 16, 128])]
    cp_d = [din("cp_re", [128, 16, 128]), din("cp_im", [128, 16, 128])]
    y_d = nc.dram_tensor("y", [T, D], F32, kind="ExternalOutput").ap()
    S_d = nc.dram_tensor("S_scr", [D, T], F32, kind="Internal").ap()
    XB_d = nc.dram_tensor("XB_scr", [D, T], BF16, kind="Internal").ap()
    if dbg:
        dS_d = nc.dram_tensor("dbgS", [D, T], F32, kind="ExternalOutput").ap()
        dX_d = nc.dram_tensor("dbgX", [D, T], BF16, kind="ExternalOutput").ap()
        dC_d = nc.dram_tensor("dbgC", [D, T], BF16, kind="ExternalOutput").ap()
        dC_v = dC_d.rearrange("(c p) t -> p c t", p=128)
    HG_d = nc.dram_tensor("HG_scr", [16, 2, 128, 2048], BF16, kind="Internal").ap()
    S_v = S_d.rearrange("(c p) t -> p c t", p=128)
    XB_v = XB_d.rearrange("(c p) t -> p c t", p=128)

    c = Ctx(nc)
    op = c.op

    vec = c.sb([128, NV], F32, "vec")
    cst = c.sb([128, 640], F32, "cst")
    cstb = c.sb([128, 640], BF16, "cstb")
    der = c.sb([128, 160], F32, "der")
    banks = [c.ps([128, 512], F32, f"bank{i}") for i in range(8)]
    ident = cst[:, 0:128]
    identb = cstb[:, 0:128]
    trib = cstb[:, 128:256]
    nmaskb = cstb[:, 256:384]
    pmaskb = cstb[:, 384:512]
    onesb = cstb[:, 512:640]

    def V(name, i=0, n=1):
        o, w = VOFF[name]
        return vec[:, o + i:o + i + n]

    c.dma("sp", vec[:], vec_d, writes=["vec"], key="k_vec")
    c.dma("sp", cst[:], cst_d, writes=["cst"], key="k_cst")
    op("dve", lambda h: h.tensor_copy(out=cstb[:], in_=cst[:]), reads=["cst"], writes=["cstb"])

    DER = {}
    _do = [0]

    def dalloc(name, n):
        DER[name] = _do[0]
        _do[0] += n
        return der[:, DER[name]:DER[name] + n]

    bq8 = dalloc("bq8", 4)
    op("dve", lambda h: h.tensor_scalar(out=bq8, in0=V("b_in", 4, 4), scalar1=0.125, scalar2=None,
                                        op0=ALU.mult), reads=["vec"], writes=["der"])

    def ln_consts(tag, gname, bname, nextb):
        ga = dalloc("ga" + tag, 8)
        ba = dalloc("ba" + tag, 8)
        op("dve", lambda h: h.tensor_scalar(out=ga, in0=V(gname, 0, 8), scalar1=ALPHA, scalar2=None,
                                            op0=ALU.mult), reads=["vec"], writes=["der"])
        op("dve", lambda h: h.scalar_tensor_tensor(out=ba, in0=V(bname, 0, 8), scalar=ALPHA,
                                                   in1=V(nextb, 0, 8), op0=ALU.mult, op1=ALU.add),
           reads=["vec"], writes=["der"])
        return ga, ba

    ga10, ba10 = ln_consts("10", "ln1_g0", "ln1_b0", "fb2_0")
    ga20, ba20 = ln_consts("20", "ln2_g0", "ln2_b0", "b_pw2")
    ga11, ba11 = ln_consts("11", "ln1_g1", "ln1_b1", "fb2_1")

    bkrr = [0]

    def nb(pool=(0, 1, 2, 3, 4, 5, 6, 7)):
        bkrr[0] += 1
        return pool[bkrr[0] % len(pool)]

    def BK(i):
        return ("bk", i)

    def mm(bank, out_ap, lhsT, rhs, start, stop, reads, **kw):
        op("pe", lambda h: h.matmul(out_ap, lhsT=lhsT, rhs=rhs, start=start, stop=stop, **kw),
           reads=reads, writes=[BK(bank)])

    def load_w(dst, src_d, kc_n, ncols, key, rkey, cs0=0, c0b=0):
        sv = src_d.rearrange("(c p) f -> p c f", p=128)
        step = 2048
        for kc in range(kc_n):
            for c0 in range(0, ncols, step):
                c1 = min(ncols, c0 + step)
                c.dma("pool", dst[:, kc, c0b + c0:c0b + c1], sv[:, kc, cs0 + c0:cs0 + c1],
                      writes=[(rkey, kc, c0)], key=key)
        fin = (key, c.dcnt[key])
        for kc in range(kc_n):
            for c0 in range(0, ncols, step):
                c.lastw[(rkey, kc, c0)] = fin

    def wreads(rkey, kc, ncols):
        return [(rkey, kc, c0) for c0 in range(0, ncols, 2048)]

    def layer_norm(zt, N, zkey, xh, xhkey, lnb):
        zb, zq, msq, var, rstd, nmr = lnb["zb"], lnb["zq"], lnb["msq"], lnb["var"], lnb["rstd"], lnb["nmr"]
        op("act", lambda h: h.activation(out=zb[:, :, 0:N], in_=zt[:, :, 0:N], func=AF.Copy),
           reads=[zkey], writes=["ln_zb"])
        op("act", lambda h: h.activation(out=zq[:, :, 0:N], in_=zt[:, :, 0:N], func=AF.Square),
           reads=[zkey], writes=["ln_zq"])
        bm, bq = nb(), nb()
        for kc in range(8):
            mm(bm, banks[bm][:, 0:N], onesb, zb[:, kc, 0:N], kc == 0, kc == 7, ["ln_zb", "cstb"])
        for kc in range(8):
            mm(bq, banks[bq][:, 0:N], onesb, zq[:, kc, 0:N], kc == 0, kc == 7, ["ln_zq", "cstb"])
        op("act", lambda h: h.activation(out=msq[:, 0:N], in_=banks[bm][:, 0:N], func=AF.Square),
           writes=[BK(bm), "ln_msq"])
        op("dve", lambda h: h.tensor_tensor(out=var[:, 0:N], in0=banks[bq][:, 0:N], in1=msq[:, 0:N],
                                            op=ALU.subtract), reads=["ln_msq"], writes=[BK(bq), "ln_var"])
        op("dve", lambda h: h.tensor_scalar(out=var[:, 0:N], in0=var[:, 0:N], scalar1=EPS, scalar2=None,
                                            op0=ALU.add), reads=["ln_var"], writes=["ln_var"])
        op("act", lambda h: h.activation(out=var[:, 0:N], in_=var[:, 0:N], func=AF.Ln),
           reads=["ln_var"], writes=["ln_var"])
        op("act", lambda h: h.activation(out=banks[bq][:, 0:N], in_=var[:, 0:N], func=AF.Exp, scale=-0.5),
           reads=["ln_var"], writes=[BK(bq)])
        mb = banks[bm][:, 0:N].unsqueeze(1).broadcast_to([128, 8, N])
        rb = banks[bq][:, 0:N].unsqueeze(1).broadcast_to([128, 8, N])
        op("dve", lambda h: h.tensor_tensor(out=xh[:, :, 0:N], in0=zt[:, :, 0:N], in1=mb, op=ALU.subtract),
           reads=[zkey], writes=[BK(bm), xhkey])
        op("dve", lambda h: h.tensor_tensor(out=xh[:, :, 0:N], in0=xh[:, :, 0:N], in1=rb, op=ALU.mult),
           reads=[], writes=[BK(bq), xhkey])

    def ln_scratch(N):
        return {"zb": c.sb([128, 8, N], BF16, "zb"), "zq": c.sb([128, 8, N], BF16, "zq"),
                "msq": c.sb([128, N], F32, "msq"), "var": c.sb([128, N], F32, "var"),
                "rstd": c.sb([128, N], F32, "rstd"), "nmr": c.sb([128, N], F32, "nmr")}

    def ln_epilogue_stream(xh, xhkey, N, tok0, ga, ba, gname, bname, sst, xbt, slot, sstkey=None, xb_eng="act"):
        sk = sstkey or ("sst", slot)
        for kc in range(8):
            if xb_eng == "dve":
                op("dve", lambda h, kc=kc: h.tensor_scalar(out=xbt[:, kc, 0:N], in0=xh[:, kc, 0:N],
                                                           scalar1=V(gname, kc, 1), scalar2=V(bname, kc, 1),
                                                           op0=ALU.mult, op1=ALU.add),
                   reads=[xhkey, "vec"], writes=[("xbt", slot)])
            else:
                op("act", lambda h, kc=kc: h.activation(out=xbt[:, kc, 0:N], in_=xh[:, kc, 0:N], func=AF.Identity,
                                                        bias=V(bname, kc, 1), scale=V(gname, kc, 1)),
                   reads=[xhkey, "vec"], writes=[("xbt", slot)])
        for kc in range(8):
            op("act", lambda h, kc=kc: h.activation(out=sst[:, kc, 0:N], in_=xh[:, kc, 0:N], func=AF.Identity,
                                                    bias=ba[:, kc:kc + 1], scale=ga[:, kc:kc + 1]),
               reads=[xhkey, "der"], writes=[sk])
        c.dma("pool", S_v[:, :, tok0:tok0 + N], sst[:, :, 0:N], reads=[sk],
              writes=[("S_d", tok0)], key=f"k_sst{slot}")
        c.dma("pool", XB_v[:, :, tok0:tok0 + N], xbt[:, :, 0:N], reads=[("xbt", slot)],
              writes=[("XB_d", tok0)], key=f"k_xbt{slot}")

    with c.scope():
        bv = c.sb([128, 512], F32, "bv")
        c.dma("sp", bv[:], bv_d, writes=["bv"], key="k_bv")

        lam = c.sb([128, 3, 16], F32, "lam")
        c.dma("sp", lam[:], lam_d, writes=["lam"], key="k_lam")
        tb = c.sb([128, 24, 16], F32, "s5tmp")
        tbi = c.sb([128, 16], I32, "s5tmpi")
        AR = c.sb([128, 11, 16], F32, "AR")
        AI = c.sb([128, 11, 16], F32, "AI")
        NAI = c.sb([128, 11, 16], F32, "NAI")
        TK = "s5t"

        def tt(out, a, b, o, e="dve"):
            op(e, lambda h: h.tensor_tensor(out=out, in0=a, in1=b, op=o), reads=[TK, "lam"], writes=[TK])

        def ts(out, a, s1, o1, s2=None, o2=None):
            if o2 is None:
                op("dve", lambda h: h.tensor_scalar(out=out, in0=a, scalar1=s1, scalar2=None, op0=o1),
                   reads=[TK, "lam"], writes=[TK])
            else:
                op("dve", lambda h: h.tensor_scalar(out=out, in0=a, scalar1=s1, scalar2=s2, op0=o1, op1=o2),
                   reads=[TK, "lam"], writes=[TK])

        def act(out, a, f, **kw):
            op("act", lambda h: h.activation(out=out, in_=a, func=f, **kw), reads=[TK, "lam"], writes=[TK])

        dt_, lr_, a_, th_, mag_ = tb[:, 0, :], tb[:, 1, :], tb[:, 2, :], tb[:, 3, :], tb[:, 4, :]
        act(dt_, lam[:, 2, :], AF.Exp)
        ts(lr_, lam[:, 0, :], -1e-4, ALU.min)
        tt(a_, lr_, dt_, ALU.mult)
        tt(th_, lam[:, 1, :], dt_, ALU.mult)
        act(mag_, a_, AF.Exp)

        def sin_of(out, src, shift):
            u, kf, r, g = tb[:, 5, :], tb[:, 6, :], tb[:, 7, :], tb[:, 8, :]
            ts(u, src, shift, ALU.add, 1.0 / (2 * PI), ALU.mult)
            op("dve", lambda h: h.tensor_copy(out=tbi[:], in_=u), reads=[TK], writes=[TK])
            op("dve", lambda h: h.tensor_copy(out=kf, in_=tbi[:]), reads=[TK], writes=[TK])
            ts(kf, kf, -2 * PI, ALU.mult, shift, ALU.add)
            tt(r, src, kf, ALU.add)
            ts(g, r, PI, ALU.is_gt, -2 * PI, ALU.mult)
            tt(r, r, g, ALU.add)
            ts(g, r, -PI, ALU.is_lt, 2 * PI, ALU.mult)
            tt(r, r, g, ALU.add)
            act(out, r, AF.Sin)

        sn_, cs_ = tb[:, 9, :], tb[:, 10, :]
        sin_of(sn_, th_, 0.0)
        sin_of(cs_, th_, PI / 2)
        tt(AR[:, 0, :], mag_, cs_, ALU.mult)
        tt(AI[:, 0, :], mag_, sn_, ALU.mult)
        for s in range(10):
            t1, t2 = tb[:, 11, :], tb[:, 12, :]
            tt(t1, AR[:, s, :], AR[:, s, :], ALU.mult)
            tt(t2, AI[:, s, :], AI[:, s, :], ALU.mult)
            tt(AR[:, s + 1, :], t1, t2, ALU.subtract)
            tt(t1, AR[:, s, :], AI[:, s, :], ALU.mult)
            ts(AI[:, s + 1, :], t1, 2.0, ALU.mult)
        ts(NAI[:], AI[:], -1.0, ALU.mult)
        den, nr, Fr, Fi = tb[:, 13, :], tb[:, 14, :], tb[:, 15, :], tb[:, 16, :]
        t1, t2 = tb[:, 11, :], tb[:, 12, :]
        tt(t1, lr_, lr_, ALU.mult)
        tt(t2, lam[:, 1, :], lam[:, 1, :], ALU.mult)
        tt(den, t1, t2, ALU.add)
        op("dve", lambda h: h.reciprocal(out=den, in_=den), reads=[TK], writes=[TK])
        ts(nr, AR[:, 0, :], -1.0, ALU.add)
        tt(t1, nr, lr_, ALU.mult)
        tt(t2, AI[:, 0, :], lam[:, 1, :], ALU.mult)
        tt(t1, t1, t2, ALU.add)
        tt(Fr, t1, den, ALU.mult)
        tt(t1, AI[:, 0, :], lr_, ALU.mult)
        tt(t2, nr, lam[:, 1, :], ALU.mult)
        tt(t1, t1, t2, ALU.subtract)
        tt(Fi, t1, den, ALU.mult)

        bb = [c.sb([128, 16, 128], F32, "bb_re"), c.sb([128, 16, 128], F32, "bb_im")]
        cp = [c.sb([128, 16, 128], F32, "cp_re"), c.sb([128, 16, 128], F32, "cp_im")]
        PR = c.sb([128, 9, 16], F32, "PR")
        PI_ = c.sb([128, 9, 16], F32, "PI")
        A8R = c.sb([128, 8, 16], F32, "A8R")
        A8I = c.sb([128, 8, 16], F32, "A8I")
        NA8I = c.sb([128, 8, 16], F32, "NA8I")
        Kblk = c.sb([128, 4, 8, 128], BF16, "Kblk")
        for i in range(2):
            c.dma("sp", cp[i][:], cp_d[i], writes=[("cp", i)], key=f"k_cp{i}")
        op("dve", lambda h: h.memset(PR[:, 0, :], 1.0), reads=[TK], writes=[TK])
        op("dve", lambda h: h.memset(PI_[:, 0, :], 0.0), reads=[TK], writes=[TK])
        ts(PR[:, 1, :], AR[:, 0, :], 1.0, ALU.mult)
        ts(PI_[:, 1, :], AI[:, 0, :], 1.0, ALU.mult)
        for j in range(1, 8):
            t1, t2 = tb[:, 11, :], tb[:, 12, :]
            tt(t1, PR[:, j, :], AR[:, 0, :], ALU.mult)
            tt(t2, PI_[:, j, :], AI[:, 0, :], ALU.mult)
            tt(PR[:, j + 1, :], t1, t2, ALU.subtract)
            tt(t1, PR[:, j, :], AI[:, 0, :], ALU.mult)
            tt(t2, PI_[:, j, :], AR[:, 0, :], ALU.mult)
            tt(PI_[:, j + 1, :], t1, t2, ALU.add)
        ts(A8R[:, 0, :], AR[:, 3, :], 1.0, ALU.mult)
        ts(A8I[:, 0, :], AI[:, 3, :], 1.0, ALU.mult)
        for s_ in range(1, 8):
            ts(A8R[:, s_, :], AR[:, 3 + s_, :], 1.0, ALU.mult)
            ts(A8I[:, s_, :], AI[:, 3 + s_, :], 1.0, ALU.mult)
        ts(NA8I[:], A8I[:], -1.0, ALU.mult)
        NPI = c.sb([128, 9, 16], F32, "NPI")
        ts(NPI[:], PI_[:], -1.0, ALU.mult)
        with c.scope():
            bt = [c.sb([128, 16, 128], F32, "bt_re"), c.sb([128, 16, 128], F32, "bt_im")]
            w1t = c.sb([128, 16, 128], F32, "w1t")
            w2t = c.sb([128, 16, 128], F32, "w2t")
            sre = c.sb([128, 16, 128], F32, "sre")
            sim = c.sb([128, 16, 128], F32, "sim")
            for i in range(2):
                c.dma("sp", bt[i][:], bt_d[i], writes=[("bt", i)], key=f"k_bt{i}")
            Frb = Fr.unsqueeze(2).broadcast_to([128, 16, 128])
            Fib = Fi.unsqueeze(2).broadcast_to([128, 16, 128])

            def t3(out, a, b, o, rk, wk, e="dve"):
                op(e, lambda h: h.tensor_tensor(out=out, in0=a, in1=b, op=o), reads=rk + [TK], writes=wk)

            t3(w1t[:], bt[0][:], Frb, ALU.mult, [("bt", 0)], ["w1t"])
            t3(w2t[:], bt[1][:], Fib, ALU.mult, [("bt", 1)], ["w2t"])
            t3(bb[0][:], w1t[:], w2t[:], ALU.subtract, ["w1t", "w2t"], [("bb", 0)])
            t3(w1t[:], bt[1][:], Frb, ALU.mult, [("bt", 1)], ["w1t"])
            t3(w2t[:], bt[0][:], Fib, ALU.mult, [("bt", 0)], ["w2t"])
            t3(bb[1][:], w1t[:], w2t[:], ALU.add, ["w1t", "w2t"], [("bb", 1)])
            for tau in range(8):
                prb = PR[:, tau, :].unsqueeze(2).broadcast_to([128, 16, 128])
                pib = PI_[:, tau, :].unsqueeze(2).broadcast_to([128, 16, 128])
                t3(w1t[:], bb[0][:], prb, ALU.mult, [("bb", 0)], ["w1t"])
                t3(w2t[:], bb[1][:], pib, ALU.mult, [("bb", 1)], ["w2t"])
                t3(sre[:], w1t[:], w2t[:], ALU.subtract, ["w1t", "w2t"], ["sre"])
                t3(w1t[:], bb[0][:], pib, ALU.mult, [("bb", 0)], ["w1t"])
                t3(w2t[:], bb[1][:], prb, ALU.mult, [("bb", 1)], ["w2t"])
                t3(sim[:], w1t[:], w2t[:], ALU.add, ["w1t", "w2t"], ["sim"])
                op("dve", lambda h: h.tensor_scalar(out=sim[:], in0=sim[:], scalar1=-1.0, scalar2=None, op0=ALU.mult),
                   reads=["sim"], writes=["sim"])
                for q in range(4):
                    b_ = nb()
                    n_ = 0
                    for kk in range(4):
                        for (a_, c_, ak) in ((sre, cp[0], "sre"), (sim, cp[1], "sim")):
                            mm(b_, banks[b_][:, 0:128], a_[:, 4 * q + kk, :], c_[:, 4 * q + kk, :], n_ == 0, n_ == 7,
                               [ak, ("cp", 0), ("cp", 1)])
                            n_ += 1
                    op("act", lambda h, q=q, tau=tau, b_=b_: h.activation(out=Kblk[:, q, tau, :],
                                                                          in_=banks[b_][:, 0:128], func=AF.Copy),
                       writes=[BK(b_), "Kblk"])


        for sq in range(NSEQ):
            with c.scope():
                T0 = sq * L
                catA = c.sb([128, 4, L], BF16, "catA")

                def phaseA(part, outs):
                    with c.scope():
                        ncol = 512 if part == 0 else 1536
                        w_in = c.sb([128, 8, ncol], BF16, "w_in")
                        load_w(w_in, w_in_d, 8, ncol, f"k_w_in{part}", "w_in", cs0=(0 if part == 0 else 512))
                        XBt = [c.sb([128, 8, 512], BF16, "XBt0"), c.sb([128, 8, 512], BF16, "XBt1")]
                        xt = [c.sb([128, D], F32, f"xt{i}") for i in range(4)]
                        if part == 0:
                            sst = [c.sb([128, 8, 128], F32, f"sstA{i}") for i in range(4)]
                        for tt_ in range(4):
                            XB = XBt[tt_ % 2]
                            xk = ("XB", tt_ % 2)
                            for bl in range(4):
                                tbk = tt_ * 4 + bl
                                sl = tbk % 4
                                tok = T0 + tbk * 128
                                c.dma("sp", xt[sl][:], x_d[tok:tok + 128, :], writes=[("xt", sl)], key=f"k_xt{sl}")
                                b0, b1 = nb(), nb()
                                for fc in range(8):
                                    bq_ = b0 if fc < 4 else b1
                                    op("pe", lambda h, fc=fc, bq_=bq_, sl=sl: h.transpose(
                                        out=banks[bq_][:, (fc % 4) * 128:(fc % 4) * 128 + 128],
                                        in_=xt[sl][:, fc * 128:(fc + 1) * 128], identity=ident),
                                        reads=[("xt", sl), "cst"], writes=[BK(bq_)])
                                for fc in range(8):
                                    bq_ = b0 if fc < 4 else b1
                                    src = banks[bq_][:, (fc % 4) * 128:(fc % 4) * 128 + 128]
                                    if part == 0:
                                        op("act", lambda h, fc=fc, src=src, sl=sl: h.activation(
                                            out=sst[sl][:, fc, :], in_=src, func=AF.Identity,
                                            bias=V("b_out", fc, 1), scale=ALPHA),
                                            reads=["vec"], writes=[BK(bq_), ("sstA", sl)])
                                    op("dve", lambda h, fc=fc, src=src, bl=bl: h.tensor_copy(
                                        out=XB[:, fc, bl * 128:(bl + 1) * 128], in_=src),
                                        writes=[BK(bq_), xk])
                                if part == 0:
                                    c.dma("pool", S_v[:, :, tok:tok + 128], sst[sl][:], reads=[("sstA", sl)],
                                          writes=[("S_d", tok)], key=f"k_sstA{sl}")
                            cs = slice(tt_ * 512, (tt_ + 1) * 512)
                            for oc in range(4 if part == 0 else 8):
                                b_ = nb()
                                for kc in range(8):
                                    mm(b_, banks[b_][:, :], w_in[:, kc, oc * 128:(oc + 1) * 128], XB[:, kc, :],
                                       kc == 0, kc == 7, wreads("w_in", kc, ncol) + [xk])
                                if part == 0:
                                    u_f = outs[0]
                                    op("act", lambda h, oc=oc, b_=b_, cs=cs: h.activation(
                                        out=u_f[:, oc, cs], in_=banks[b_][:, :], func=AF.Identity,
                                        bias=V("b_in", oc, 1), scale=1.0), reads=["vec"],
                                        writes=[BK(b_), ("u_f", tt_)])
                                elif oc < 4:
                                    qT = outs[0]
                                    op("act", lambda h, oc=oc, b_=b_, cs=cs: h.activation(
                                        out=qT[:, oc, cs], in_=banks[b_][:, :], func=AF.Identity,
                                        bias=bq8[:, oc:oc + 1], scale=0.125), reads=["der"],
                                        writes=[BK(b_), ("qT", tt_)])
                                else:
                                    kT, nkT = outs[1], outs[2]
                                    op("act", lambda h, oc=oc, b_=b_, cs=cs: h.activation(
                                        out=kT[:, oc - 4, cs], in_=banks[b_][:, :], func=AF.Identity,
                                        bias=V("b_in", 4 + oc, 1), scale=1.0), reads=["vec"],
                                        writes=[BK(b_), ("kT", tt_)])
                                    op("dve", lambda h, oc=oc, cs=cs: h.tensor_scalar(
                                        out=nkT[:, oc - 4, cs], in0=kT[:, oc - 4, cs], scalar1=-1.0, scalar2=None,
                                        op0=ALU.mult), reads=[("kT", tt_)], writes=[("nkT", tt_)])
                            if part == 1:
                                Vt = outs[3]
                                for bl in range(4):
                                    tbk = tt_ * 4 + bl
                                    b_ = nb()
                                    for kc in range(8):
                                        mm(b_, banks[b_][:, :], XB[:, kc, bl * 128:(bl + 1) * 128],
                                           w_in[:, kc, 1024:1536], kc == 0, kc == 7,
                                           wreads("w_in", kc, ncol) + [xk])
                                    op("dve", lambda h, tbk=tbk, b_=b_: h.tensor_tensor(
                                        out=Vt[:, tbk, :], in0=banks[b_][:, :], in1=bv[:], op=ALU.add),
                                        reads=["bv"], writes=[BK(b_), ("Vt", tbk)])

                with c.scope():
                    u_f = c.sb([128, 4, L], BF16, "u_f")
                    phaseA(0, [u_f])
                    w_glu = c.sb([128, 4, 1024], BF16, "w_glu")
                    load_w(w_glu, w_glu_d, 4, 1024, "k_w_glu", "w_glu")
                    YB = (0, 1, 2, 3)
                    WB = (4, 5, 6, 7)
                    XAs = [[c.sb([128, 2, 256], F32, f"XA{a}{b}") for b in range(2)] for a in range(2)]
                    Xc = [c.sb([128, 2, 256], BF16, f"Xc{i}") for i in range(2)]
                    Hf = [c.sb([128, 8, 2, 128], F32, f"Hf{i}") for i in range(2)]
                    Hp = [c.sb([128, 8, 2, 128], BF16, f"Hp{i}") for i in range(2)]
                    Gp = [c.sb([128, 8, 2, 128], BF16, f"Gp{i}") for i in range(4)]
                    tmpg = c.sb([128, 128], F32, "tmpg")
                    zfull = c.sb([128, L], F32, "zfull")
                    gfull = c.sb([128, L], F32, "gfull")
                    yg = c.sb([128, 4, L], BF16, "yg")
                    sg_ = [c.sb([128, 512], F32, "sg0"), c.sb([128, 512], F32, "sg1")]
                    tmpA = c.sb([128, 8, 128], F32, "tmpA")
                    tmpB = c.sb([128, 8, 128], F32, "tmpB")

                    def uq_of(q):
                        return u_f[:, q, :].rearrange("p (c i) -> p i c", i=8)

                    UK = [("u_f", t_) for t_ in range(4)]

                    def tab_ops(p):
                        k, sl = p, p % 2
                        th = []
                        rk = [TK, ("bb", 0), ("bb", 1), ("cp", 0), ("cp", 1)]
                        if sq == 1:
                            th.append(lambda: c.dma("sp", Hp[sl][:].rearrange("p a b n -> p (a b n)"), HG_d[p, 0],
                                                    reads=[("HG_d", p, 0)],
                                                    writes=[("Hp", sl, jj) for jj in range(8)], key=f"k_hpl{sl}"))
                            th.append(lambda: c.dma("sp", Gp[p % 4][:].rearrange("p a b n -> p (a b n)"), HG_d[p, 1],
                                                    reads=[("HG_d", p, 1)],
                                                    writes=[(("Gp", p % 4), i, r) for i in range(8) for r in range(2)],
                                                    key=f"k_gpl{p % 4}"))
                            return th

                        def add(fn, reads, writes):
                            th.append(lambda: op("dve", fn, reads=reads, writes=writes))

                        for (o, a_re, a_im, j0, wk, neg) in ((Hf[sl], bb[0][:, k, :], bb[1][:, k, :], 0, ("Hf", sl), False),
                                                             (Gp[p % 4], cp[0][:, k, :], cp[1][:, k, :], 1, ("Gp", p % 4),
                                                              True)):
                            tA = ("tmpA", neg)
                            tB = ("tmpB", neg)
                            ta = tmpA if not neg else tmpB
                            for j in range(8):
                                si = PI_[:, j0 + j, k:k + 1]
                                add(lambda h, j=j, si=si, ta=ta, a_im=a_im: h.tensor_scalar(
                                    out=ta[:, j, :], in0=a_im, scalar1=si, scalar2=None, op0=ALU.mult), rk, [(tA, j)])
                            for j in range(8):
                                sr = PR[:, j0 + j, k:k + 1]
                                add(lambda h, j=j, sr=sr, ta=ta, a_re=a_re, o=o: h.scalar_tensor_tensor(
                                    out=o[:, j, 0, :], in0=a_re, scalar=sr, in1=ta[:, j, :], op0=ALU.mult,
                                    op1=ALU.subtract), rk + [(tA, j)], [(wk, j, 0)])
                            for j in range(8):
                                sr = PR[:, j0 + j, k:k + 1]
                                if neg:
                                    add(lambda h, j=j, sr=sr, ta=ta, a_im=a_im: h.tensor_scalar(
                                        out=ta[:, j, :], in0=a_im, scalar1=sr, scalar2=-1.0, op0=ALU.mult, op1=ALU.mult),
                                        rk, [(tA, j)])
                                else:
                                    add(lambda h, j=j, sr=sr, ta=ta, a_im=a_im: h.tensor_scalar(
                                        out=ta[:, j, :], in0=a_im, scalar1=sr, scalar2=None, op0=ALU.mult), rk, [(tA, j)])
                            for j in range(8):
                                si = (NPI if neg else PI_)[:, j0 + j, k:k + 1]
                                add(lambda h, j=j, si=si, ta=ta, a_re=a_re, o=o: h.scalar_tensor_tensor(
                                    out=o[:, j, 1, :], in0=a_re, scalar=si, in1=ta[:, j, :], op0=ALU.mult, op1=ALU.add),
                                    rk + [(tA, j)], [(wk, j, 1)])
                        return th

                    def emit_transposes_and_V(p):
                        k, sl, q = p, p % 2, p // 4
                        uq = uq_of(q)
                        for jj in range(8 if sq == 0 else 0):
                            b_ = nb(WB)
                            for r in range(2):
                                op("pe", lambda h, jj=jj, r=r, b_=b_: h.transpose(
                                    out=banks[b_][:, r * 128:(r + 1) * 128], in_=Hf[sl][:, jj, r, :], identity=ident),
                                    reads=[(("Hf", sl), jj, r), "cst"], writes=[BK(b_)])
                            op("act", lambda h, jj=jj, b_=b_: h.activation(
                                out=Hp[sl][:, jj, :, :], in_=banks[b_][:, 0:256].rearrange("p (r n) -> p r n", r=2),
                                func=AF.Copy), writes=[BK(b_), ("Hp", sl, jj)])
                        if sq == 0:
                            c.dma("pool", HG_d[p, 0], Hp[sl][:].rearrange("p a b n -> p (a b n)"),
                                  reads=[("Hp", sl, jj) for jj in range(8)], writes=[("HG_d", p, 0)], key=f"k_hps{sl}")
                            c.dma("pool", HG_d[p, 1], Gp[p % 4][:].rearrange("p a b n -> p (a b n)"),
                                  reads=[(("Gp", p % 4), i, r) for i in range(8) for r in range(2)],
                                  writes=[("HG_d", p, 1)], key=f"k_gps{p % 4}")
                        b_ = nb(WB)
                        for r in range(2):
                            for j in range(8):
                                mm(b_, banks[b_][:, r * 256:(r + 1) * 256], Hp[sl][:, 7 - j, r, :], uq[:, j, :],
                                   j == 0 and r == 0, j == 7 and r == 1, [("Hp", sl, 7 - j)] + UK)
                        op("act", lambda h, b_=b_: h.activation(
                            out=XAs[sl][0][:], in_=banks[b_][:, :].rearrange("p (r n) -> p r n", r=2), func=AF.Copy),
                            writes=[BK(b_), ("XA", sl, 0)])

                    def ks_ops(p):
                        k, sl = p, p % 2
                        th = []
                        for s in range(8):
                            d = 1 << s
                            sa, da = s % 2, 1 - (s % 2)
                            src, dst = XAs[sl][sa], XAs[sl][da]
                            last = (s == 7)
                            o = Xc[sl] if last else dst
                            ok = ("Xc", sl) if last else ("XA", sl, da)
                            kr = [("XA", sl, sa)]

                            def stt(out, in0, sc, in1, rk, wk):
                                th.append(lambda: op("dve", lambda h: h.scalar_tensor_tensor(
                                    out=out, in0=in0, scalar=sc, in1=in1, op0=ALU.mult, op1=ALU.add),
                                    reads=rk + [TK], writes=wk))
                            th.append(lambda o=o, src=src, d=d, kr=kr, ok=ok: op(
                                "pool", lambda h: h.tensor_copy(out=o[:, :, 0:d], in_=src[:, :, 0:d]), reads=kr,
                                writes=[ok]))
                            stt(dst[:, 0, d:], src[:, 0, 0:256 - d], A8R[:, s, k:k + 1], src[:, 0, d:], kr, [("XA", sl, da)])
                            stt(dst[:, 1, d:], src[:, 0, 0:256 - d], A8I[:, s, k:k + 1], src[:, 1, d:], kr, [("XA", sl, da)])
                            stt(o[:, 0, d:], src[:, 1, 0:256 - d], NA8I[:, s, k:k + 1], dst[:, 0, d:],
                                kr + [("XA", sl, da)], [ok])
                            stt(o[:, 1, d:], src[:, 1, 0:256 - d], A8R[:, s, k:k + 1], dst[:, 1, d:],
                                kr + [("XA", sl, da)], [ok])
                        return th

                    def toeplitz(q):
                        uq = uq_of(q)
                        for b in range(4):
                            first = True
                            for i in (2 * b, 2 * b + 1):
                                for tau in range(i + 1):
                                    mm(YB[b], banks[YB[b]][:, (i % 2) * 256:(i % 2) * 256 + 256], Kblk[:, q, tau, :],
                                       uq[:, i - tau, :], first, False, ["Kblk"] + UK)
                                    first = False

                    def farfield(p):
                        sl, kk = p % 2, p % 4
                        for i in range(8):
                            for r in range(2):
                                mm(YB[i // 2], banks[YB[i // 2]][:, (i % 2) * 256 + 1:(i % 2) * 256 + 256],
                                   Gp[p % 4][:, i, r, :], Xc[sl][:, r, 0:255], False, (kk == 3 and i % 2 == 1 and r == 1),
                                   [(("Gp", p % 4), i, r), ("Xc", sl)])

                    def zgelu(q):
                        uq = uq_of(q)
                        zv = zfull[:, :].rearrange("p (c i) -> p i c", i=8)
                        for b in range(4):
                            op("dve", lambda h, b=b: h.scalar_tensor_tensor(
                                out=zv[:, 2 * b:2 * b + 2, :], in0=uq[:, 2 * b:2 * b + 2, :], scalar=V("s5_d", q, 1),
                                in1=banks[YB[b]][:, :].rearrange("p (i c) -> p i c", i=2), op0=ALU.mult, op1=ALU.add),
                                reads=UK + ["vec"], writes=[BK(YB[b]), "zfull"])
                        op("act", lambda h: h.activation(out=gfull[:], in_=zfull[:], func=AF.Square),
                           reads=["zfull"], writes=["gfull"])
                        op("dve", lambda h: h.tensor_scalar(out=gfull[:], in0=gfull[:], scalar1=0.044715, scalar2=1.0,
                                                            op0=ALU.mult, op1=ALU.add), reads=["gfull"], writes=["gfull"])
                        op("dve", lambda h: h.tensor_tensor(out=gfull[:], in0=gfull[:], in1=zfull[:], op=ALU.mult),
                           reads=["gfull", "zfull"], writes=["gfull"])
                        op("act", lambda h: h.activation(out=gfull[:], in_=gfull[:], func=AF.Sigmoid,
                                                         scale=1.5957691216057308), reads=["gfull"], writes=["gfull"])
                        op("dve", lambda h: h.tensor_tensor(out=yg[:, q, :], in0=gfull[:], in1=zfull[:], op=ALU.mult),
                           reads=["gfull", "zfull"], writes=[("yg", q, t_) for t_ in range(4)])

                    def interleave(*lists):
                        idx = [0] * len(lists)
                        more = True
                        while more:
                            more = False
                            for li, l_ in enumerate(lists):
                                if idx[li] < len(l_):
                                    l_[idx[li]]()
                                    idx[li] += 1
                                    more = True

                    interleave(tab_ops(0) + tab_ops(1))
                    for pp in range(8):
                        p0, p1 = 2 * pp, 2 * pp + 1
                        emit_transposes_and_V(p0)
                        emit_transposes_and_V(p1)
                        if p0 % 4 == 0:
                            toeplitz(p0 // 4)
                        nxt = (tab_ops(p0 + 2) + tab_ops(p1 + 2)) if p0 + 2 < 16 else []
                        interleave(ks_ops(p0), ks_ops(p1), nxt[0::2], nxt[1::2])
                        farfield(p0)
                        farfield(p1)
                        if p1 % 4 == 3:
                            zgelu(p1 // 4)
                    for tt_ in range(4):
                        cs = slice(tt_ * 512, (tt_ + 1) * 512)
                        for oc in range(4):
                            bv_, bg_ = nb(), nb()
                            sl = oc % 2
                            for kc in range(4):
                                mm(bv_, banks[bv_][:, :], w_glu[:, kc, oc * 128:(oc + 1) * 128], yg[:, kc, cs],
                                   kc == 0, kc == 3, wreads("w_glu", kc, 1024) + [("yg", kc, tt_)])
                            for kc in range(4):
                                mm(bg_, banks[bg_][:, :], w_glu[:, kc, 512 + oc * 128:512 + (oc + 1) * 128],
                                   yg[:, kc, cs], kc == 0, kc == 3,
                                   wreads("w_glu", kc, 1024) + [("yg", kc, tt_)])
                            op("act", lambda h, oc=oc, bg_=bg_, sl=sl: h.activation(
                                out=sg_[sl][:], in_=banks[bg_][:, :], func=AF.Sigmoid,
                                bias=V("b_glu", 4 + oc, 1), scale=1.0), reads=["vec"],
                                writes=[BK(bg_), ("sg", sl)])
                            op("dve", lambda h, oc=oc, bv_=bv_, sl=sl, cs=cs: h.scalar_tensor_tensor(
                                out=catA[:, oc, cs], in0=banks[bv_][:, :], scalar=V("b_glu", oc, 1), in1=sg_[sl][:],
                                op0=ALU.add, op1=ALU.mult), reads=["vec", ("sg", sl)],
                                writes=[BK(bv_), ("catA", oc, tt_)])

                catB = c.sb([128, 4, L], BF16, "catB")
                with c.scope():
                    qT = c.sb([128, 4, L], BF16, "qT")
                    kT = c.sb([128, 4, L], BF16, "kT")
                    nkT = c.sb([128, 4, L], BF16, "nkT")
                    Vt = c.sb([128, 16, 512], BF16, "Vt")
                    phaseA(1, [qT, kT, nkT, Vt])
                    ZB = (0, 1, 2)
                    BB = (3, 4, 5)
                    OB2 = (6, 7)
                    NBUF = 4
                    FP16 = mybir.dt.float16
                    ebuf = [c.sb([128, 512], F32, f"ebuf{i}") for i in range(NBUF)]
                    spb = [c.sb([128, 512], BF16, f"spb{i}") for i in range(NBUF)]
                    Pb = [c.sb([128, 512], BF16, f"Pb{i}") for i in range(NBUF)]
                    zrow = [c.sb([1, 512], F32, f"zrow{i}") for i in range(NBUF)]
                    A16 = [c.sb([1, 512], FP16, f"A16_{i}") for i in range(2)]
                    onesr = c.sb([1, 128], BF16, "onesr")
                    op("dve", lambda h: h.memset(onesr[:], 1.0), writes=["onesr"])
                    items = []
                    for m_ in range(4):
                        for qt in range(4):
                            for kb in range(4 * qt + 3, -1, -1):
                                for st in range(2):
                                    items.append((2 * m_ + st, qt, kb, st))

                    def geo(i):
                        hd, qt, kb, st = items[i]
                        r = max(0, kb - 4 * qt)
                        return dict(hd=hd, qt=qt, kb=kb, st=st, ch=hd // 2, pb=(hd % 2) * 64, r=r, diag=kb >= 4 * qt,
                                    c0=r * 128, qs=slice(qt * 512 + r * 128, (qt + 1) * 512),
                                    ks=slice(kb * 128, (kb + 1) * 128), sl=i % NBUF,
                                    zb=ZB[i % 3], bb=BB[i % 3], ob=OB2[st],
                                    first=(kb == 4 * qt + 3), last=(kb == 0))

                    def stageA(ii):
                        gs = [geo(i) for i in ii]
                        for g in gs:
                            mm(g["zb"], banks[g["zb"]][:, g["c0"]:], kT[g["pb"]:g["pb"] + 64, g["ch"], g["ks"]],
                               qT[g["pb"]:g["pb"] + 64, g["ch"], g["qs"]], True, not g["diag"],
                               [("kT", g["kb"] // 4), ("qT", g["qt"])])
                        for g in gs:
                            if g["diag"]:
                                mm(g["zb"], banks[g["zb"]][:, g["c0"]:g["c0"] + 128], identb, nmaskb, False, True, ["cstb"])
                        for g in gs:
                            zb_, sl, c0 = g["zb"], g["sl"], g["c0"]
                            op("act", lambda h: h.activation(out=ebuf[sl][:, c0:], in_=banks[zb_][:, c0:], func=AF.Exp),
                               writes=[BK(zb_), ("ebuf", sl)])
                            if not g["last"]:
                                op("dve", lambda h: h.tensor_copy(out=zrow[sl][0:1, c0:], in_=banks[zb_][0:1, c0:]),
                                   writes=[BK(zb_), ("zrow", sl)])
                            op("act", lambda h: h.activation(out=spb[sl][:, c0:], in_=ebuf[sl][:, c0:], func=AF.Ln,
                                                             bias=1.0, scale=1.0),
                               reads=[("ebuf", sl)], writes=[("spb", sl)])

                    def stageB(ii):
                        gs = [geo(i) for i in ii]
                        for g in gs:
                            mm(g["bb"], banks[g["bb"]][:, g["c0"]:], trib, spb[g["sl"]][:, g["c0"]:], True, False,
                               ["cstb", ("spb", g["sl"])])
                        for g in gs:
                            mm(g["bb"], banks[g["bb"]][:, g["c0"]:], nkT[g["pb"]:g["pb"] + 64, g["ch"], g["ks"]],
                               qT[g["pb"]:g["pb"] + 64, g["ch"], g["qs"]], False, False,
                               [("nkT", g["kb"] // 4), ("qT", g["qt"])])
                        for g in gs:
                            if g["diag"]:
                                mm(g["bb"], banks[g["bb"]][:, g["c0"]:g["c0"] + 128], identb, pmaskb, False, g["first"],
                                   ["cstb"])
                        for g in gs:
                            if not g["first"]:
                                mm(g["bb"], banks[g["bb"]][:, g["c0"]:], onesr[:], A16[g["st"]][0:1, g["c0"]:], False,
                                   True, ["onesr", ("A16", g["st"])])
                        for g in gs:
                            bb_, sl, c0, st = g["bb"], g["sl"], g["c0"], g["st"]
                            op("act", lambda h: h.activation(out=Pb[sl][:, c0:], in_=banks[bb_][:, c0:], func=AF.Exp,
                                                             scale=-1.0),
                               writes=[BK(bb_), ("Pb", sl)])
                            if g["first"]:
                                op("dve", lambda h: h.memset(A16[st][:], 0.0), writes=[("A16", st)])
                            if not g["last"]:
                                op("dve", lambda h: h.tensor_tensor(out=A16[st][0:1, c0:], in0=banks[bb_][0:1, c0:],
                                                                    in1=zrow[sl][0:1, c0:], op=ALU.add),
                                   reads=[("zrow", sl)], writes=[BK(bb_), ("A16", st)])

                    def stageC(ii):
                        gs = [geo(i) for i in ii]
                        for g in gs:
                            ob, sl, c0, pb, hd, kb = g["ob"], g["sl"], g["c0"], g["pb"], g["hd"], g["kb"]
                            mm(ob, banks[ob][pb:pb + 64, c0:], Vt[:, kb, hd * 64:(hd + 1) * 64], Pb[sl][:, c0:],
                               g["first"], g["last"], [("Pb", sl), ("Vt", kb)], skip_group_check=True)
                        for g in gs:
                            if g["last"]:
                                ob, pb, ch, qt = g["ob"], g["pb"], g["ch"], g["qt"]
                                op("dve", lambda h: h.tensor_copy(out=catB[pb:pb + 64, ch, qt * 512:(qt + 1) * 512],
                                                                  in_=banks[ob][pb:pb + 64, :]),
                                   writes=[BK(ob), ("catB", ch, qt, pb)])

                    n_st = len(items) // 2
                    for step in range(n_st + 2):
                        if step < n_st:
                            stageA((2 * step, 2 * step + 1))
                        if 0 <= step - 1 < n_st:
                            stageB((2 * (step - 1), 2 * (step - 1) + 1))
                        if 0 <= step - 2 < n_st:
                            stageC((2 * (step - 2), 2 * (step - 2) + 1))

                with c.scope():
                    N = 512
                    if dbg:
                        c.dma("sp", dC_v[:, 0:4, T0:T0 + L], catA[:], writes=["dC0"], key="k_dbg")
                        c.dma("sp", dC_v[:, 4:8, T0:T0 + L], catB[:], writes=["dC1"], key="k_dbg")
                    w_out = c.sb([128, 8, 1024], BF16, "w_out")
                    load_w(w_out, w_out_d, 8, 1024, "k_w_out", "w_out")
                    lnb = ln_scratch(N)
                    ztD = [c.sb([128, 8, N], F32, f"ztD{i}") for i in range(2)]
                    sin_t = c.sb([128, 8, N], F32, "sinD")
                    xbt = c.sb([128, 8, N], BF16, "xbtD")

                    def d_mm(tt_):
                        tok0 = T0 + tt_ * N
                        cs = slice(tt_ * N, (tt_ + 1) * N)
                        z = ztD[tt_ % 2]
                        zk = ("ztD", tt_ % 2)
                        c.dma("sp", sin_t[:], S_v[:, :, tok0:tok0 + N], writes=["sinD"], key="k_sinD")
                        for oc in range(8):
                            b_ = nb()
                            for kc in range(8):
                                src = catA[:, kc, cs] if kc < 4 else catB[:, kc - 4, cs]
                                mm(b_, banks[b_][:, :], w_out[:, kc, oc * 128:(oc + 1) * 128], src,
                                   kc == 0, kc == 7, wreads("w_out", kc, 1024))
                            op("dve", lambda h: h.tensor_tensor(out=z[:, oc, :], in0=banks[b_][:, :],
                                                                in1=sin_t[:, oc, :], op=ALU.add),
                               reads=["sinD"], writes=[BK(b_), zk])

                    def d_ln(tt_):
                        tok0 = T0 + tt_ * N
                        z = ztD[tt_ % 2]
                        zk = ("ztD", tt_ % 2)
                        layer_norm(z, N, zk, z, zk, lnb)
                        ln_epilogue_stream(z, zk, N, tok0, ga10, ba10, "ln1_g0", "ln1_b0", z, xbt, 0, sstkey=zk, xb_eng="dve")

                    for tt_ in range(4):
                        d_mm(tt_)
                        if tt_ > 0:
                            d_ln(tt_ - 1)
                    d_ln(3)


    def dbg_stop(tag):
        if dbg and dbg.get("stop") == tag:
            c.barrier()
            c.dma("sp", dS_d, S_d, writes=["dS"], key="k_dbg")
            c.dma("sp", dX_d, XB_d, writes=["dX"], key="k_dbg")
            c.barrier()
            c.emit()
            c.close()
            return True
        return False

    if dbg_stop("D"):
        return nc
    H2_d = nc.dram_tensor("H2_scr", [D, T], BF16, kind="Internal").ap()
    H2_v = H2_d.rearrange("(c p) t -> p c t", p=128)

    def ffn_phase(layer, final, ga, ba, gname, bname):
        N = 512
        NTL = T // N
        NQ = 4
        b1n = f"fb1_{layer}"
        with c.scope():
            w1s = [c.sb([128, 8, 1024], BF16, f"w1s{i}") for i in range(2)]
            w2s = [c.sb([128, 8, 1024], BF16, f"w2s{i}") for i in range(2)]
            zt = [c.sb([128, 8, N], F32, f"ztE{i}") for i in range(2)]
            xin = [c.sb([128, 8, N], BF16, f"xinE{i}") for i in range(2)]
            sin2 = [c.sb([128, 8, N], F32, f"sinE{i}") for i in range(2)]
            hb = c.sb([128, 8, N], BF16, "hb")
            rl = [c.sb([128, N], F32, f"rl{i}") for i in range(2)]
            lnb = ln_scratch(N)
            xbt = c.sb([128, 8, N], BF16, "xbtE")
            yo = c.sb([128, D], F32, "yo")

            def load_q(qr):
                sl = qr % 2
                load_w(w1s[sl], w1_d[layer], 8, 1024, f"k_w1s{sl}", ("w1s", sl), cs0=qr * 1024)
                load_w(w2s[sl], w2_d[layer][qr * 1024:(qr + 1) * 1024, :], 8, 1024, f"k_w2s{sl}", ("w2s", sl))

            def part_h(qr, tt_):
                sl = tt_ % 2
                ws = qr % 2
                tok0 = tt_ * N
                for hc in range(8):
                    b_ = nb()
                    s2 = hc % 2
                    hcg = qr * 8 + hc
                    for kc in range(8):
                        mm(b_, banks[b_][:, :], w1s[ws][:, kc, hc * 128:(hc + 1) * 128], xin[sl][:, kc, :],
                           kc == 0, kc == 7, wreads(("w1s", ws), kc, 1024) + [("xinE", sl)])
                    op("act", lambda h: h.activation(out=rl[s2][:], in_=banks[b_][:, :], func=AF.Relu,
                                                     bias=V(b1n, hcg, 1), scale=1.0),
                       reads=["vec"], writes=[BK(b_), ("rl", s2)])
                    op("dve", lambda h: h.scalar_tensor_tensor(
                        out=hb[:, hc, :], in0=banks[b_][:, :], scalar=V(b1n, hcg, 1), in1=rl[s2][:],
                        op0=ALU.add, op1=ALU.mult), reads=["vec", ("rl", s2)], writes=[BK(b_), ("hb", hc)])

            def part_out(qr, tt_, mode="store"):
                ws = qr % 2
                tok0 = tt_ * N
                z = zt[tt_ % 2]
                zk = ("ztE", tt_ % 2)
                sin_t = sin2[tt_ % 2]
                sk_ = ("sinE", tt_ % 2)
                for oc in range(8):
                    b_ = nb()
                    for hc in range(8):
                        mm(b_, banks[b_][:, :], w2s[ws][:, hc, oc * 128:(oc + 1) * 128], hb[:, hc, :],
                           hc == 0, hc == 7, wreads(("w2s", ws), hc, 1024) + [("hb", hc)])
                    if mode == "acc":
                        op("dve", lambda h: h.tensor_tensor(out=z[:, oc, :], in0=banks[b_][:, :], in1=z[:, oc, :],
                                                            op=ALU.add), reads=[], writes=[BK(b_), zk])
                    else:
                        op("dve", lambda h: h.tensor_tensor(out=z[:, oc, :], in0=banks[b_][:, :], in1=sin_t[:, oc, :],
                                                            op=ALU.add), reads=[sk_], writes=[BK(b_), zk])
                if mode == "store":
                    c.dma("pool", S_v[:, :, tok0:tok0 + N], z[:], reads=[zk], writes=[("S_d", tok0)],
                          key=f"k_zpart{tt_ % 2}")

            def loads(qr, tt_):
                sl = tt_ % 2
                tok0 = tt_ * N
                c.dma("sp", xin[sl][:], XB_v[:, :, tok0:tok0 + N], writes=[("xinE", sl)], key=f"k_xinE{sl}")
                c.dma("sp", sin2[sl][:], S_v[:, :, tok0:tok0 + N], reads=[("S_d", tok0)], writes=[("sinE", sl)],
                      key=f"k_sinE{sl}")

            def part_ln(tt_):
                tok0 = tt_ * N
                z = zt[tt_ % 2]
                zk = ("ztE", tt_ % 2)
                layer_norm(z, N, zk, z, zk, lnb)
                if not final:
                    ln_epilogue_stream(z, zk, N, tok0, ga, ba, gname, bname, z, xbt, 1, sstkey=zk)
                else:
                    for kc in range(8):
                        op("act", lambda h, kc=kc: h.activation(out=z[:, kc, :], in_=z[:, kc, :], func=AF.Identity,
                                                                bias=V(bname, kc, 1), scale=V(gname, kc, 1)),
                           reads=["vec"], writes=[zk])
                    for bl in range(N // 128):
                        b0, b1 = nb(), nb()
                        for fc in range(8):
                            bq_ = b0 if fc < 4 else b1
                            op("pe", lambda h, fc=fc, bq_=bq_: h.transpose(
                                out=banks[bq_][:, (fc % 4) * 128:(fc % 4) * 128 + 128],
                                in_=z[:, fc, bl * 128:(bl + 1) * 128], identity=ident),
                                reads=[zk, "cst"], writes=[BK(bq_)])
                        op("act", lambda h: h.activation(out=yo[:, 0:512], in_=banks[b0][:, :], func=AF.Copy),
                           writes=[BK(b0), "yo"])
                        op("dve", lambda h: h.tensor_copy(out=yo[:, 512:1024], in_=banks[b1][:, :]),
                           writes=[BK(b1), "yo"])
                        c.dma("sp", y_d[tok0 + bl * 128:tok0 + (bl + 1) * 128, :], yo[:],
                              reads=["yo"], writes=[("y", tok0, bl)], key="k_yo")

            load_q(0)
            load_q(1)
            for qr in range(2):
                loads(qr, 0)
                for tt_ in range(NTL):
                    if tt_ + 1 < NTL:
                        loads(qr, tt_ + 1)
                    part_h(qr, tt_)
                    part_out(qr, tt_, "store")
                    if qr == 0 and tt_ == NTL - 1:
                        load_q(2)
            load_q(3)
            loads(2, 0)
            for tt_ in range(NTL):
                if tt_ + 1 < NTL:
                    loads(2, tt_ + 1)
                part_h(2, tt_)
                part_out(2, tt_, "keep")
                part_h(3, tt_)
                if tt_ > 0:
                    part_ln(tt_ - 1)
                part_out(3, tt_, "acc")
            part_ln(NTL - 1)

    ffn_phase(0, False, ga20, ba20, "ln2_g0", "ln2_b0")
    if dbg_stop("E0"):
        return nc

    N = 512
    NTL = T // N
    with c.scope():
        pw1 = c.sb([128, 8, 2048], BF16, "pw1")
        load_w(pw1, pw1_d, 8, 2048, "k_pw1", "pw1")
        dg = c.sb([128, 8, 31, 128], BF16, "dg")
        for oc in range(8):
            for k in range(31):
                op("dve", lambda h, oc=oc, k=k: h.tensor_scalar(out=dg[:, oc, k, :], in0=ident,
                                                                scalar1=V("w_dw", k * 8 + oc, 1), scalar2=None,
                                                                op0=ALU.mult), reads=["cst", "vec"], writes=[("dg", oc)])
        lnb = ln_scratch(N)
        xin = [c.sb([128, 8, N], BF16, f"xinF{i}") for i in range(2)]
        hbuf = c.sb([128, 8, 30 + N], BF16, "hbuf")
        sgF = [c.sb([128, N], F32, f"sgF{i}") for i in range(2)]
        cvs = [c.sb([128, 8, N], F32, f"cv{i}") for i in range(2)]
        h2 = c.sb([128, 8, N], BF16, "h2")

        def f1_load(tt_):
            sl = tt_ % 2
            c.dma("sp", xin[sl][:], XB_v[:, :, tt_ * N:(tt_ + 1) * N], writes=[("xinF", sl)], key=f"k_xinF{sl}")

        def f1_glu(tt_):
            sl = tt_ % 2
            if tt_ % 4 == 0:
                op("dve", lambda h: h.memset(hbuf[:, :, 0:30], 0.0), writes=[("hbuf", i) for i in range(8)])
            else:
                for oc in range(8):
                    op("dve", lambda h, oc=oc: h.tensor_copy(out=hbuf[:, oc, 0:30], in_=hbuf[:, oc, N:N + 30]),
                       reads=[], writes=[("hbuf", oc)])
            for oc in range(8):
                bv_, bg_ = nb(), nb()
                s2 = oc % 2
                for kc in range(8):
                    mm(bv_, banks[bv_][:, :], pw1[:, kc, oc * 128:(oc + 1) * 128], xin[sl][:, kc, :],
                       kc == 0, kc == 7, wreads("pw1", kc, 2048) + [("xinF", sl)])
                for kc in range(8):
                    mm(bg_, banks[bg_][:, :], pw1[:, kc, 1024 + oc * 128:1024 + (oc + 1) * 128], xin[sl][:, kc, :],
                       kc == 0, kc == 7, wreads("pw1", kc, 2048) + [("xinF", sl)])
                op("act", lambda h: h.activation(out=sgF[s2][:], in_=banks[bg_][:, :], func=AF.Sigmoid,
                                                 bias=V("b_pw1", 8 + oc, 1), scale=1.0),
                   reads=["vec"], writes=[BK(bg_), ("sgF", s2)])
                op("dve", lambda h: h.scalar_tensor_tensor(
                    out=hbuf[:, oc, 30:30 + N], in0=banks[bv_][:, :], scalar=V("b_pw1", oc, 1), in1=sgF[s2][:],
                    op0=ALU.add, op1=ALU.mult), reads=["vec", ("sgF", s2)], writes=[BK(bv_), ("hbuf", oc)])

        def f1_conv(tt_):
            cv = cvs[tt_ % 2]
            for oc in range(8):
                b_ = nb()
                for k in range(31):
                    mm(b_, banks[b_][:, :], dg[:, oc, k, :], hbuf[:, oc, k:k + N], k == 0, k == 30,
                       [("dg", oc), ("hbuf", oc)])
                op("act", lambda h: h.activation(out=cv[:, oc, :], in_=banks[b_][:, :], func=AF.Identity,
                                                 bias=V("b_dw", oc, 1), scale=1.0),
                   reads=["vec"], writes=[BK(b_), ("cv", tt_ % 2)])

        def f1_ln(tt_):
            cv = cvs[tt_ % 2]
            ck = ("cv", tt_ % 2)
            layer_norm(cv, N, ck, cv, ck, lnb)
            for oc in range(8):
                op("act", lambda h, oc=oc: h.activation(out=h2[:, oc, :], in_=cv[:, oc, :], func=AF.Silu,
                                                        bias=V("cln_b", oc, 1), scale=V("cln_g", oc, 1)),
                   reads=[ck, "vec"], writes=["h2"])
            c.dma("pool", H2_v[:, :, tt_ * N:(tt_ + 1) * N], h2[:], reads=["h2"], writes=[("H2_d", tt_)], key="k_h2")

        f1_load(0)
        for tt_ in range(NTL):
            if tt_ + 1 < NTL:
                f1_load(tt_ + 1)
            f1_glu(tt_)
            if tt_ > 0:
                f1_ln(tt_ - 1)
            f1_conv(tt_)
        f1_ln(NTL - 1)
    with c.scope():
        pw2 = c.sb([128, 8, 1024], BF16, "pw2")
        load_w(pw2, pw2_d, 8, 1024, "k_pw2", "pw2")
        lnb = ln_scratch(N)
        h2i = [c.sb([128, 8, N], BF16, f"h2i{i}") for i in range(2)]
        sin2 = [c.sb([128, 8, N], F32, f"sinF{i}") for i in range(2)]
        zts = [c.sb([128, 8, N], F32, f"ztF{i}") for i in range(2)]
        xbt = c.sb([128, 8, N], BF16, "xbtF")

        def f2_load(tt_):
            sl = tt_ % 2
            tok0 = tt_ * N
            c.dma("sp", h2i[sl][:], H2_v[:, :, tok0:tok0 + N], writes=[("h2i", sl)], key=f"k_h2i{sl}")
            c.dma("sp", sin2[sl][:], S_v[:, :, tok0:tok0 + N], writes=[("sinF", sl)], key=f"k_sinF{sl}")

        def f2_mm(tt_):
            sl = tt_ % 2
            z = zts[sl]
            for oc in range(8):
                b_ = nb()
                for kc in range(8):
                    mm(b_, banks[b_][:, :], pw2[:, kc, oc * 128:(oc + 1) * 128], h2i[sl][:, kc, :], kc == 0, kc == 7,
                       wreads("pw2", kc, 1024) + [("h2i", sl)])
                op("dve", lambda h: h.tensor_tensor(out=z[:, oc, :], in0=banks[b_][:, :], in1=sin2[sl][:, oc, :],
                                                    op=ALU.add), reads=[("sinF", sl)], writes=[BK(b_), ("ztF", sl)])

        def f2_ln(tt_):
            sl = tt_ % 2
            z = zts[sl]
            zk = ("ztF", sl)
            layer_norm(z, N, zk, z, zk, lnb)
            ln_epilogue_stream(z, zk, N, tt_ * N, ga11, ba11, "ln1_g1", "ln1_b1", z, xbt, 2, sstkey=zk, xb_eng="dve")

        f2_load(0)
        for tt_ in range(NTL):
            if tt_ + 1 < NTL:
                f2_load(tt_ + 1)
            f2_mm(tt_)
            if tt_ > 0:
                f2_ln(tt_ - 1)
        f2_ln(NTL - 1)

    if dbg and dbg.get("stop") == "F":
        c.barrier()
        c.dma("sp", dC_d, H2_d, writes=["dC0"], key="k_dbg")
    if dbg_stop("F"):
        return nc
    ffn_phase(1, True, None, None, "ln2_g1", "ln2_b1")
    if dbg:
        dbg_stop(dbg.get("stop"))
        return nc

    c.barrier()
    c.emit()
    c.close()
    return nc


def _col(v):
    v = np.asarray(v, np.float32).reshape(-1, 128)
    return np.ascontiguousarray(v.T)


def _host_layout(inp):
    f = lambda a: np.ascontiguousarray(np.asarray(a, np.float32))
    vec = np.zeros((128, NV), np.float32)

    def put(name, arr):
        o, w = VOFF[name]
        vec[:, o:o + w] = arr

    for l in range(2):
        put(f"ln1_g{l}", _col(inp["ln1_g"][l])); put(f"ln1_b{l}", _col(inp["ln1_b"][l]))
        put(f"ln2_g{l}", _col(inp["ln2_g"][l])); put(f"ln2_b{l}", _col(inp["ln2_b"][l]))
        put(f"fb1_{l}", _col(inp["ffn_b1"][l])); put(f"fb2_{l}", _col(inp["ffn_b2"][l]))
    put("b_in", _col(inp["mix_b_in"][0][:1536]))
    put("s5_d", _col(inp["s5_d"][0])); put("b_glu", _col(inp["s5_b_glu"][0])); put("b_out", _col(inp["mix_b_out"][0]))
    put("b_pw1", _col(inp["conv_b_pw1"][0])); put("b_dw", _col(inp["conv_b_dw"][0]))
    put("cln_g", _col(inp["conv_ln_g"][0])); put("cln_b", _col(inp["conv_ln_b"][0]))
    put("b_pw2", _col(inp["conv_b_pw2"][0]))
    wd = np.asarray(inp["conv_w_dw"][0], np.float32)
    put("w_dw", np.concatenate([_col(wd[k]) for k in range(31)], axis=1))
    bv = np.ascontiguousarray(np.broadcast_to(np.asarray(inp["mix_b_in"][0][1536:], np.float32)[None, :], (128, 512)))
    cst = np.zeros((128, 640), np.float32)
    cst[:, 0:128] = np.eye(128)
    j = np.arange(128)[:, None]; s = np.arange(128)[None, :]
    cst[:, 128:256] = (j >= s)
    cst[:, 256:384] = np.where(j >= s, -30000.0, 0.0)
    cst[:, 384:512] = np.where(j >= s, 30000.0, 0.0)
    cst[:, 512:640] = 1.0 / 1024.0
    lr = np.asarray(inp["s5_lambda_re"][0], np.float32); li = np.asarray(inp["s5_lambda_im"][0], np.float32)
    ld = np.asarray(inp["s5_log_dt"][0], np.float32)
    lam = np.zeros((128, 3, 16), np.float32)
    for k in range(16):
        for g2 in range(2):
            g = 2 * k + g2
            lam[g2 * 64:(g2 + 1) * 64, 0, k] = lr[g]
            lam[g2 * 64:(g2 + 1) * 64, 1, k] = li[g]
            lam[g2 * 64:(g2 + 1) * 64, 2, k] = ld[g]

    def pad_layout(arr_gnp):
        out = np.zeros((128, 16, 128), np.float32)
        for k in range(16):
            for g2 in range(2):
                g = 2 * k + g2
                c0 = 16 * (g % 8)
                out[g2 * 64:(g2 + 1) * 64, k, c0:c0 + 16] = arr_gnp[g]
        return out

    bre = np.asarray(inp["s5_b_re"][0], np.float32); bim = np.asarray(inp["s5_b_im"][0], np.float32)
    cre = np.asarray(inp["s5_c_re"][0], np.float32).transpose(0, 2, 1)
    cim = np.asarray(inp["s5_c_im"][0], np.float32).transpose(0, 2, 1)
    shared = {
        "w_in": f(inp["mix_w_in"][0]), "w_glu": f(inp["s5_w_glu"][0]), "w_out": f(inp["mix_w_out"][0]),
        "pw1": f(inp["conv_w_pw1"][0]), "pw2": f(inp["conv_w_pw2"][0]),
        "w1_0": f(inp["ffn_w1"][0]), "w1_1": f(inp["ffn_w1"][1]),
        "w2_0": f(inp["ffn_w2"][0]), "w2_1": f(inp["ffn_w2"][1]),
        "vecs": vec, "bv_bc": bv, "consts": cst, "lamll": lam,
        "bt_re": pad_layout(bre), "bt_im": pad_layout(bim), "cp_re": pad_layout(cre), "cp_im": pad_layout(cim),
    }
    return shared


def kernel(**inputs):
    x = np.asarray(inputs["x"], np.float32)
    shared = _host_layout(inputs)
    nc = build()
    in_maps = []
    for i in range(8):
        m = dict(shared)
        m["x"] = np.ascontiguousarray(x[2 * i:2 * i + 2].reshape(T, D))
        in_maps.append(m)
    res = run_bass_kernel_spmd(nc, in_maps, core_ids=list(range(8)))
    out = np.concatenate([r["y"].reshape(2, L, D) for r in res.results], axis=0)
    return out.astype(np.float32)
```
